# Optimizing a Trainium2 kernel written in Bass

```python
import math
import jax, jax.numpy as jnp
from jax import lax
import numpy as np

D_MODEL = 1024
BATCH = 4
SEQ = 4096
DEPTH = 1
DEC_BATCH = 128
DEC_SEQ = 8
PAST_LEN = 8192
PAGE_SIZE = 128

HEAD_DIM = 64
NSA_WIDTH = D_MODEL // 2
GMLP_WIDTH = D_MODEL - NSA_WIDTH
NSA_HEADS = NSA_WIDTH // HEAD_DIM
NSA_KV_HEADS = 2
GQA_GROUP = NSA_HEADS // NSA_KV_HEADS
KV_DIM = NSA_KV_HEADS * HEAD_DIM
N_KV_STREAMS = 4
N_BRANCH = 3
CMP_BLOCK = 32
CMP_STRIDE = 16
CMP_HIDDEN = 128
SLC_BLOCK = 64
N_SELECT = 16
WINDOW = 512
QUERY_BLOCK = 128
FORCE_SCORE = 1e9
GMLP_GROUPS = GMLP_WIDTH // 64
GMLP_GROUP_DIM = GMLP_WIDTH // GMLP_GROUPS
CHUNK = 128
D_FF = 4 * D_MODEL
IN_COLS = NSA_WIDTH + N_KV_STREAMS * KV_DIM + 2 * KV_DIM + N_BRANCH * NSA_HEADS + 2 * GMLP_WIDTH
EPS = 1e-6

kernel_name = 'nsa_gmlp_hybrid_step'


def rmsnorm(x, g):
    xf = x.astype(jnp.float32)
    y = xf * lax.rsqrt(jnp.mean(xf * xf, axis=-1, keepdims=True) + EPS)
    return (y * g).astype(x.dtype)


def layernorm(x, g, b):
    xf = x.astype(jnp.float32)
    mu = jnp.mean(xf, axis=-1, keepdims=True)
    var = jnp.mean(jnp.square(xf - mu), axis=-1, keepdims=True)
    return ((xf - mu) * lax.rsqrt(var + EPS) * g + b).astype(x.dtype)


def masked_softmax(s, mask):
    s = jnp.where(mask, s.astype(jnp.float32), -jnp.inf)
    m = jnp.max(s, axis=-1, keepdims=True)
    m = jnp.where(jnp.isfinite(m), m, 0.0)
    p = jnp.exp(s - m)
    return p / jnp.maximum(jnp.sum(p, axis=-1, keepdims=True), 1e-30)


def project(x, ln1_g, w_in, ln_v_g, ln_v_b):
    B, S, _ = x.shape
    z = rmsnorm(x, ln1_g) @ w_in
    sizes = [NSA_WIDTH, N_KV_STREAMS * KV_DIM, 2 * KV_DIM, N_BRANCH * NSA_HEADS, 2 * GMLP_WIDTH]
    idx = [int(c) for c in np.cumsum(sizes)[:-1]]
    zq, zkv, zwin, zgate, zg = jnp.split(z, idx, axis=-1)
    q = zq.reshape(B, S, NSA_HEADS, HEAD_DIM)
    kv = zkv.reshape(B, S, N_KV_STREAMS, NSA_KV_HEADS, HEAD_DIM)
    wkv = zwin.reshape(B, S, 2, NSA_KV_HEADS, HEAD_DIM)
    gate_logits = zgate.reshape(B, S, N_BRANCH, NSA_HEADS)
    zg = jax.nn.gelu(zg)
    u = zg[..., :GMLP_WIDTH]
    v = layernorm(zg[..., GMLP_WIDTH:], ln_v_g, ln_v_b)
    return q, kv, wkv, gate_logits, u, v


def compress_blocks(k, w1, b1, w2, b2, pos):
    B, T, KVH, dh = k.shape
    R = CMP_BLOCK // CMP_STRIDE
    n_seg = -(-T // CMP_STRIDE)
    seg = jnp.pad(k, ((0, 0), (0, n_seg * CMP_STRIDE - T), (0, 0), (0, 0)))
    seg = seg.reshape(B, n_seg, CMP_STRIDE, KVH, dh)
    n_c = n_seg - R + 1
    w1r = w1.reshape(R, CMP_STRIDE, dh, CMP_HIDDEN)
    posr = pos.reshape(R, CMP_STRIDE, 1, dh)
    h = b1
    for r in range(R):
        h = h + jnp.einsum('bnskd,sdh->bnkh', seg[:, r:r + n_c] + posr[r], w1r[r])
    return jax.nn.gelu(h) @ w2 + b2


def cmp_to_slc_matrix(n_c, n_s):
    ci = jnp.arange(n_c)[:, None] * CMP_STRIDE
    sj = jnp.arange(n_s)[None, :] * SLC_BLOCK
    cover = jnp.clip(jnp.minimum(ci + CMP_BLOCK, sj + SLC_BLOCK) - jnp.maximum(ci, sj), 0, None)
    return cover.astype(jnp.float32) / CMP_BLOCK


def nsa_mixer(q, kv_full, win_ext, gate_logits, cmp_w1, cmp_b1, cmp_w2, cmp_b2, cmp_pos):
    B, Sq, H, dh = q.shape
    T = kv_full.shape[1]
    q_pos0 = T - Sq
    qg = q.reshape(B, Sq, NSA_KV_HEADS, GQA_GROUP, HEAD_DIM) * (HEAD_DIM ** -0.5)
    t = q_pos0 + jnp.arange(Sq)

    kc = compress_blocks(kv_full[:, :, 0], cmp_w1[0], cmp_b1[0], cmp_w2[0], cmp_b2[0], cmp_pos[0])
    vc = compress_blocks(kv_full[:, :, 1], cmp_w1[1], cmp_b1[1], cmp_w2[1], cmp_b2[1], cmp_pos[1])
    n_c = kc.shape[1]
    s_c = jnp.einsum('bqkgd,bnkd->bkgqn', qg, kc)
    mask_c = (jnp.arange(n_c) * CMP_STRIDE + CMP_BLOCK - 1)[None, :] <= t[:, None]
    p_c = masked_softmax(s_c, mask_c)
    o_cmp = jnp.einsum('bkgqn,bnkd->bqkgd', p_c.astype(vc.dtype), vc)

    n_s = -(-T // SLC_BLOCK)
    imp = jnp.einsum('bkgqn,ns->bkqs', p_c, cmp_to_slc_matrix(n_c, n_s))
    blk = jnp.arange(n_s)[None, :]
    cur = (t // SLC_BLOCK)[:, None]
    visible = blk * SLC_BLOCK <= t[:, None]
    forced = (blk == 0) | (blk == cur) | (blk == cur - 1)
    score = jnp.where(forced, FORCE_SCORE, jnp.where(visible, imp, -jnp.inf))
    n_sel = min(N_SELECT, n_s)
    _, sel = lax.top_k(score, n_sel)

    pad = n_s * SLC_BLOCK - T

    def to_blocks(x):
        x = jnp.pad(x, ((0, 0), (0, pad), (0, 0), (0, 0)))
        return x.reshape(B, n_s, SLC_BLOCK, NSA_KV_HEADS, HEAD_DIM).transpose(0, 3, 1, 2, 4)

    ks = to_blocks(kv_full[:, :, 2])
    vs = to_blocks(kv_full[:, :, 3])
    qb_size = math.gcd(Sq, QUERY_BLOCK)
    n_qb = Sq // qb_size
    n_keys = n_sel * SLC_BLOCK
    gather = jax.vmap(jax.vmap(lambda blocks, i: blocks[i]))

    def sel_block(args):
        qb, ib, tb = args
        gk = gather(ks, ib).reshape(B, NSA_KV_HEADS, qb_size, n_keys, HEAD_DIM)
        gv = gather(vs, ib).reshape(B, NSA_KV_HEADS, qb_size, n_keys, HEAD_DIM)
        s = jnp.einsum('bqkgd,bkqmd->bkgqm', qb, gk)
        pos = (ib[..., None] * SLC_BLOCK + jnp.arange(SLC_BLOCK)).reshape(B, NSA_KV_HEADS, qb_size, n_keys)
        mask = (pos <= tb[None, None, :, None])[:, :, None]
        p = masked_softmax(s, mask)
        return jnp.einsum('bkgqm,bkqmd->bqkgd', p.astype(gv.dtype), gv)

    q_blocks = qg.reshape(B, n_qb, qb_size, NSA_KV_HEADS, GQA_GROUP, HEAD_DIM).transpose(1, 0, 2, 3, 4, 5)
    sel_blocks = sel.reshape(B, NSA_KV_HEADS, n_qb, qb_size, n_sel).transpose(2, 0, 1, 3, 4)
    o_slc = lax.map(sel_block, (q_blocks, sel_blocks, t.reshape(n_qb, qb_size)))
    o_slc = o_slc.transpose(1, 0, 2, 3, 4, 5).reshape(B, Sq, NSA_KV_HEADS, GQA_GROUP, HEAD_DIM)

    kw = win_ext[:, :, 0]
    vw = win_ext[:, :, 1]

    def win_block(args):
        qb, start = args
        kb = lax.dynamic_slice_in_dim(kw, start, qb_size + WINDOW, axis=1)
        vb = lax.dynamic_slice_in_dim(vw, start, qb_size + WINDOW, axis=1)
        s = jnp.einsum('bqkgd,bmkd->bkgqm', qb, kb)
        qi = start + jnp.arange(qb_size)
        ki = start + jnp.arange(qb_size + WINDOW)
        mask = ((ki[None, :] > qi[:, None]) & (ki[None, :] <= qi[:, None] + WINDOW)
                & (ki[None, :] >= WINDOW - q_pos0))
        p = masked_softmax(s, mask)
        return jnp.einsum('bkgqm,bmkd->bqkgd', p.astype(vb.dtype), vb)

    o_win = lax.map(win_block, (q_blocks, jnp.arange(n_qb) * qb_size))
    o_win = o_win.transpose(1, 0, 2, 3, 4, 5).reshape(B, Sq, NSA_KV_HEADS, GQA_GROUP, HEAD_DIM)

    g = jax.nn.sigmoid(gate_logits.astype(jnp.float32)).astype(q.dtype)
    g = g.reshape(B, Sq, N_BRANCH, NSA_KV_HEADS, GQA_GROUP)[..., None]
    o = g[:, :, 0] * o_cmp + g[:, :, 1] * o_slc + g[:, :, 2] * o_win
    return o.reshape(B, Sq, NSA_WIDTH)


def spatial_gating(u, v, w_s, b_s):
    B, S, _ = v.shape
    n_chunk = -(-S // CHUNK)
    vp = jnp.pad(v, ((0, 0), (0, n_chunk * CHUNK - S), (0, 0)))
    vp = vp.reshape(B, n_chunk, CHUNK, GMLP_GROUPS, GMLP_GROUP_DIM)
    w = jnp.where(jnp.tril(jnp.ones((CHUNK, CHUNK), dtype=bool)), w_s, 0.0)
    s = jnp.einsum('gij,bnjgc->bnigc', w, vp) + b_s.T[None, None, :, :, None]
    s = s.reshape(B, n_chunk * CHUNK, GMLP_WIDTH)[:, :S]
    return u * s


def residual_ffn(x, o_a, o_b, w_out, ln2_g, w_ff1, w_ff2):
    h = x + jnp.concatenate([o_a, o_b], axis=-1) @ w_out
    f = jax.nn.relu(rmsnorm(h, ln2_g) @ w_ff1)
    return h + (f * f) @ w_ff2


def layer_step(x, kv_past, win_past, ln1_g, w_in, cmp_w1, cmp_b1, cmp_w2, cmp_b2, cmp_pos,
               ln_v_g, ln_v_b, w_s, b_s, w_out, ln2_g, w_ff1, w_ff2):
    Sq = x.shape[1]
    q, kv, wkv, gate_logits, u, v = project(x, ln1_g, w_in, ln_v_g, ln_v_b)
    kv_full = kv if kv_past is None else jnp.concatenate([kv_past, kv], axis=1)
    win_cat = wkv if win_past is None else jnp.concatenate([win_past, wkv], axis=1)
    win_ext = jnp.pad(win_cat, ((0, 0), (WINDOW + Sq - win_cat.shape[1], 0), (0, 0), (0, 0), (0, 0)))
    o_a = nsa_mixer(q, kv_full, win_ext, gate_logits, cmp_w1, cmp_b1, cmp_w2, cmp_b2, cmp_pos)
    o_b = spatial_gating(u, v, w_s, b_s)
    y = residual_ffn(x, o_a, o_b, w_out, ln2_g, w_ff1, w_ff2)
    keep = min(WINDOW, Sq) if win_past is None else win_past.shape[1]
    return y, kv, win_cat[:, win_cat.shape[1] - keep:], v


def setup_inputs(seed: int = 0) -> dict:
    key = jax.random.key(seed)
    ks = jax.random.split(key, 24)
    n_pages = PAST_LEN // PAGE_SIZE
    n_used = DEC_BATCH * n_pages
    n_pool = n_used + n_used // 4
    win_buf = min(WINDOW, PAST_LEN)

    def nrm(k, shape, scale):
        return scale * jax.random.normal(k, shape, jnp.float32)

    page_table = jax.random.permutation(ks[4], n_pool)[:n_used].reshape(DEC_BATCH, n_pages).astype(jnp.int32)
    return {
        'x_prompt': nrm(ks[0], (BATCH, SEQ, D_MODEL), 1.0),
        'x_sample': nrm(ks[1], (DEC_BATCH, DEC_SEQ, D_MODEL), 1.0),
        'cache_kv': nrm(ks[2], (DEPTH, n_pool, PAGE_SIZE, N_KV_STREAMS, NSA_KV_HEADS, HEAD_DIM), 1.0),
        'state_win_kv': nrm(ks[3], (DEPTH, DEC_BATCH, win_buf, 2, NSA_KV_HEADS, HEAD_DIM), 1.0),
        'page_table': page_table,
        'ln1_g': 1.0 + nrm(ks[5], (DEPTH, D_MODEL), 0.02),
        'w_in': nrm(ks[6], (DEPTH, D_MODEL, IN_COLS), D_MODEL ** -0.5),
        'cmp_w1': nrm(ks[7], (DEPTH, 2, CMP_BLOCK * HEAD_DIM, CMP_HIDDEN), (CMP_BLOCK * HEAD_DIM) ** -0.5),
        'cmp_b1': nrm(ks[8], (DEPTH, 2, CMP_HIDDEN), 0.02),
        'cmp_w2': nrm(ks[9], (DEPTH, 2, CMP_HIDDEN, HEAD_DIM), CMP_HIDDEN ** -0.5),
        'cmp_b2': nrm(ks[10], (DEPTH, 2, HEAD_DIM), 0.02),
        'cmp_pos': nrm(ks[11], (DEPTH, 2, CMP_BLOCK, HEAD_DIM), 0.1),
        'ln_v_g': 1.0 + nrm(ks[12], (DEPTH, GMLP_WIDTH), 0.02),
        'ln_v_b': nrm(ks[13], (DEPTH, GMLP_WIDTH), 0.02),
        'w_s': nrm(ks[14], (DEPTH, GMLP_GROUPS, CHUNK, CHUNK), CHUNK ** -0.5),
        'b_s': 1.0 + nrm(ks[15], (DEPTH, GMLP_GROUPS, CHUNK), 0.02),
        'w_out': nrm(ks[16], (DEPTH, NSA_WIDTH + GMLP_WIDTH, D_MODEL), (NSA_WIDTH + GMLP_WIDTH) ** -0.5),
        'ln2_g': 1.0 + nrm(ks[17], (DEPTH, D_MODEL), 0.02),
        'w_ff1': nrm(ks[18], (DEPTH, D_MODEL, D_FF), D_MODEL ** -0.5),
        'w_ff2': nrm(ks[19], (DEPTH, D_FF, D_MODEL), D_FF ** -0.5),
        'ln_f_g': 1.0 + nrm(ks[20], (D_MODEL,), 0.02),
    }


def reference(x_prompt, x_sample, cache_kv, state_win_kv, page_table, ln1_g, w_in, cmp_w1, cmp_b1,
              cmp_w2, cmp_b2, cmp_pos, ln_v_g, ln_v_b, w_s, b_s, w_out, ln2_g, w_ff1, w_ff2, ln_f_g):
    hp = x_prompt
    hs = x_sample
    kv_p, kv_s, win_p, win_s, v_s = [], [], [], [], []
    for l in range(DEPTH):
        params = (ln1_g[l], w_in[l], cmp_w1[l], cmp_b1[l], cmp_w2[l], cmp_b2[l], cmp_pos[l],
                  ln_v_g[l], ln_v_b[l], w_s[l], b_s[l], w_out[l], ln2_g[l], w_ff1[l], w_ff2[l])
        hp, kv_new, win_new, _ = layer_step(hp, None, None, *params)
        kv_p.append(kv_new)
        win_p.append(win_new)
        past = cache_kv[l][page_table].reshape(DEC_BATCH, PAST_LEN, N_KV_STREAMS, NSA_KV_HEADS, HEAD_DIM)
        hs, kv_new, win_new, v_new = layer_step(hs, past, state_win_kv[l], *params)
        kv_s.append(kv_new)
        win_s.append(win_new)
        v_s.append(v_new)
    y_prompt = rmsnorm(hp, ln_f_g)
    y_sample = rmsnorm(hs, ln_f_g)
    return (y_prompt, y_sample, jnp.stack(kv_p), jnp.stack(kv_s), jnp.stack(win_p), jnp.stack(win_s), jnp.stack(v_s))
```

```python
import contextlib
import numpy as np
import concourse.bass as bass
import concourse.mybir as mybir
from concourse.bass_utils import run_bass_kernel_spmd

F32 = mybir.dt.float32
BF16 = mybir.dt.bfloat16
I32 = mybir.dt.int32
AF = mybir.ActivationFunctionType
ALU = mybir.AluOpType
AX = mybir.AxisListType

ENGS = ("pe", "act", "dve", "pool", "sp")
EPS = 1e-6
NEG = -30000.0


class Op:
    __slots__ = ("eng", "fn", "deps", "idx", "sig", "sem", "val", "n_dma")

    def __init__(self, eng, fn):
        self.eng = eng
        self.fn = fn
        self.deps = []
        self.sig = False
        self.sem = None
        self.val = 0
        self.n_dma = 0


class Prog:
    def __init__(self, nc, stack):
        self.nc = nc
        self.stack = stack
        self.ops = []
        self.lastw = {}
        self.readers = {}
        self.esem = {}
        self.dsem = {}
        self.dcount = {}
        self.dlast = {}
        self.elast = {}
        self.ecount = {e: 0 for e in ENGS}
        for e in ("pe", "act", "dve", "pool"):
            self.esem[e] = stack.enter_context(nc.semaphore("s_" + e))
        self.nps = 0

    def sb(self, name, shape, dt, stack=None):
        return (stack or self.stack).enter_context(self.nc.sbuf_tensor(name, list(shape), dt))

    def _dsem(self, key):
        if key not in self.dsem:
            self.dsem[key] = self.stack.enter_context(self.nc.semaphore("d_" + str(len(self.dsem))))
            self.dcount[key] = 0
        return self.dsem[key]

    def op(self, eng, fn, reads=(), writes=(), dkey=None, ndma=1, extra=()):
        o = Op(eng, fn)
        o.idx = len(self.ops)
        deps = set(extra)
        _psr = [r for r in reads if isinstance(r, tuple) and r and r[0] == "ps"]
        if _psr:
            reads = [r for r in reads if not (isinstance(r, tuple) and r and r[0] == "ps")]
            writes = list(writes) + _psr
        if isinstance(dkey, str) and dkey.startswith("c"):
            dkey = "cser"
            if "cser" in self.dlast:
                deps.add(self.dlast["cser"])
        for r in reads:
            w = self.lastw.get(r)
            if w is not None:
                deps.add(w)
        for r in writes:
            w = self.lastw.get(r)
            if w is not None:
                deps.add(w)
            for rd in self.readers.get(r, ()):
                deps.add(rd)
        for r in reads:
            self.readers.setdefault(r, []).append(o)
        for r in writes:
            self.lastw[r] = o
            self.readers[r] = []
        if dkey is not None:
            o.sem = self._dsem(dkey)
            self.dcount[dkey] += 16 * ndma
            o.val = self.dcount[dkey]
            o.n_dma = ndma
            o.sig = True
            self.dlast[dkey] = o
        else:
            self.elast[eng] = o
        deps.discard(o)
        for d in deps:
            if d.eng == "pe" and eng == "pe" and d.n_dma == 0:
                continue
            o.deps.append(d)
        self.ops.append(o)
        return o

    def barrier(self):
        tg = list(self.elast.values()) + list(self.dlast.values())
        for e in ENGS:
            o = Op(e, lambda eng: None)
            o.deps = [d for d in tg]
            self.ops.append(o)
        self.lastw = {}
        self.readers = {}
        if not hasattr(self, "cuts"):
            self.cuts = []
        self.cuts.append(len(self.ops))

    def emit(self, final_waits=()):
        nc = self.nc
        for o in self.ops:
            for d in o.deps:
                if d.n_dma == 0:
                    d.sig = True
        for o in self.ops:
            if o.n_dma == 0 and o.sig:
                self.ecount[o.eng] += 1
                o.sem = self.esem[o.eng]
                o.val = self.ecount[o.eng]
        handles = {"pe": "tensor", "act": "scalar", "dve": "vector", "pool": "gpsimd", "sp": "sync"}
        cuts = [0] + list(getattr(self, "cuts", [])) + [len(self.ops)]
        seen_all = {e: {} for e in ENGS}
        nseg = len(cuts) - 1
        for si in range(nseg):
            seg = self.ops[cuts[si]:cuts[si + 1]]
            if not seg:
                continue
            per_eng = {e: [o for o in seg if o.eng == e] for e in ENGS}
            last_seg = (si == nseg - 1)
            with nc.Block() as block:
                for e in ENGS:
                    ops = per_eng[e]

                    def body(eng, ops=ops, e=e, last_seg=last_seg):
                        seen = seen_all[e]
                        for o in ops:
                            need = {}
                            for d in o.deps:
                                k = id(d.sem)
                                if seen.get(k, 0) >= d.val:
                                    continue
                                if k not in need or need[k][1] < d.val:
                                    need[k] = (d.sem, d.val)
                            for k, (s, v) in need.items():
                                eng.wait_ge(s, v)
                                seen[k] = v
                            insts = o.fn(eng)
                            if o.n_dma:
                                if not isinstance(insts, (list, tuple)):
                                    insts = [insts]
                                assert len(insts) == o.n_dma, (len(insts), o.n_dma)
                                for i in insts:
                                    i.then_inc(o.sem, 16)
                            elif o.sig:
                                if isinstance(insts, (list, tuple)):
                                    insts = insts[-1]
                                insts.then_inc(o.sem, 1)
                        if e == "sp" and last_seg:
                            for o in final_waits:
                                eng.wait_ge(o.sem, o.val)

                    getattr(block, handles[e])(body)


def I(method, **kw):
    return lambda e: getattr(e, method)(**kw)


def MM(lst):
    def f(e):
        last = None
        for (out, lhsT, rhs, start) in lst:
            last = e.matmul(out, lhsT, rhs, start=start, stop=True, skip_group_check=True)
        return last
    return f


def TR(lst):
    def f(e):
        last = None
        for (out, in_, ident) in lst:
            last = e.transpose(out, in_, ident)
        return last
    return f


class Rot:
    def __init__(self, p, name, n, shape, dt, stack=None):
        self.t = [p.sb("%s%d" % (name, i), shape, dt, stack) for i in range(n)]
        self.name = name
        self.n = n
        self.i = 0

    def get(self):
        k = self.i % self.n
        self.i += 1
        return self.t[k], (self.name, k)


NQ, NKV, NWG, NU, NV = 512, 512, 280, 512, 512
C_Q, C_KV, C_WG, C_U, C_V = 0, 512, 1024, 1304, 1816
INC = 2328


def build(stage, npool=10240):
    nc = bass.Bass("TRN2", target_bir_lowering=False)

    def DI(name, shape, dt=F32):
        return nc.dram_tensor("i_" + name, list(shape), dt, kind="ExternalInput").ap()

    def DO(name, shape, dt=F32):
        return nc.dram_tensor("o_" + name, list(shape), dt, kind="ExternalOutput").ap()

    xo = DI("xo", [2048, 1024]); xc = DI("xc", [2048, 1024]); xs = DI("xs", [128, 1024])
    swin = DI("swin", [16, 512, 256])
    w_in = DI("w_in", [1024, INC]); w_out = DI("w_out", [1024, 1024])
    w_ff1 = DI("w_ff1", [1024, 4096]); w_ff2 = DI("w_ff2", [4096, 1024])
    g1T = DI("g1T", [128, 8]); g2T = DI("g2T", [128, 8]); gfB = DI("gfB", [128, 1024])
    lvg = DI("lvg", [128, 512]); lvb = DI("lvb", [128, 512])
    wsT = DI("wsT", [8, 128, 128]); bs8 = DI("bs8", [8, 128]); bs8s = DI("bs8s", [8, 128])
    identd = DI("ident", [128, 128]); trild = DI("tril", [128, 128]); gindd = DI("gind", [8, 512])

    cw1 = DI("cw1", [2, 2048, 128]); cposd = DI("cpos", [128, 2, 16]); cb1d = DI("cb1", [128, 2])
    cw2 = DI("cw2", [2, 128, 64]); cb2kd = DI("cb2k", [128, 1]); cb2vd = DI("cb2v", [128, 128])
    coverd = DI("cover", [2, 128, 64])
    cmpmd = DI("cmpm", [8, 128, 512]); bandmd = DI("bandm", [8, 128, 512]); negcd = DI("negc", [4, 128, 512])
    Ed = DI("Eexp", [64, 32, 128]); selbd = DI("selb", [128, 16, 64]); visd = DI("vis", [128, 16, 64]); ctxbd = DI("ctxb", [128, 2])

    cache = DI("cache", [npool * 64, 1024]); ptxd = DI("ptx", [128, 16, 32], I32); pcold = DI("pcol", [128, 1])
    covsd = DI("covs", [128, 4, 129]); fbsd = DI("fbs", [64, 129]); sel2d = DI("sel2", [64, 64])
    cpos64d = DI("cpos64", [64, 2, 32]); cm8d = DI("cm8", [64, 8]); wmd = DI("wm", [64, 512]); selgd = DI("selg", [24, 3, 4, 128]); rep16d = DI("rep16", [16, 64])

    yo = DO("yo", [2048, 1024]); ys = DO("ys", [128, 1024])
    kvo = DO("kvo", [2048, 512]); kvs = DO("kvs", [128, 512])
    wino = DO("wino", [512, 256]); wins = DO("wins", [16, 512, 256]); vso = DO("vso", [128, 512])

    outs = []
    with contextlib.ExitStack() as st:
        p = Prog(nc, st)
        ps = [st.enter_context(nc.psum_tensor("ps%d" % i, [128, 512], F32)) for i in range(8)]
        rot = {"g": 0, "a": 0}

        def psg():
            k = rot["g"] % 4
            rot["g"] += 1
            return ps[k], ("ps", k)

        def psa():
            k = 4 + rot["a"] % 4
            rot["a"] += 1
            return ps[k], ("ps", k)

        ident_f = p.sb("ident_f", [128, 128], F32)
        ident = p.sb("ident", [128, 128], BF16)
        p.op("sp", I("dma_start", out=ident_f[:], in_=identd), writes=["ident_f"], dkey="c0")
        p.op("dve", I("tensor_copy", out=ident[:], in_=ident_f[:]), reads=["ident_f"], writes=["ident"])
        g1T_sb = p.sb("g1T_sb", [128, 8], F32); g2T_sb = p.sb("g2T_sb", [128, 8], F32)
        p.op("sp", I("dma_start", out=g1T_sb[:], in_=g1T), writes=["g1T"], dkey="c1")
        p.op("sp", I("dma_start", out=g2T_sb[:], in_=g2T), writes=["g2T"], dkey="c2")
        mixT = p.sb("mixT", [128, 8, 2176], BF16)
        sP = contextlib.ExitStack()
        Qbd = p.sb("Qbd", [128, 16, 64], BF16)
        KnT = p.sb("KnT", [128, 2, 128], BF16)
        p.op("pool", I("memset", ap=Qbd[:], constant=0.0), writes=["Qbd"])
        gates_sb = p.sb("gates_sb", [128, 17, 24], F32)
        KT = p.sb("KT", [128, 4, 4096], BF16, sP)
        vsaug = p.sb("vsaug", [128, 32, 2, 65], BF16, sP)
        vwaug = p.sb("vwaug", [128, 32, 2, 65], BF16, sP)
        qT = p.sb("qT", [128, 4, 2048], BF16, sP)
        p.op("pool", I("memset", ap=vsaug[:, :, :, 64:65], constant=1.0), writes=["vs1"])
        p.op("pool", I("memset", ap=vwaug[:, :, :, 64:65], constant=1.0), writes=["vw1"])

        with contextlib.ExitStack() as s1:
            win_sb = p.sb("win_sb", [128, 8, INC], BF16, s1)
            w_in_v = w_in.rearrange("(kc p) n -> p kc n", p=128)
            for kc in range(8):
                for h in range(2):
                    c0 = h * 1164
                    p.op("pool", I("dma_start", out=win_sb[:, kc, c0:c0 + 1164], in_=w_in_v[:, kc, c0:c0 + 1164]),
                         writes=[("win", kc, h)], dkey=("win", (kc * 2 + h) % 4))
            win_res = [("win", kc, h) for kc in range(8) for h in range(2)]
            lvg_sb = p.sb("lvg_sb", [128, 512], F32, s1); lvb_sb = p.sb("lvb_sb", [128, 512], F32, s1)
            p.op("sp", I("dma_start", out=lvg_sb[:], in_=lvg), writes=["lvg"], dkey="c3")
            p.op("sp", I("dma_start", out=lvb_sb[:], in_=lvb), writes=["lvb"], dkey="c4")
            tril_sb = p.sb("tril_sb", [128, 128], F32, s1)
            p.op("sp", I("dma_start", out=tril_sb[:], in_=trild), writes=["tril"], dkey="c5")
            ws_f = p.sb("ws_f", [128, 8, 128], F32, s1)
            ws_p = p.sb("ws_p", [128, 8, 128], BF16, s1)
            ws_s = p.sb("ws_s", [128, 8, 128], BF16, s1)
            p.op("sp", I("dma_start", out=ws_f[:], in_=wsT.rearrange("g j i -> j g i")), writes=["ws_f"], dkey="c6")
            trb = tril_sb[:, :].unsqueeze(1).broadcast_to([128, 8, 128])
            p.op("dve", I("tensor_tensor", out=ws_p[:], in0=ws_f[:], in1=trb, op=ALU.mult), reads=["ws_f", "tril"], writes=["ws_p"])
            ws_f2 = p.sb("ws_f2", [128, 8, 128], F32, s1)
            p.op("pool", I("memset", ap=ws_f2[:], constant=0.0), writes=["ws_f2"])

            def blk_dma(e, ws_f2=ws_f2):
                r = []
                for b in range(16):
                    r.append(e.dma_start(out=ws_f2[b * 8:(b + 1) * 8, :, b * 8:(b + 1) * 8],
                                         in_=wsT.rearrange("g j i -> j g i")[0:8, :, 0:8],
                                         allow_slow_non_contiguous=True))
                return r
            p.op("sp", blk_dma, reads=[], writes=["ws_f2"], dkey="c7", ndma=16)
            p.op("dve", I("tensor_tensor", out=ws_s[:], in0=ws_f2[:], in1=trb, op=ALU.mult), reads=["ws_f2", "tril"], writes=["ws_s"])
            bs_f = p.sb("bs_f", [8, 2, 128], F32, s1); bs_b = p.sb("bs_b", [8, 2, 128], BF16, s1)
            gind_f = p.sb("gind_f", [8, 512], F32, s1); gind = p.sb("gind", [8, 512], BF16, s1)
            p.op("sp", I("dma_start", out=bs_f[:, 0, :], in_=bs8), writes=["bs_f0"], dkey="c8")
            p.op("sp", I("dma_start", out=bs_f[:, 1, :], in_=bs8s), writes=["bs_f1"], dkey="c9")
            p.op("sp", I("dma_start", out=gind_f[:], in_=gindd), writes=["gind_f"], dkey="c10")
            p.op("dve", I("tensor_copy", out=bs_b[:], in_=bs_f[:]), reads=["bs_f0", "bs_f1"], writes=["bs_b"])
            p.op("dve", I("tensor_copy", out=gind[:], in_=gind_f[:]), reads=["gind_f"], writes=["gind"])

            xbuf = Rot(p, "xb", 2, [128, 1024], F32, s1)
            junk = Rot(p, "junk", 2, [128, 1024], BF16, s1)
            xn = Rot(p, "xn", 2, [128, 1024], BF16, s1)
            hT = Rot(p, "hT", 2, [128, 8, 128], BF16, s1)
            st4 = Rot(p, "st4", 4, [128, 8], F32, s1)
            zkv = Rot(p, "zkv", 2, [128, 512], F32, s1)
            zkvb = Rot(p, "zkvb", 2, [128, 512], BF16, s1)
            zwgb = Rot(p, "zwgb", 2, [128, 256], BF16, s1)
            zwg = Rot(p, "zwg", 2, [128, 280], F32, s1)
            qb = Rot(p, "qb", 2, [128, 512], BF16, s1)
            ub = Rot(p, "ub", 2, [128, 512], BF16, s1)
            vg = Rot(p, "vg", 2, [128, 512], F32, s1)
            vn = Rot(p, "vn", 2, [128, 512], F32, s1)
            vnb = Rot(p, "vnb", 2, [128, 512], BF16, s1)
            ob = Rot(p, "ob", 2, [128, 512], BF16, s1)
            bst = Rot(p, "bst", 2, [128, 6], F32, s1)

            def proj_tile(kind, ti):
                src = {"ctx": xc, "own": xo, "smp": xs}[kind]
                r0 = ti * 128
                xt, xr = xbuf.get()
                p.op("sp", I("dma_start", out=xt[:], in_=src[r0:r0 + 128, :]), writes=[xr], dkey=xr)
                jk, jr = junk.get(); s4, s4r = st4.get()
                p.op("act", I("activation", out=jk[:], in_=xt[:], func=AF.Square, accum_out=s4[:, 0:1]), reads=[xr], writes=[jr, (s4r, 0)])
                p.op("act", I("activation", out=s4[:, 1:2], in_=s4[:, 0:1], func=AF.Sqrt, scale=1.0 / 1024, bias=EPS), reads=[(s4r, 0)], writes=[(s4r, 1)])
                p.op("dve", I("reciprocal", out=s4[:, 2:3], in_=s4[:, 1:2]), reads=[(s4r, 1)], writes=[(s4r, 2)])
                xnt, xnr = xn.get()
                p.op("act", I("activation", out=xnt[:], in_=xt[:], func=AF.Copy, scale=s4[:, 2:3]), reads=[xr, (s4r, 2)], writes=[xnr])
                pt_, ptr = psg()
                ptb = pt_[:].bitcast(BF16)
                p.op("pe", TR([(ptb[:, k * 128:(k + 1) * 128], xnt[:, k * 128:(k + 1) * 128], ident[:]) for k in range(8)]),
                     reads=[xnr, "ident"], writes=[ptr])
                hTt, hTr = hT.get()
                p.op("dve", I("tensor_tensor", out=hTt[:], in0=ptb.rearrange("p (k t) -> p k t", k=8),
                              in1=g1T_sb[:, :].unsqueeze(2).broadcast_to([128, 8, 128]), op=ALU.mult),
                     reads=[ptr, "g1T"], writes=[hTr])

                def zgroup(c0, n):
                    z, zr = psa()
                    p.op("pe", MM([(z[:, 0:n], hTt[:, k, :], win_sb[:, k, c0:c0 + n], k == 0) for k in range(8)]),
                         reads=[hTr] + win_res, writes=[zr])
                    return z, zr

                col = (ti if kind == "ctx" else 16 + ti) * 128
                z, zr = zgroup(C_KV, NKV)
                zk, zkr = zkv.get()
                p.op("act", I("activation", out=zk[:], in_=z[:, :], func=AF.Copy), reads=[zr], writes=[zkr])
                if kind == "own":
                    outs.append(p.op("sp", I("dma_start", out=kvo[r0:r0 + 128, :], in_=zk[:]), reads=[zkr], dkey=("okv", ti % 2)))
                elif kind == "smp":
                    outs.append(p.op("sp", I("dma_start", out=kvs[:, :], in_=zk[:]), reads=[zkr], dkey="okvs"))
                z, zr = zgroup(C_WG, NWG)
                zw, zwr = zwg.get()
                p.op("act", I("activation", out=zw[:, 0:256], in_=z[:, 0:256], func=AF.Copy), reads=[zr], writes=[(zwr, 0)])
                if kind == "own" and ti >= 12:
                    outs.append(p.op("sp", I("dma_start", out=wino[(ti - 12) * 128:(ti - 11) * 128, :], in_=zw[:, 0:256]),
                                     reads=[(zwr, 0)], dkey=("owin", ti % 2)))
                if kind == "smp":
                    def wnew(e):
                        return [e.dma_start(out=wins[b, 504:512, :], in_=zw[b * 8:(b + 1) * 8, 0:256]) for b in range(16)]
                    outs.append(p.op("sp", wnew, reads=[(zwr, 0)], dkey="owins", ndma=16))
                zkbt, zkbr = zkvb.get()
                p.op("pool", I("tensor_copy", out=zkbt[:], in_=zk[:]), reads=[zkr], writes=[zkbr])
                zwbt, zwbr = zwgb.get()
                p.op("pool", I("tensor_copy", out=zwbt[:], in_=zw[:, 0:256]), reads=[(zwr, 0)], writes=[zwbr])
                if kind != "smp":
                    L = ti if kind == "ctx" else 16 + ti
                    pt3, pt3r = psg()
                    pt3b = pt3[:].bitcast(BF16)
                    p.op("pe", TR([(pt3b[:, 0:128], zkbt[:, 0:128], ident[:]), (pt3b[:, 128:256], zkbt[:, 128:256], ident[:]),
                                   (pt3b[:, 256:384], zkbt[:, 256:384], ident[:]), (pt3b[:, 384:512], zwbt[:, 0:128], ident[:])]),
                         reads=[zkbr, zwbr, "ident"], writes=[pt3r])
                    p.op("dve", I("tensor_copy", out=KT[:, :, col:col + 128], in_=pt3b[:, 0:512].rearrange("p (k t) -> p k t", k=4)),
                         reads=[pt3r], writes=[("KT", L)])
                    p.op("pool", I("tensor_copy", out=vsaug[:, L, :, 0:64], in_=zkbt[:, 384:512].rearrange("p (k d) -> p k d", k=2)),
                         reads=[zkbr], writes=[("vsaug", L)])
                    p.op("pool", I("tensor_copy", out=vwaug[:, L, :, 0:64], in_=zwbt[:, 128:256].rearrange("p (k d) -> p k d", k=2)),
                         reads=[zwbr], writes=[("vwaug", L)])
                if kind == "ctx":
                    return
                gi = 16 if kind == "smp" else ti
                p.op("act", I("activation", out=gates_sb[:, gi, :], in_=z[:, 256:280], func=AF.Sigmoid), reads=[zr], writes=[("gates", gi)])
                z, zr = zgroup(C_Q, NQ)
                qbt, qbr = qb.get()
                p.op("act", I("activation", out=qbt[:], in_=z[:, :], func=AF.Copy, scale=0.125), reads=[zr], writes=[qbr])
                pt4, pt4r = psg()
                pt4b = pt4[:].bitcast(BF16)
                p.op("pe", TR([(pt4b[:, k * 128:(k + 1) * 128], qbt[:, k * 128:(k + 1) * 128], ident[:]) for k in range(4)]),
                     reads=[qbr, "ident"], writes=[pt4r])
                if kind == "own":
                    p.op("dve", I("tensor_copy", out=qT[:, :, ti * 128:(ti + 1) * 128], in_=pt4b[:, 0:512].rearrange("p (k t) -> p k t", k=4)),
                         reads=[pt4r], writes=[("qT", ti)])
                else:
                    for k in range(2):
                        hs = slice(k * 64, (k + 1) * 64)
                        for g in range(4):
                            p.op("dve", I("tensor_copy", out=Qbd[hs, :, k * 32 + g * 8:k * 32 + g * 8 + 8],
                                          in_=pt4b[hs, g * 128:(g + 1) * 128].rearrange("p (b q) -> p b q", q=8)),
                                 reads=[pt4r, "Qbd"], writes=["Qbd"])
                    pt5, pt5r = psg()
                    pt5b = pt5[:].bitcast(BF16)
                    p.op("pe", TR([(pt5b[:, 0:128], zkbt[:, 256:384], ident[:]), (pt5b[:, 128:256], zwbt[:, 0:128], ident[:])]),
                         reads=[zkbr, zwbr, "ident"], writes=[pt5r])
                    p.op("dve", I("tensor_copy", out=KnT[:], in_=pt5b[:, 0:256].rearrange("p (k t) -> p k t", k=2)), reads=[pt5r], writes=["KnT"])
                z, zr = zgroup(C_U, NU)
                ut, ur = ub.get()
                p.op("act", I("activation", out=ut[:], in_=z[:, :], func=AF.Gelu_apprx_tanh), reads=[zr], writes=[ur])
                z, zr = zgroup(C_V, NV)
                vgt, vgr = vg.get()
                p.op("act", I("activation", out=vgt[:], in_=z[:, :], func=AF.Gelu_apprx_tanh), reads=[zr], writes=[vgr])
                b6, b6r = bst.get()
                p.op("dve", I("bn_stats", out=b6[:, 0:6], in_=vgt[:]), reads=[vgr], writes=[(b6r, 0)])
                p.op("dve", I("bn_aggr", out=s4[:, 3:5], in_=b6[:, 0:6]), reads=[(b6r, 0)], writes=[(s4r, 3)])
                p.op("act", I("activation", out=s4[:, 5:6], in_=s4[:, 4:5], func=AF.Sqrt, scale=1.0, bias=EPS), reads=[(s4r, 3)], writes=[(s4r, 5)])
                p.op("dve", I("reciprocal", out=s4[:, 6:7], in_=s4[:, 5:6]), reads=[(s4r, 5)], writes=[(s4r, 6)])
                vnt, vnr = vn.get()
                p.op("dve", I("tensor_scalar", out=vnt[:], in0=vgt[:], scalar1=s4[:, 3:4], scalar2=s4[:, 6:7], op0=ALU.subtract, op1=ALU.mult),
                     reads=[vgr, (s4r, 3), (s4r, 6)], writes=[vnr])
                p.op("pool", I("tensor_tensor", out=vnt[:], in0=vnt[:], in1=lvg_sb[:], op=ALU.mult), reads=[vnr, "lvg"], writes=[vnr])
                p.op("pool", I("tensor_tensor", out=vnt[:], in0=vnt[:], in1=lvb_sb[:], op=ALU.add), reads=[vnr, "lvb"], writes=[vnr])
                if kind == "smp":
                    outs.append(p.op("sp", I("dma_start", out=vso[:, :], in_=vnt[:]), reads=[vnr], dkey="ovs"))
                vbt, vbr = vnb.get()
                p.op("pool", I("tensor_copy", out=vbt[:], in_=vnt[:]), reads=[vnr], writes=[vbr])
                sp_, spr = psg()
                wsx = ws_s if kind == "smp" else ws_p
                bi = 1 if kind == "smp" else 0
                lst = [(sp_[:, g * 64:(g + 1) * 64], wsx[:, g, :], vbt[:, g * 64:(g + 1) * 64], g == 0) for g in range(8)]
                lst.append((sp_[:, :], bs_b[:, bi, :], gind[:, :], False))
                p.op("pe", MM(lst), reads=[vbr, "ws_p", "ws_s", "bs_b", "gind"], writes=[spr])
                obt, obr = ob.get()
                p.op("dve", I("tensor_tensor", out=obt[:], in0=sp_[:, :], in1=ut[:], op=ALU.mult), reads=[spr, ur], writes=[obr])
                pt2, pt2r = psg()
                pt2b = pt2[:].bitcast(BF16)
                p.op("pe", TR([(pt2b[:, k * 128:(k + 1) * 128], obt[:, k * 128:(k + 1) * 128], ident[:]) for k in range(4)]),
                     reads=[obr, "ident"], writes=[pt2r])
                mcol = (2048 if kind == "smp" else ti * 128)
                p.op("act", I("activation", out=mixT[:, 4:8, mcol:mcol + 128], in_=pt2b[:, 0:512].rearrange("p (k t) -> p k t", k=4), func=AF.Copy),
                     reads=[pt2r], writes=[("mixT", "b", mcol)])

            proj_tile("smp", 0)
            for ti in range(16):
                proj_tile("ctx", ti)
            for ti in range(16):
                proj_tile("own", ti)
            outs.append(p.op("sp", I("dma_start", out=wins[:, 0:504, :], in_=swin[:, 8:512, :]), dkey="owins2"))
            p.barrier()

        if stage < 2:
            p.op("pool", I("memset", ap=mixT[:, 0:4, :], constant=0.0), writes=[("mixT", "a")])
        elif stage < 3:
            p.op("pool", I("memset", ap=mixT[:, 0:4, 2048:2176], constant=0.0), writes=[("mixT", "a")])

        if stage >= 2:
          with contextlib.ExitStack() as s2:
            W1r = p.sb("W1r", [128, 2, 32, 128], BF16, s2)
            W2c = p.sb("W2c", [128, 2, 16, 128], BF16, s2)
            for stq in range(2):
                v1 = cw1[stq].rearrange("(s d) h -> d s h", d=64)
                for hf in range(2):
                    p.op("pool", I("dma_start", out=W1r[hf * 64:(hf + 1) * 64, stq, :, :], in_=v1), writes=[("W1r", stq, hf)], dkey=("w1r", hf))
                p.op("pool", I("dma_start", out=W2c[:, stq, :, :], in_=cw1[stq].rearrange("(c p) h -> p c h", p=128)), writes=[("W2c", stq)], dkey=("w2c", stq))
            W1r_res = [("W1r", a_, b_) for a_ in range(2) for b_ in range(2)]
            posc = p.sb("posc", [128, 2, 16], BF16, s2)
            p.op("pool", I("dma_start", out=posc[:], in_=cposd), writes=["posc"], dkey="c20")
            b1T = p.sb("b1T", [128, 2], F32, s2)
            p.op("sp", I("dma_start", out=b1T[:], in_=cb1d), writes=["b1T"], dkey="c21")
            w2pad = p.sb("w2pad", [128, 2, 2, 128], BF16, s2)
            p.op("dve", I("memset", ap=w2pad[:], constant=0.0), writes=["w2pad"])

            def w2dma(e, w2pad=w2pad):
                r = []
                for stq in range(2):
                    for va in range(2):
                        r.append(e.dma_start(out=w2pad[:, stq, va, va * 64:(va + 1) * 64], in_=cw2[stq]))
                return r
            p.op("pool", w2dma, writes=["w2pad"], dkey="c22", ndma=4)
            b2k2 = p.sb("b2k2", [128, 1], F32, s2)
            p.op("sp", I("dma_start", out=b2k2[:], in_=cb2kd), writes=["b2k2"], dkey="c23")
            b2vB = p.sb("b2vB", [128, 128], F32, s2)
            p.op("sp", I("dma_start", out=b2vB[:], in_=cb2vd), writes=["b2vB"], dkey="c24")
            kcT = p.sb("kcT", [128, 256], BF16, s2)
            vcaug = p.sb("vcaug", [128, 2, 2, 128], BF16, s2)
            p.op("dve", I("memset", ap=kcT[:], constant=0.0), writes=["kcT"])
            p.op("dve", I("memset", ap=vcaug[:], constant=0.0), writes=["vcaug"])

            def covdma(e, vcaug=vcaug):
                r = []
                for c2 in range(2):
                    for k in range(2):
                        r.append(e.dma_start(out=vcaug[:, c2, k, 64:128], in_=coverd[c2]))
                return r
            p.op("pool", covdma, writes=["vcaug"], dkey="c25", ndma=4)
            b1tot = p.sb("b1tot", [128, 2], F32, s2)
            z, zr = psg()
            lst = []
            for stq in range(2):
                for c in range(16):
                    lst.append((z[:, stq:stq + 1], W2c[:, stq, c, :], posc[:, stq, c:c + 1], (stq == 0 and c == 0)))
            p.op("pe", MM(lst), reads=[("W2c", 0), ("W2c", 1), "posc"], writes=[zr])
            p.op("dve", I("tensor_tensor", out=b1tot[:], in0=z[:, 0:2], in1=b1T[:], op=ALU.add), reads=[zr, "b1T"], writes=["b1tot"])
            ghT = p.sb("ghT", [128, 2, 2, 256], BF16, s2)
            KT_res = [("KT", L) for L in range(32)]
            for stq in range(2):
                for k in range(2):
                    z, zr = psg()
                    hs = slice(k * 64, (k + 1) * 64)
                    p.op("pe", MM([(z[:, 0:255], W1r[hs, stq, s_, :], KT[hs, stq, s_:s_ + 16 * 254 + 1:16], s_ == 0) for s_ in range(32)]),
                         reads=W1r_res + KT_res, writes=[zr])
                    p.op("act", I("activation", out=ghT[:, stq, k, 0:255], in_=z[:, 0:255], func=AF.Gelu_apprx_tanh, bias=b1tot[:, stq:stq + 1]),
                         reads=[zr, "b1tot"], writes=[("ghT", stq, k)])
            z, zr = psg()
            p.op("pe", MM([(z[:, 0:255], w2pad[:, 0, 0, :], ghT[:, 0, 0, 0:255], True), (z[:, 0:255], w2pad[:, 0, 1, :], ghT[:, 0, 1, 0:255], False)]),
                 reads=["w2pad", ("ghT", 0, 0), ("ghT", 0, 1)], writes=[zr])
            p.op("act", I("activation", out=kcT[:, 0:255], in_=z[:, 0:255], func=AF.Identity, bias=b2k2[:, 0:1]), reads=[zr, "b2k2"], writes=["kcT"])
            for c2 in range(2):
                nb = 128 if c2 == 0 else 127
                z, zr = psg()
                p.op("pe", MM([(z[0:nb, k * 64:(k + 1) * 64], ghT[:, 1, k, c2 * 128:c2 * 128 + nb], w2pad[:, 1, 0, 0:64], k == 0) for k in range(2)]),
                     reads=["w2pad", ("ghT", 1, 0), ("ghT", 1, 1)], writes=[zr])
                p.op("dve", I("tensor_tensor", out=vcaug[0:nb, c2, :, 0:64], in0=z[0:nb, 0:128].rearrange("p (k d) -> p k d", k=2),
                              in1=b2vB[0:nb, :].rearrange("p (k d) -> p k d", k=2), op=ALU.add),
                     reads=[zr, "b2vB", "vcaug"], writes=["vcaug"])

            cmpm = p.sb("cmpm", [128, 8, 512], BF16, s2)
            bandm = p.sb("bandm", [128, 8, 512], BF16, s2)
            negc = p.sb("negc", [128, 4, 512], BF16, s2)
            Eexp = p.sb("Eexp", [64, 32, 128], BF16, s2)
            for c in range(8):
                p.op("pool", I("dma_start", out=cmpm[:, c, :], in_=cmpmd[c]), writes=[("cmpm", c)], dkey=("cm", c % 2))
                p.op("pool", I("dma_start", out=bandm[:, c, :], in_=bandmd[c]), writes=[("bandm", c)], dkey=("bm", c % 2))
            for c in range(4):
                p.op("pool", I("dma_start", out=negc[:, c, :], in_=negcd[c]), writes=[("negc", c)], dkey=("nm", c % 2))
                p.op("pool", I("dma_start", out=Eexp[:, c * 8:(c + 1) * 8, :], in_=Ed[:, c * 8:(c + 1) * 8, :]), writes=[("E", c)], dkey=("em", c % 2))
            selb_sb = p.sb("selb_sb", [128, 16, 64], F32, s2)
            vis_sb = p.sb("vis_sb", [128, 16, 64], F32, s2)
            ctxb = p.sb("ctxb", [128, 2], F32, s2)
            p.op("sp", I("dma_start", out=selb_sb[:], in_=selbd), writes=["selb"], dkey="c26")
            p.op("sp", I("dma_start", out=vis_sb[:], in_=visd), writes=["vis"], dkey="c27")
            p.op("sp", I("dma_start", out=ctxb[:], in_=ctxbd), writes=["ctxb"], dkey="c28")
            pT = Rot(p, "pT", 4, [128, 512], BF16, s2)
            pTm = Rot(p, "pTm", 5, [128, 512], BF16, s2)
            maskb = Rot(p, "maskb", 2, [128, 512], BF16, s2)
            pcm = p.sb("pcm", [128, 8, 512], BF16, s2)
            selT = Rot(p, "selT", 2, [64, 512], BF16, s2)
            selbf = Rot(p, "selbf", 2, [128, 64], BF16, s2)
            sc_r = Rot(p, "sc", 2, [128, 64], F32, s2)
            scr_r = Rot(p, "scr", 2, [128, 64], F32, s2)
            sm = Rot(p, "sm", 4, [128, 32], F32, s2)
            oacc = Rot(p, "oacc", 8, [128, 4, 64], F32, s2)
            otmp = Rot(p, "otmp", 2, [128, 4, 64], F32, s2)
            oa_tok = p.sb("oa_tok", [128, 4, 512], BF16, s2)
            mres = [("cmpm", c) for c in range(8)] + [("bandm", c) for c in range(8)] + [("negc", c) for c in range(4)] + [("E", c) for c in range(4)]

            def sT_exp(kstream, kcol, k, g, qt, use_ctx):
                hs = slice(k * 64, (k + 1) * 64)
                z, zr = psg()
                if kstream is None:
                    lhsT = kcT[hs, kcol:kcol + 128]
                    rd = ["kcT"]
                else:
                    lhsT = KT[hs, kstream, kcol:kcol + 128]
                    rd = [("KT", kcol // 128)]
                p.op("pe", MM([(z[:, :], lhsT, qT[hs, g, qt * 512:(qt + 1) * 512], True)]),
                     reads=rd + [("qT", qt * 4 + j) for j in range(4)], writes=[zr])
                t, tr = pT.get()
                bi = 0 if use_ctx else 1
                p.op("act", I("activation", out=t[:], in_=z[:, :], func=AF.Exp, bias=ctxb[:, bi:bi + 1]), reads=[zr, "ctxb"], writes=[tr])
                return t, tr

            def finish_branch(obanks, br, k, qt, accs, width, first):
                for sub in range(4):
                    ti = qt * 4 + sub
                    o, orr = obanks[sub]
                    o3 = o[:, 0:4 * width].rearrange("p (g w) -> p g w", g=4)
                    m, mr = sm.get()
                    if width == 128:
                        p.op("dve", I("tensor_reduce", out=m[:, 0:4], in_=o3[:, :, 64:128], axis=AX.X, op=ALU.add), reads=[orr], writes=[(mr, 0)])
                    else:
                        p.op("dve", I("tensor_copy", out=m[:, 0:4], in_=o3[:, :, 64]), reads=[orr], writes=[(mr, 0)])
                    p.op("dve", I("tensor_scalar", out=m[:, 12:16], in0=m[:, 0:4], scalar1=1e-30, scalar2=None, op0=ALU.max), reads=[(mr, 0)], writes=[(mr, 5)])
                    p.op("dve", I("reciprocal", out=m[:, 4:8], in_=m[:, 12:16]), reads=[(mr, 5)], writes=[(mr, 1)])
                    gc = br * 8 + k * 4
                    p.op("dve", I("tensor_tensor", out=m[:, 8:12], in0=m[:, 4:8], in1=gates_sb[:, ti, gc:gc + 4], op=ALU.mult),
                         reads=[(mr, 1), ("gates", ti)], writes=[(mr, 2)])
                    coefb = m[:, 8:12].unsqueeze(2).broadcast_to([128, 4, 64])
                    a, ar = accs[sub]
                    if first:
                        p.op("dve", I("tensor_tensor", out=a[:], in0=o3[:, :, 0:64], in1=coefb, op=ALU.mult), reads=[orr, (mr, 2)], writes=[ar])
                    else:
                        tt, ttr = otmp.get()
                        p.op("dve", I("tensor_tensor", out=tt[:], in0=o3[:, :, 0:64], in1=coefb, op=ALU.mult), reads=[orr, (mr, 2)], writes=[ttr])
                        p.op("pool", I("tensor_tensor", out=a[:], in0=a[:], in1=tt[:], op=ALU.add), reads=[ttr, ar], writes=[ar])
                    if width == 128:
                        sc, scr = sc_r.get()
                        for g in range(4):
                            in1 = selb_sb[:, ti, :] if g == 0 else sc[:]
                            p.op("dve", I("scalar_tensor_tensor", out=sc[:], in0=o3[:, g, 64:128], scalar=m[:, 4 + g:5 + g], in1=in1, op0=ALU.mult, op1=ALU.add),
                                 reads=[orr, (mr, 1), "selb", scr], writes=[scr])
                        p.op("dve", I("max", out=m[:, 16:24], in_=sc[:]), reads=[scr], writes=[(mr, 3)])
                        s2_, s2r = scr_r.get()
                        p.op("dve", I("match_replace", out=s2_[:], in_to_replace=m[:, 16:24], in_values=sc[:], imm_value=-3.0e38), reads=[scr, (mr, 3)], writes=[s2r])
                        p.op("dve", I("max", out=m[:, 24:32], in_=s2_[:]), reads=[s2r], writes=[(mr, 4)])
                        sb_, sbr = selbf.get()
                        p.op("dve", I("scalar_tensor_tensor", out=sb_[:], in0=sc[:], scalar=m[:, 31:32], in1=vis_sb[:, ti, :], op0=ALU.is_ge, op1=ALU.mult),
                             reads=[scr, (mr, 4), "vis"], writes=[sbr])
                        z, zr = psg()
                        zb = z[:].bitcast(BF16)
                        p.op("pe", TR([(zb[0:64, 0:128], sb_[:, :], ident[:])]), reads=[sbr, "ident"], writes=[zr])
                        p.op("act", I("activation", out=cur_selT[0][0:64, sub * 128:(sub + 1) * 128], in_=zb[0:64, 0:128], func=AF.Copy),
                             reads=[zr], writes=[(cur_selT[1], sub)])

            cur_selT = [None, None]
            for qt in range(4):
                for k in range(2):
                    hs = slice(k * 64, (k + 1) * 64)
                    accs = [oacc.get() for _ in range(4)]
                    cur_selT[0], cur_selT[1] = selT.get()
                    for c2 in range(2):
                        for g in range(4):
                            t, tr = sT_exp(None, c2 * 128, k, g, qt, c2 == 0)
                            p.op("pool", I("tensor_tensor", out=pcm[:, c2 * 4 + g, :], in0=t[:], in1=cmpm[:, qt * 2 + c2, :], op=ALU.mult),
                                 reads=[tr] + mres, writes=[("pcm", c2 * 4 + g)])
                    ob = [psa() for _ in range(4)]
                    for sub in range(4):
                        o, orr = ob[sub]
                        lst = []
                        for g in range(4):
                            for c2 in range(2):
                                lst.append((o[:, g * 128:(g + 1) * 128], pcm[:, c2 * 4 + g, sub * 128:(sub + 1) * 128], vcaug[:, c2, k, :], (g == 0 and c2 == 0)))
                        p.op("pe", MM(lst), reads=[("pcm", j) for j in range(8)] + ["vcaug"], writes=[orr])
                    finish_branch(ob, 0, k, qt, accs, 128, True)
                    nch = 16 + 4 * (qt + 1)
                    ob = [psa() for _ in range(4)]
                    obr = [r_ for (_, r_) in ob]
                    pend = []
                    LAG = 2

                    def pv_slc(it, ob=ob, obr=obr, k=k):
                        tm, tmr, c, g = it
                        p.op("pe", MM([(ob[sub][0][:, g * 65:(g + 1) * 65], tm[:, sub * 128:(sub + 1) * 128], vsaug[:, c, k, :], (c == 0 and g == 0))
                                       for sub in range(4)]),
                             reads=[tmr, ("vsaug", c), "vs1"], writes=obr)
                    for c in range(nch):
                        z, zr = psg()
                        d = c - (16 + 4 * qt)
                        lst = [(z[:, :], Eexp[:, c, :], cur_selT[0][:, :], True)]
                        if d >= 0:
                            lst.append((z[:, :], ident[:], negc[:, d, :], False))
                        p.op("pe", MM(lst), reads=[(cur_selT[1], j) for j in range(4)] + mres + ["ident"], writes=[zr])
                        mb, mbr = maskb.get()
                        p.op("dve", I("tensor_scalar", out=mb[:], in0=z[:, :], scalar1=0.0, scalar2=None, op0=ALU.max), reads=[zr], writes=[mbr])
                        for g in range(4):
                            t, tr = sT_exp(2, c * 128, k, g, qt, c < 16)
                            tm, tmr = pTm.get()
                            p.op("dve" if g % 2 == 0 else "pool", I("tensor_tensor", out=tm[:], in0=t[:], in1=mb[:], op=ALU.mult), reads=[tr, mbr], writes=[tmr])
                            pend.append((tm, tmr, c, g))
                            if len(pend) > LAG:
                                pv_slc(pend.pop(0))
                    while pend:
                        pv_slc(pend.pop(0))
                    finish_branch(ob, 1, k, qt, accs, 65, False)
                    ob = [psa() for _ in range(4)]
                    obr = [r_ for (_, r_) in ob]
                    pend = []

                    def pv_win(it, ob=ob, obr=obr, k=k):
                        tm, tmr, c, g, kc_ = it
                        p.op("pe", MM([(ob[sub][0][:, g * 65:(g + 1) * 65], tm[:, sub * 128:(sub + 1) * 128], vwaug[:, kc_, k, :], (c == 0 and g == 0))
                                       for sub in range(4)]),
                             reads=[tmr, ("vwaug", kc_), "vw1"], writes=obr)
                    for c in range(8):
                        kc_ = 16 + 4 * qt - 4 + c
                        for g in range(4):
                            t, tr = sT_exp(3, kc_ * 128, k, g, qt, kc_ < 16)
                            tm, tmr = pTm.get()
                            p.op("dve" if g % 2 == 0 else "pool", I("tensor_tensor", out=tm[:], in0=t[:], in1=bandm[:, c, :], op=ALU.mult), reads=[tr] + mres, writes=[tmr])
                            pend.append((tm, tmr, c, g, kc_))
                            if len(pend) > LAG:
                                pv_win(pend.pop(0))
                    while pend:
                        pv_win(pend.pop(0))
                    finish_branch(ob, 2, k, qt, accs, 65, False)
                    for sub in range(4):
                        a, ar = accs[sub]
                        p.op("pool", I("tensor_copy", out=oa_tok[:, sub, k * 256:(k + 1) * 256], in_=a[:].rearrange("p g d -> p (g d)")),
                             reads=[ar], writes=[("oa_tok", sub, k)])
                for sub in range(4):
                    ti = qt * 4 + sub
                    z, zr = psg()
                    zb = z[:].bitcast(BF16)
                    p.op("pe", TR([(zb[:, j * 128:(j + 1) * 128], oa_tok[:, sub, j * 128:(j + 1) * 128], ident[:]) for j in range(4)]),
                         reads=[("oa_tok", sub, 0), ("oa_tok", sub, 1), "ident"], writes=[zr])
                    p.op("act", I("activation", out=mixT[:, 0:4, ti * 128:(ti + 1) * 128], in_=zb[:, 0:512].rearrange("p (k t) -> p k t", k=4), func=AF.Copy),
                         reads=[zr], writes=[("mixT", "a", ti * 128)])
            p.barrier()

        sP.close()
        import os as _os0
        if stage >= 3 and _os0.environ.get('DBG_NOS') != '1':
          with contextlib.ExitStack() as sS:
            idx_i = p.sb("idx_i", [128, 512], I32, sS); idx_f = p.sb("idx_f", [128, 512], F32, sS); idx = p.sb("idx", [128, 512], I32, sS)
            pcol = p.sb("pcol", [128, 1], F32, sS)
            p.op("sp", I("dma_start", out=idx_i[:], in_=ptxd.rearrange("p b c -> p (b c)")), writes=["idx_i"], dkey="c30")
            p.op("sp", I("dma_start", out=pcol[:], in_=pcold), writes=["pcol"], dkey="c31")
            p.op("dve", I("tensor_copy", out=idx_f[:], in_=idx_i[:]), reads=["idx_i"], writes=["idx_f"])
            p.op("dve", I("tensor_scalar", out=idx_f[:], in0=idx_f[:], scalar1=64.0, scalar2=pcol[:, 0:1], op0=ALU.mult, op1=ALU.add),
                 reads=["idx_f", "pcol"], writes=["idx_f"])
            p.op("dve", I("tensor_copy", out=idx[:], in_=idx_f[:]), reads=["idx_f"], writes=["idx"])
            W1s = p.sb("W1s", [128, 2, 32, 128], BF16, sS)
            for stq in range(2):
                v1 = cw1[stq].rearrange("(s d) h -> d s h", d=64)
                for hf in range(2):
                    p.op("pool", I("dma_start", out=W1s[hf * 64:(hf + 1) * 64, stq, :, :], in_=v1), writes=[("W1s", stq, hf)], dkey=("w1r", hf))
            W1s_res = [("W1s", a_, b_) for a_ in range(2) for b_ in range(2)]
            posd_sb = p.sb("posd_sb", [64, 2, 32], BF16, sS)
            p.op("pool", I("dma_start", out=posd_sb[:], in_=cpos64d), writes=["posd"], dkey="c20")
            b1T = p.sb("b1Ts", [128, 2], F32, sS)
            p.op("sp", I("dma_start", out=b1T[:], in_=cb1d), writes=["b1T"], dkey="c21")
            w2pad = p.sb("w2pads", [128, 2, 2, 128], BF16, sS)
            p.op("dve", I("memset", ap=w2pad[:], constant=0.0), writes=["w2pad"])

            def w2dma_s(e, w2pad=w2pad):
                r = []
                for stq in range(2):
                    for va in range(2):
                        r.append(e.dma_start(out=w2pad[:, stq, va, va * 64:(va + 1) * 64], in_=cw2[stq]))
                return r
            p.op("pool", w2dma_s, writes=["w2pad"], dkey="c22", ndma=4)
            b2k2 = p.sb("b2k2s", [128, 1], F32, sS)
            p.op("sp", I("dma_start", out=b2k2[:], in_=cb2kd), writes=["b2k2"], dkey="c23")
            b2vB = p.sb("b2vBs", [128, 128], F32, sS)
            p.op("sp", I("dma_start", out=b2vB[:], in_=cb2vd), writes=["b2vB"], dkey="c24")
            b1tot = p.sb("b1tots", [128, 2], F32, sS)
            z, zr = psg()
            lst = []
            for stq in range(2):
                for c in range(32):
                    lst.append((z[:, stq:stq + 1], W1s[0:64, stq, c, :], posd_sb[:, stq, c:c + 1], (stq == 0 and c == 0)))
            p.op("pe", MM(lst), reads=W1s_res + ["posd"], writes=[zr])
            p.op("dve", I("tensor_tensor", out=b1tot[:], in0=z[:, 0:2], in1=b1T[:], op=ALU.add), reads=[zr, "b1T"], writes=["b1tot"])
            fbs = p.sb("fbs", [64, 129], F32, sS); sel2 = p.sb("sel2", [64, 64], F32, sS)
            cm8f = p.sb("cm8f", [64, 8], F32, sS); cm8 = p.sb("cm8", [64, 8], BF16, sS)
            wm = p.sb("wm", [64, 512], BF16, sS); selg = p.sb("selg", [24, 3, 4, 128], F32, sS)
            p.op("sp", I("dma_start", out=fbs[:], in_=fbsd), writes=["fbs"], dkey="c32")
            p.op("sp", I("dma_start", out=sel2[:], in_=sel2d), writes=["sel2"], dkey="c33")
            p.op("sp", I("dma_start", out=cm8f[:], in_=cm8d), writes=["cm8f"], dkey="c34")
            p.op("dve", I("tensor_copy", out=cm8[:], in_=cm8f[:]), reads=["cm8f"], writes=["cm8"])
            p.op("pool", I("dma_start", out=wm[:], in_=wmd), writes=["wm"], dkey="c35")
            p.op("sp", I("dma_start", out=selg[:], in_=selgd), writes=["selg"], dkey="c36")
            vcs = p.sb("vcs", [128, 4, 257], BF16, sS)
            p.op("dve", I("memset", ap=vcs[:], constant=0.0), writes=["vcs"])
            p.op("pool", I("dma_start", out=vcs[:, :, 128:257], in_=covsd), writes=["vcs"], reads=["vcs"], dkey="c37")
            Vn = p.sb("Vn", [8, 2, 16, 129], BF16, sS)
            p.op("dve", I("memset", ap=Vn[:], constant=1.0), writes=["Vn"])
            p.op("pool", I("dma_start", out=Vn[:, 0, :, 0:128], in_=kvs.rearrange("(b q) c -> q b c", q=8)[:, :, 384:512]), reads=["Vn"], writes=["Vn"], dkey="c38")
            p.op("pool", I("dma_start", out=Vn[:, 1, :, 0:128], in_=wins[:, 504:512, 128:256].rearrange("b q c -> q b c")), reads=["Vn"], writes=["Vn"], dkey="c39")
            X2T = p.sb("X2T", [128, 2, 2, 4096], BF16, sS)
            KsT = p.sb("KsT", [128, 2, 4096], BF16, sS)
            Vs = p.sb("Vs", [128, 32, 2, 129], BF16, sS)
            p.op("pool", I("memset", ap=Vs[:, :, :, 128:129], constant=1.0), writes=["Vs1"])
            G = Rot(p, "G", 4, [128, 1024], BF16, sS)
            gh = p.sb("gh", [128, 2, 2, 512], BF16, sS)
            kcTs = p.sb("kcTs", [128, 512], BF16, sS)
            Pc = Rot(p, "Pc", 2, [64, 512], BF16, sS)
            for t_ in Pc.t:
                p.op("dve", I("memset", ap=t_[:], constant=0.0), writes=[("Pc", Pc.t.index(t_))])
            PcT = Rot(p, "PcT", 2, [128, 4, 64], BF16, sS)
            Pq = Rot(p, "Pq", 3, [64, 512], BF16, sS)
            Pqm = Rot(p, "Pqm", 4, [64, 512], BF16, sS)
            PsT = Rot(p, "PsT", 4, [128, 4, 64], BF16, sS)
            Pn = Rot(p, "Pn", 2, [64, 8], BF16, sS); Pnm = Rot(p, "Pnm", 2, [64, 8], BF16, sS); PnT = Rot(p, "PnT", 2, [8, 64], BF16, sS)
            on_r = Rot(p, "on", 3, [64, 128], BF16, sS)
            smm = Rot(p, "smm", 6, [64, 32], F32, sS)
            Pe32 = Rot(p, "Pe32", 2, [64, 512], F32, sS)
            Pg = Rot(p, "Pg", 2, [128, 4, 16], BF16, sS)
            Pg32 = Rot(p, "Pg32", 2, [128, 4, 16], F32, sS)
            sel16 = Rot(p, "sel16", 2, [16, 129], BF16, sS)
            rep16f = p.sb("rep16f", [16, 64], F32, sS); rep16 = p.sb("rep16", [16, 64], BF16, sS)
            p.op("sp", I("dma_start", out=rep16f[:], in_=rep16d), writes=["rep16f"], dkey="c41")
            p.op("dve", I("tensor_copy", out=rep16[:], in_=rep16f[:]), reads=["rep16f"], writes=["rep16"])
            scs = Rot(p, "scs", 2, [64, 129], F32, sS); scs2 = Rot(p, "scs2", 2, [64, 129], F32, sS)
            sel_r = Rot(p, "sel", 2, [64, 129], BF16, sS)
            SW = Rot(p, "SW", 2, [128, 4, 256], BF16, sS)
            SWv = Rot(p, "SWv", 2, [128, 4, 129], BF16, sS)
            for i_, t_ in enumerate(SWv.t):
                p.op("pool", I("memset", ap=t_[:, :, 128:129], constant=1.0), writes=[("SWv1", i_)])
            KwT = Rot(p, "KwT", 2, [128, 512], BF16, sS)
            OBR = p.sb("OBR", [128, 3, 4, 128], F32, sS)
            id64 = ident[0:64, 0:64]
            G_all = [("G", i_) for i_ in range(4)]

            def to_obr(ont, onr, br, b):
                z, zr = psg()
                zb = z[:].bitcast(BF16)
                p.op("pe", TR([(zb[:, 0:64], ont[:, :], id64)]), reads=[onr, "ident"], writes=[zr])
                for k in range(2):
                    hs = slice(k * 64, (k + 1) * 64)
                    p.op("act", I("activation", out=OBR[hs, br, :, b * 8:(b + 1) * 8], in_=zb[hs, k * 32:(k + 1) * 32].rearrange("p (g q) -> p g q", q=8), func=AF.Copy),
                         reads=[zr], writes=[("OBR", br, b, k)])

            def finish_s(o, orr, br, b, rs_from_cover):
                m, mr = smm.get()
                p.op("dve", I("tensor_scalar", out=m[:, 1:2], in0=o[0:64, 128:129], scalar1=1e-30, scalar2=None, op0=ALU.max), reads=[orr], writes=[(mr, 1)])
                p.op("dve", I("reciprocal", out=m[:, 2:3], in_=m[:, 1:2]), reads=[(mr, 1)], writes=[(mr, 2)])
                ont, onr = on_r.get()
                p.op("dve", I("tensor_scalar", out=ont[:], in0=o[0:64, 0:128], scalar1=m[:, 2:3], scalar2=None, op0=ALU.mult), reads=[orr, (mr, 2)], writes=[onr])
                to_obr(ont, onr, br, b)
                return m, mr

            def new_keys(o, orr, which, b, tag):
                z, zr = psg()
                p.op("pe", MM([(z[0:64, 0:8], Qbd[:, b, :], KnT[:, which, b * 8:(b + 1) * 8], True)]), reads=["Qbd", "KnT"], writes=[zr])
                t, tr = Pn.get()
                p.op("act", I("activation", out=t[:], in_=z[0:64, 0:8], func=AF.Exp), reads=[zr], writes=[tr])
                tm, tmr = Pnm.get()
                p.op("dve", I("tensor_tensor", out=tm[:], in0=t[:], in1=cm8[:], op=ALU.mult), reads=[tr, "cm8"], writes=[tmr])
                z2, z2r = psg()
                z2b = z2[:].bitcast(BF16)
                p.op("pe", TR([(z2b[0:8, 0:64], tm[:, :], id64)]), reads=[tmr, "ident"], writes=[z2r])
                tt, ttr = PnT.get()
                p.op("act", I("activation", out=tt[:], in_=z2b[0:8, 0:64], func=AF.Copy), reads=[z2r], writes=[ttr])
                p.op("pe", MM([(o[0:64, 0:129], tt[:, :], Vn[:, which, b, :], False)]), reads=[ttr, "Vn"], writes=[orr])

            def dk_A(kT_ap, kT_res, mask_fn):
                z, zr = psg()
                p.op("pe", MM([(z[0:64, :], Qbd[:, b_cur[0], :], kT_ap, True)]), reads=["Qbd"] + kT_res, writes=[zr])
                t, tr = Pq.get()
                p.op("act", I("activation", out=t[:], in_=z[0:64, :], func=AF.Exp), reads=[zr], writes=[tr])
                tm, tmr = Pqm.get()
                mask_fn(tm, tmr, t, tr)
                return tm, tmr

            def dk_B(tm, tmr):
                z2, z2r = psg()
                z2b = z2[:].bitcast(BF16)
                p.op("pe", TR([(z2b[:, j * 64:(j + 1) * 64], tm[:, j * 128:(j + 1) * 128], id64) for j in range(4)]), reads=[tmr, "ident"], writes=[z2r])
                tt, ttr = PsT.get()
                p.op("act", I("activation", out=tt[:], in_=z2b[:, 0:256].rearrange("p (j c) -> p j c", j=4), func=AF.Copy), reads=[z2r], writes=[ttr])
                return tt, ttr

            def dk_C(o, orr, tt, ttr, v_fn, v_res, first):
                p.op("pe", MM([(o[0:64, 0:129], tt[:, j, :], v_fn(j), first and j == 0) for j in range(4)]), reads=[ttr] + v_res, writes=[orr])

            def dense_keys(o, orr, kT_ap, kT_res, mask_fn, v_fn, v_res, first):
                tm, tmr = dk_A(kT_ap, kT_res, mask_fn)
                tt, ttr = dk_B(tm, tmr)
                dk_C(o, orr, tt, ttr, v_fn, v_res, first)

            b_cur = [0]
            cache_v = cache
            import os as _os
            SPART = int(_os.environ.get('DBG_SPART', '5'))
            for b in range(int(_os.environ.get('DBG_NB', '16'))):
                b_cur[0] = b
                for c in range(32):
                    Gt, Gr = G.get()
                    p.op("pool", I("indirect_dma_start", out=Gt[:], out_offset=None, in_=cache_v,
                                   in_offset=bass.IndirectOffsetOnAxis(ap=idx[:, b * 32 + c:b * 32 + c + 1], axis=0)),
                         reads=["idx"], writes=[Gr], dkey=Gr)
                    if _os.environ.get('DBG_NOTR') == '1':
                        p.op("pool", I("tensor_copy", out=Vs[:, c, :, 0:128], in_=Gt[:].rearrange("p (t r) -> p t r", t=2)[:, :, 384:512]), reads=[Gr], writes=[("Vs", c)])
                        continue
                    z, zr = psg()
                    zb = z[:].bitcast(BF16)
                    lst = []
                    for stq in range(3):
                        for t2 in range(2):
                            o_ = t2 * 512 + stq * 128
                            lst.append((zb[:, (stq * 2 + t2) * 128:(stq * 2 + t2 + 1) * 128], Gt[:, o_:o_ + 128], ident[:]))
                    p.op("pe", TR(lst), reads=[Gr, "ident"], writes=[zr])
                    p.op("act", I("activation", out=X2T[:, :, :, c * 128:(c + 1) * 128], in_=zb[:, 0:512].rearrange("p (s t r) -> p s t r", s=2, t=2), func=AF.Copy),
                         reads=[zr], writes=[("X2T", c)])
                    p.op("dve", I("tensor_copy", out=KsT[:, :, c * 128:(c + 1) * 128], in_=zb[:, 512:768].rearrange("p (k t) -> p k t", k=2)),
                         reads=[zr], writes=[("KsT", c)])
                    p.op("pool", I("tensor_copy", out=Vs[:, c, :, 0:128], in_=Gt[:].rearrange("p (t r) -> p t r", t=2)[:, :, 384:512]), reads=[Gr], writes=[("Vs", c)])
                X2T_res = [("X2T", c) for c in range(32)]
                if SPART < 2:
                    continue
                for stq in range(2):
                    zz = [psg(), psg()]
                    lst = []
                    for s_ in range(32):
                        for k in range(2):
                            hs = slice(k * 64, (k + 1) * 64)
                            lst.append((zz[k][0][:, 0:511], W1s[hs, stq, s_, :], X2T[hs, stq, s_ % 2, (s_ // 2):(s_ // 2) + 8 * 510 + 1:8], s_ == 0))
                    p.op("pe", MM(lst), reads=W1s_res + X2T_res, writes=[zz[0][1], zz[1][1]])
                    for k in range(2):
                        p.op("act", I("activation", out=gh[:, stq, k, 0:511], in_=zz[k][0][:, 0:511], func=AF.Gelu_apprx_tanh, bias=b1tot[:, stq:stq + 1]),
                             reads=[zz[k][1], "b1tot"], writes=[("gh", stq, k)])
                z, zr = psg()
                p.op("pe", MM([(z[:, 0:511], w2pad[:, 0, 0, :], gh[:, 0, 0, 0:511], True), (z[:, 0:511], w2pad[:, 0, 1, :], gh[:, 0, 1, 0:511], False)]),
                     reads=["w2pad", ("gh", 0, 0), ("gh", 0, 1)], writes=[zr])
                p.op("act", I("activation", out=kcTs[:, 0:511], in_=z[:, 0:511], func=AF.Identity, bias=b2k2[:, 0:1]), reads=[zr, "b2k2"], writes=["kcTs"])
                for c4 in range(4):
                    nb = 128 if c4 < 3 else 127
                    z, zr = psg()
                    p.op("pe", MM([(z[0:nb, k * 64:(k + 1) * 64], gh[:, 1, k, c4 * 128:c4 * 128 + nb], w2pad[:, 1, 0, 0:64], k == 0) for k in range(2)]),
                         reads=["w2pad", ("gh", 1, 0), ("gh", 1, 1)], writes=[zr])
                    p.op("dve", I("tensor_tensor", out=vcs[0:nb, c4, 0:128], in0=z[0:nb, 0:128], in1=b2vB[0:nb, :], op=ALU.add),
                         reads=[zr, "b2vB", "vcs"], writes=[("vcs", c4)])
                if SPART < 3:
                    continue
                z, zr = psg()
                p.op("pe", MM([(z[0:64, 0:511], Qbd[:, b, :], kcTs[:, 0:511], True)]), reads=["Qbd", "kcTs"], writes=[zr])
                m, mr = smm.get()
                pe_, per = Pe32.get()
                p.op("act", I("activation", out=pe_[:, 0:511], in_=z[0:64, 0:511], func=AF.Exp, accum_out=m[:, 0:1]), reads=[zr], writes=[per, (mr, 0)])
                p.op("dve", I("tensor_scalar", out=m[:, 1:2], in0=m[:, 0:1], scalar1=1e-30, scalar2=None, op0=ALU.max), reads=[(mr, 0)], writes=[(mr, 1)])
                p.op("dve", I("reciprocal", out=m[:, 2:3], in_=m[:, 1:2]), reads=[(mr, 1)], writes=[(mr, 2)])
                pc, pcr = Pc.get()
                p.op("dve", I("tensor_scalar", out=pc[:, 0:511], in0=pe_[:, 0:511], scalar1=m[:, 2:3], scalar2=None, op0=ALU.mult), reads=[per, (mr, 2)], writes=[pcr])
                z2, z2r = psg()
                z2b = z2[:].bitcast(BF16)
                p.op("pe", TR([(z2b[:, j * 64:(j + 1) * 64], pc[:, j * 128:(j + 1) * 128], id64) for j in range(4)]), reads=[pcr, "ident"], writes=[z2r])
                pct, pctr = PcT.get()
                p.op("act", I("activation", out=pct[:], in_=z2b[:, 0:256].rearrange("p (j c) -> p j c", j=4), func=AF.Copy), reads=[z2r], writes=[pctr])
                pg32, pg32r = Pg32.get()
                p.op("dve", I("tensor_reduce", out=pg32[:].rearrange("p j (k q) -> p j k q", k=2),
                              in_=pct[:].rearrange("p j (k g q) -> p j k q g", k=2, g=4), axis=AX.X, op=ALU.add), reads=[pctr], writes=[pg32r])
                pg, pgr = Pg.get()
                p.op("dve", I("tensor_copy", out=pg[:], in_=pg32[:]), reads=[pg32r], writes=[pgr])
                oc, ocr = psa()
                p.op("pe", MM([(oc[0:64, 0:128], pct[:, j, :], vcs[:, j, 0:128], j == 0) for j in range(4)]),
                     reads=[pctr, "vcs"] + [("vcs", j) for j in range(4)], writes=[ocr])
                ont, onr = on_r.get()
                p.op("act", I("activation", out=ont[:], in_=oc[0:64, 0:128], func=AF.Copy), reads=[ocr], writes=[onr])
                to_obr(ont, onr, 0, b)
                z, zr = psg()
                p.op("pe", MM([(z[0:16, 0:129], pg[:, j, :], vcs[:, j, 128:257], j == 0) for j in range(4)]), reads=[pgr, "vcs"], writes=[zr])
                sc, scr = scs.get()
                p.op("dve", I("tensor_tensor", out=sc[0:16, :], in0=z[0:16, 0:129], in1=fbs[0:16, :], op=ALU.add), reads=[zr, "fbs"], writes=[scr])
                p.op("dve", I("max", out=m[0:16, 8:16], in_=sc[0:16, :]), reads=[scr], writes=[(mr, 3)])
                sc2, sc2r = scs2.get()
                p.op("dve", I("match_replace", out=sc2[0:16, :], in_to_replace=m[0:16, 8:16], in_values=sc[0:16, :], imm_value=-3.0e38), reads=[scr, (mr, 3)], writes=[sc2r])
                p.op("dve", I("max", out=m[0:16, 16:24], in_=sc2[0:16, :]), reads=[sc2r], writes=[(mr, 4)])
                s16, s16r = sel16.get()
                p.op("dve", I("tensor_scalar", out=s16[:], in0=sc[0:16, :], scalar1=m[0:16, 23:24], scalar2=None, op0=ALU.is_ge), reads=[scr, (mr, 4)], writes=[s16r])
                z, zr = psg()
                p.op("pe", MM([(z[0:64, 0:129], rep16[:, :], s16[:, :], True)]), reads=[s16r, "rep16"], writes=[zr])
                sel, selr = sel_r.get()
                p.op("act", I("activation", out=sel[:], in_=z[0:64, 0:129], func=AF.Copy), reads=[zr], writes=[selr])
                if SPART < 4:
                    continue
                osl, oslr = psa()
                qa, qb_ = [], []
                nC = [0]

                def run_C(it):
                    tt, ttr, t2, mm_ = it
                    dk_C(osl, oslr, tt, ttr, lambda j, t2=t2, mm_=mm_: Vs[:, 4 * mm_ + j, t2, :], [("Vs", 4 * mm_ + j) for j in range(4)] + ["Vs1"], nC[0] == 0)
                    nC[0] += 1

                def run_B(it):
                    tm, tmr, t2, mm_ = it
                    tt, ttr = dk_B(tm, tmr)
                    qb_.append((tt, ttr, t2, mm_))
                    if len(qb_) > 1:
                        run_C(qb_.pop(0))
                for t2 in range(2):
                    for mm_ in range(8):
                        def mask_sel(tm, tmr, t, tr, mm_=mm_):
                            p.op("dve", I("tensor_tensor", out=tm[:].rearrange("p (j r) -> p j r", r=32), in0=t[:].rearrange("p (j r) -> p j r", r=32),
                                          in1=sel[:, 16 * mm_:16 * mm_ + 16].unsqueeze(2).broadcast_to([64, 16, 32]), op=ALU.mult),
                                 reads=[tr, selr], writes=[tmr])
                        tm, tmr = dk_A(KsT[:, t2, mm_ * 512:(mm_ + 1) * 512], [("KsT", 4 * mm_ + j) for j in range(4)], mask_sel)
                        qa.append((tm, tmr, t2, mm_))
                        if len(qa) > 1:
                            run_B(qa.pop(0))
                while qa:
                    run_B(qa.pop(0))
                while qb_:
                    run_C(qb_.pop(0))
                new_keys(osl, oslr, 0, b, "s")
                finish_s(osl, oslr, 1, b, False)
                if SPART < 5:
                    continue
                swt, swr = SW.get()
                p.op("pool", I("dma_start", out=swt[:], in_=swin[b].rearrange("(c p) f -> p c f", p=128)), writes=[swr], dkey=swr)
                z, zr = psg()
                zb = z[:].bitcast(BF16)
                p.op("pe", TR([(zb[:, c * 128:(c + 1) * 128], swt[:, c, 0:128], ident[:]) for c in range(4)]), reads=[swr, "ident"], writes=[zr])
                kw, kwr = KwT.get()
                p.op("act", I("activation", out=kw[:], in_=zb[:, 0:512], func=AF.Copy), reads=[zr], writes=[kwr])
                sv, svr = SWv.get()
                p.op("pool", I("tensor_copy", out=sv[:, :, 0:128], in_=swt[:, :, 128:256]), reads=[swr], writes=[svr])
                ow, owr = psa()

                def mask_win(tm, tmr, t, tr):
                    p.op("dve", I("tensor_tensor", out=tm[:], in0=t[:], in1=wm[:], op=ALU.mult), reads=[tr, "wm"], writes=[tmr])
                dense_keys(ow, owr, kw[:, :], [kwr], mask_win, lambda j, sv=sv: sv[:, j, :], [svr], True)
                new_keys(ow, owr, 1, b, "w")
                finish_s(ow, owr, 2, b, False)
            ghi = p.sb("ghi", [128, 24], BF16, sS); ghi32 = p.sb("ghi32", [128, 24], F32, sS); glo = p.sb("glo", [128, 24], BF16, sS)
            p.op("dve", I("tensor_copy", out=ghi[:], in_=gates_sb[:, 16, :]), reads=[("gates", 16)], writes=["ghi"])
            p.op("dve", I("tensor_copy", out=ghi32[:], in_=ghi[:]), reads=["ghi"], writes=["ghi32"])
            p.op("dve", I("tensor_tensor", out=glo[:], in0=gates_sb[:, 16, :], in1=ghi32[:], op=ALU.subtract), reads=[("gates", 16), "ghi32"], writes=["glo"])
            z, zr = psg()
            zb = z[:].bitcast(BF16)
            p.op("pe", TR([(zb[0:24, 0:128], ghi[:, :], ident[:]), (zb[0:24, 128:256], glo[:, :], ident[:])]), reads=["ghi", "glo", "ident"], writes=[zr])
            gT = p.sb("gT", [24, 2, 128], BF16, sS)
            p.op("act", I("activation", out=gT[:], in_=zb[0:24, 0:256].rearrange("p (a t) -> p a t", a=2), func=AF.Copy), reads=[zr], writes=["gT"])
            selgb = p.sb("selgb", [24, 3, 4, 128], BF16, sS)
            p.op("dve", I("tensor_copy", out=selgb[:], in_=selg[:]), reads=["selg"], writes=["selgb"])
            oaT = p.sb("oaT", [128, 512], F32, sS)
            oat2 = p.sb("oat2", [128, 512], F32, sS)
            obr_res = [("OBR", br, b, k) for br in range(3) for b in range(16) for k in range(2)]
            for br in range(3):
                z, zr = psg()
                lst = []
                for g in range(4):
                    lst.append((z[:, g * 128:(g + 1) * 128], selgb[:, br, g, :], gT[:, 0, :], g == 0))
                    lst.append((z[:, g * 128:(g + 1) * 128], selgb[:, br, g, :], gT[:, 1, :], False))
                p.op("pe", MM(lst), reads=["selgb", "gT"], writes=[zr])
                src = OBR[:, br, :, :].rearrange("p g t -> p (g t)")
                if br == 0:
                    p.op("dve", I("tensor_tensor", out=oaT[:], in0=z[:, :], in1=src, op=ALU.mult), reads=[zr] + obr_res, writes=["oaT"])
                else:
                    p.op("dve", I("tensor_tensor", out=oat2[:], in0=z[:, :], in1=src, op=ALU.mult), reads=[zr] + obr_res, writes=["oat2"])
                    p.op("pool", I("tensor_tensor", out=oaT[:], in0=oaT[:], in1=oat2[:], op=ALU.add), reads=["oaT", "oat2"], writes=["oaT"])
            if _os.environ.get('DBG_NOFIN') != '1':
                p.op("act", I("activation", out=mixT[:, 0:4, 2048:2176], in_=oaT[:].rearrange("p (g t) -> p g t", g=4), func=AF.Copy), reads=["oaT"], writes=[("mixT", "a")])
            p.barrier()
        with contextlib.ExitStack() as s3:
            wout_sb = p.sb("wout_sb", [128, 8, 1024], BF16, s3)
            w_out_v = w_out.rearrange("(kc p) n -> p kc n", p=128)
            for kc in range(8):
                p.op("pool", I("dma_start", out=wout_sb[:, kc, :], in_=w_out_v[:, kc, :]), writes=[("wout", kc)], dkey=("wout", kc % 4))
            wouts_sb = p.sb("wouts_sb", [128, 4, 1024], BF16, s3)

            def wouts_dma(e, wouts_sb=wouts_sb):
                r = []
                for g in range(4):
                    for k in range(2):
                        h_ = 4 * k + g
                        r.append(e.dma_start(out=wouts_sb[k * 64:(k + 1) * 64, g, :], in_=w_out[h_ * 64:(h_ + 1) * 64, :]))
                return r
            p.op("pool", wouts_dma, writes=["wouts"], dkey="c40", ndma=8)
            wout_res = [("wout", kc) for kc in range(8)] + ["wouts"]
            gf_sb = p.sb("gf_sb", [128, 1024], F32, s3)
            p.op("sp", I("dma_start", out=gf_sb[:], in_=gfB), writes=["gf"], dkey="c11")
            w1s = Rot(p, "w1s", 2, [128, 8, 512], BF16, s3)
            w2s = Rot(p, "w2s", 3, [128, 4, 512], BF16, s3)
            fT = p.sb("fT", [128, 32, 512], BF16, s3)
            hnT = p.sb("hnT", [128, 8, 512], BF16, s3)
            h2 = p.sb("h2", [128, 4, 1024], F32, s3)
            xbuf = Rot(p, "xb3", 2, [128, 1024], F32, s3)
            junk = Rot(p, "junk3", 2, [128, 1024], BF16, s3)
            hnb = Rot(p, "hnb", 2, [128, 1024], BF16, s3)
            st4 = Rot(p, "st43", 4, [128, 8], F32, s3)
            rl = Rot(p, "rl", 3, [128, 512], F32, s3)
            yb = Rot(p, "yb", 4, [128, 1024], F32, s3)
            w_ff1_v = w_ff1.rearrange("(kc p) f -> p kc f", p=128)
            w_ff2_v = w_ff2.rearrange("(fc p) n -> p fc n", p=128)

            groups = [[("own", t) for t in range(0, 4)], [("own", t) for t in range(4, 8)],
                      [("own", t) for t in range(8, 12)], [("own", t) for t in range(12, 16)], [("smp", 0)]]
            for grp in groups:
                nt = len(grp)
                ntok = nt * 128
                for si, (kind, ti) in enumerate(grp):
                    src = xo if kind == "own" else xs
                    r0 = ti * 128
                    mcol = 2048 if kind == "smp" else ti * 128
                    xt, xr = xbuf.get()
                    p.op("sp", I("dma_start", out=xt[:], in_=src[r0:r0 + 128, :]), writes=[xr], dkey=xr)
                    for half in range(2):
                        z, zr = psa()
                        p.op("pe", MM([(z[:, :], mixT[:, k, mcol:mcol + 128],
                                        (wouts_sb if (kind == "smp" and k < 4 and stage >= 3) else wout_sb)[:, k, half * 512:(half + 1) * 512], k == 0) for k in range(8)]),
                             reads=wout_res + [("mixT", "a"), ("mixT", "a", mcol), ("mixT", "b", mcol)], writes=[zr])
                        p.op("dve", I("tensor_tensor", out=h2[:, si, half * 512:(half + 1) * 512], in0=z[:, :], in1=xt[:, half * 512:(half + 1) * 512], op=ALU.add),
                             reads=[zr, xr], writes=[("h2", si, half)])
                    jk, jr = junk.get(); s4, s4r = st4.get()
                    p.op("act", I("activation", out=jk[:], in_=h2[:, si, :], func=AF.Square, accum_out=s4[:, 0:1]),
                         reads=[("h2", si, 0), ("h2", si, 1)], writes=[jr, (s4r, 0)])
                    p.op("act", I("activation", out=s4[:, 1:2], in_=s4[:, 0:1], func=AF.Sqrt, scale=1.0 / 1024, bias=EPS), reads=[(s4r, 0)], writes=[(s4r, 1)])
                    p.op("dve", I("reciprocal", out=s4[:, 2:3], in_=s4[:, 1:2]), reads=[(s4r, 1)], writes=[(s4r, 2)])
                    hb, hbr = hnb.get()
                    p.op("act", I("activation", out=hb[:], in_=h2[:, si, :], func=AF.Copy, scale=s4[:, 2:3]),
                         reads=[("h2", si, 0), ("h2", si, 1), (s4r, 2)], writes=[hbr])
                    pt_, ptr = psg()
                    ptb = pt_[:].bitcast(BF16)
                    p.op("pe", TR([(ptb[:, k * 128:(k + 1) * 128], hb[:, k * 128:(k + 1) * 128], ident[:]) for k in range(8)]),
                         reads=[hbr, "ident"], writes=[ptr])
                    p.op("dve", I("tensor_tensor", out=hnT[:, :, si * 128:(si + 1) * 128], in0=ptb.rearrange("p (k t) -> p k t", k=8),
                                  in1=g2T_sb[:, :].unsqueeze(2).broadcast_to([128, 8, 128]), op=ALU.mult),
                         reads=[ptr, "g2T"], writes=[("hnT", si)])
                hn_res = [("hnT", si) for si in range(nt)]
                for c in range(8):
                    w1t, w1r = w1s.get()
                    p.op("pool", I("dma_start", out=w1t[:], in_=w_ff1_v[:, :, c * 512:(c + 1) * 512]), writes=[w1r], dkey=w1r)
                    for f4 in range(4):
                        fc = c * 4 + f4
                        for (t0, tn) in [(0, ntok)]:
                            z, zr = psg()
                            p.op("pe", MM([(z[:, 0:tn], w1t[:, k, f4 * 128:(f4 + 1) * 128], hnT[:, k, t0:t0 + tn], k == 0) for k in range(8)]),
                                 reads=[w1r] + hn_res, writes=[zr])
                            rt, rr = rl.get()
                            p.op("act", I("activation", out=rt[:, 0:tn], in_=z[:, 0:tn], func=AF.Relu), reads=[zr], writes=[rr])
                            p.op("pool", I("tensor_tensor", out=fT[:, fc, t0:t0 + tn], in0=rt[:, 0:tn], in1=rt[:, 0:tn], op=ALU.mult),
                                 reads=[rr], writes=[("fT", fc, t0)])
                yts = [yb.get() for _ in range(nt)]
                for half in range(2):
                    accs = [psa() for _ in range(nt)]
                    for c in range(8):
                        w2t, w2r = w2s.get()
                        p.op("pool", I("dma_start", out=w2t[:], in_=w_ff2_v[:, c * 4:(c + 1) * 4, half * 512:(half + 1) * 512]), writes=[w2r], dkey=w2r)
                        for si in range(nt):
                            z, zr = accs[si]
                            p.op("pe", MM([(z[:, :], fT[:, c * 4 + f4, si * 128:(si + 1) * 128], w2t[:, f4, :], (c == 0 and f4 == 0)) for f4 in range(4)]),
                                 reads=[w2r] + [("fT", c * 4 + f4, 0) for f4 in range(4)], writes=[zr])
                    for si in range(nt):
                        z, zr = accs[si]
                        yt, yr = yts[si]
                        p.op("dve", I("tensor_tensor", out=yt[:, half * 512:(half + 1) * 512], in0=z[:, :], in1=h2[:, si, half * 512:(half + 1) * 512], op=ALU.add),
                             reads=[zr, ("h2", si, half)], writes=[(yr, half)])
                for si, (kind, ti) in enumerate(grp):
                    dst = yo if kind == "own" else ys
                    r0 = ti * 128
                    yt, yr = yts[si]
                    jk, jr = junk.get(); s4, s4r = st4.get()
                    p.op("act", I("activation", out=jk[:], in_=yt[:], func=AF.Square, accum_out=s4[:, 0:1]), reads=[(yr, 0), (yr, 1)], writes=[jr, (s4r, 0)])
                    p.op("act", I("activation", out=s4[:, 1:2], in_=s4[:, 0:1], func=AF.Sqrt, scale=1.0 / 1024, bias=EPS), reads=[(s4r, 0)], writes=[(s4r, 1)])
                    p.op("dve", I("reciprocal", out=s4[:, 2:3], in_=s4[:, 1:2]), reads=[(s4r, 1)], writes=[(s4r, 2)])
                    p.op("dve", I("scalar_tensor_tensor", out=yt[:], in0=yt[:], scalar=s4[:, 2:3], in1=gf_sb[:], op0=ALU.mult, op1=ALU.mult),
                         reads=[(yr, 0), (yr, 1), (s4r, 2), "gf"], writes=[(yr, 0), (yr, 1)])
                    outs.append(p.op("sp", I("dma_start", out=dst[r0:r0 + 128, :], in_=yt[:]), reads=[(yr, 0), (yr, 1)], dkey=("oy", si)))
        p.emit(final_waits=outs)
    return nc


def _consts():
    I_ = np.arange(128)[:, None]
    q_ = np.arange(512)[None, :]
    cover = np.zeros((2, 128, 64), np.float32)
    for c2 in range(2):
        for i in range(128):
            bi = c2 * 128 + i
            if bi > 254:
                continue
            for j in range(64):
                ov = min(16 * bi + 32, 64 * j + 64) - max(16 * bi, 64 * j)
                if ov > 0:
                    cover[c2, i, j] = ov / 32.0
    cmpm = np.zeros((8, 128, 512), np.float32)
    for qt in range(4):
        for c2 in range(2):
            bi = c2 * 128 + I_
            cmpm[qt * 2 + c2] = ((bi <= 254) & (16 * bi + 31 <= 2048 + 512 * qt + q_)).astype(np.float32)
    bandm = np.zeros((8, 128, 512), np.float32)
    for c in range(8):
        kk = 128 * c + I_
        bandm[c] = ((kk > q_) & (kk <= q_ + 512)).astype(np.float32)
    negc = np.zeros((4, 128, 512), np.float32)
    for d in range(4):
        negc[d] = -((128 * d + I_) > q_).astype(np.float32)
    E = np.zeros((64, 32, 128), np.float32)
    for c in range(32):
        for m_ in range(128):
            E[2 * c + m_ // 64, c, m_] = 1.0
    return dict(cover=cover, cmpm=cmpm, bandm=bandm, negc=negc, Eexp=E)


def _sample_consts():
    covs = np.zeros((128, 4, 129), np.float32)
    for c4 in range(4):
        for i in range(128):
            bi = c4 * 128 + i
            if bi > 510:
                continue
            for j in range(129):
                ov = min(16 * bi + 32, 64 * j + 64) - max(16 * bi, 64 * j)
                if ov > 0:
                    covs[i, c4, j] = ov / 32.0
    fbs = np.zeros((64, 129), np.float32)
    fbs[:, [0, 127, 128]] = 1e9
    r = np.arange(64)
    k_, g_, q_ = r // 32, (r // 8) % 4, r % 8
    sel2 = ((k_[:, None] == k_[None, :]) & (q_[:, None] == q_[None, :])).astype(np.float32)
    cm8 = (np.arange(8)[None, :] <= q_[:, None]).astype(np.float32)
    wm = (np.arange(512)[None, :] > q_[:, None]).astype(np.float32)
    selg = np.zeros((24, 3, 4, 128), np.float32)
    for br in range(3):
        for g in range(4):
            for k in range(2):
                selg[br * 8 + 4 * k + g, br, g, k * 64:(k + 1) * 64] = 1.0
    pcol = (np.arange(128) % 64).astype(np.float32)[:, None]
    r16 = np.arange(16)
    rep16 = ((r16[:, None] // 8 == k_[None, :]) & (r16[:, None] % 8 == q_[None, :])).astype(np.float32)
    return dict(covs=covs, fbs=fbs, sel2=sel2, cm8=cm8, wm=wm, selg=selg, pcol=pcol, rep16=rep16)


def _core_consts(h):
    first = 0 if h == 1 else 32
    selb = np.zeros((128, 16, 64), np.float32)
    vis = np.zeros((128, 16, 64), np.float32)
    j = np.arange(64)[None, :]
    for t in range(16):
        tl = 2048 + t * 128 + np.arange(128)[:, None]
        cur = tl // 64
        visible = (j >= first) & (j <= cur)
        forced = (j == first) | (j == cur) | (j == cur - 1)
        b = np.where(forced, 1e9, 0.0)
        b = np.where(visible, b, -1e30)
        selb[:, t, :] = b
        vis[:, t, :] = visible
    ctxb = np.zeros((128, 2), np.float32)
    if h == 0:
        ctxb[:, 0] = NEG
    return dict(selb=selb, vis=vis, ctxb=ctxb)


def _host_inputs(inp):
    f = lambda a: np.ascontiguousarray(np.asarray(a, dtype=np.float32))
    w_in = f(inp["w_in"][0])
    qperm = []
    for g in range(4):
        for k in range(2):
            h = k * 4 + g
            qperm += list(range(h * 64, (h + 1) * 64))
    perm = np.array(qperm + list(range(512, INC)))
    w_in_p = np.ascontiguousarray(w_in[:, perm])
    rep = lambda v, n=128: np.ascontiguousarray(np.broadcast_to(np.asarray(v, np.float32)[None, :], (n, len(v))))
    colT = lambda v: np.ascontiguousarray(np.asarray(v, np.float32).reshape(8, 128).T)
    b_s = f(inp["b_s"][0])
    gind = np.zeros((8, 512), np.float32)
    for g in range(8):
        gind[g, g * 64:(g + 1) * 64] = 1.0
    ii = np.arange(128)
    common = dict(
        w_in=w_in_p, w_out=f(inp["w_out"][0]), w_ff1=f(inp["w_ff1"][0]), w_ff2=f(inp["w_ff2"][0]),
        g1T=colT(inp["ln1_g"][0]), g2T=colT(inp["ln2_g"][0]), gfB=rep(inp["ln_f_g"]),
        lvg=rep(inp["ln_v_g"][0]), lvb=rep(inp["ln_v_b"][0]),
        wsT=np.ascontiguousarray(f(inp["w_s"][0]).transpose(0, 2, 1)),
        bs8=b_s, bs8s=np.ascontiguousarray(np.tile(b_s[:, 0:8], (1, 16))),
        ident=np.eye(128, dtype=np.float32),
        tril=(ii[:, None] <= ii[None, :]).astype(np.float32),
        gind=gind,
    )
    common.update(_consts())
    common.update(_sample_consts())
    cache = np.asarray(inp["cache_kv"], dtype=np.float32).reshape(-1, 1024)
    common["cache"] = cache
    pt = np.asarray(inp["page_table"]).astype(np.int32)
    pp = np.arange(128) // 64
    cw1 = f(inp["cmp_w1"][0]); cpos = f(inp["cmp_pos"][0]).reshape(2, 16, 128)
    cb2 = f(inp["cmp_b2"][0])
    common.update(dict(
        cw1=cw1, cpos=np.ascontiguousarray(cpos.transpose(2, 0, 1)),
        cpos64=np.ascontiguousarray(f(inp["cmp_pos"][0]).transpose(2, 0, 1)), cb1=np.ascontiguousarray(f(inp["cmp_b1"][0]).T),
        cw2=f(inp["cmp_w2"][0]), cb2k=np.ascontiguousarray(np.tile(cb2[0], 2)[:, None]),
        cb2v=rep(np.tile(cb2[1], 2)),
    ))
    xp = f(inp["x_prompt"]); xs = f(inp["x_sample"]).reshape(1024, 1024)
    swin = f(inp["state_win_kv"][0]).reshape(128, 512, 256)
    maps = []
    for c in range(8):
        b, h = c // 2, c % 2
        m = dict(common)
        m["xo"] = np.ascontiguousarray(xp[b, h * 2048:(h + 1) * 2048])
        m["xc"] = np.ascontiguousarray(xp[b, 0:2048])
        m["xs"] = np.ascontiguousarray(xs[c * 128:(c + 1) * 128])
        m["swin"] = np.ascontiguousarray(swin[c * 16:(c + 1) * 16])
        m.update(_core_consts(h))
        ptc = pt[c * 16:(c + 1) * 16].reshape(16, 32, 2)
        m["ptx"] = np.ascontiguousarray(ptc[:, :, pp].transpose(2, 0, 1))
        maps.append(m)
    return maps


STAGE = 3
_NC = {}


def kernel(**inp):
    return _run(_host_inputs(inp))


def _run(maps):
    npool = maps[0]["cache"].shape[0] // 64
    if (STAGE, npool) not in _NC:
        _NC[(STAGE, npool)] = build(STAGE, npool)
    nc = _NC[(STAGE, npool)]
    maps = [{"i_" + k: v for k, v in m.items()} for m in maps]
    res = run_bass_kernel_spmd(nc, maps, core_ids=list(range(8)))
    r = [{k[2:]: v for k, v in d.items()} for d in res.results]
    y_p = np.zeros((4, 4096, 1024), np.float32)
    kv_p = np.zeros((1, 4, 4096, 512), np.float32)
    win_p = np.zeros((1, 4, 512, 256), np.float32)
    for c in range(8):
        b, h = c // 2, c % 2
        y_p[b, h * 2048:(h + 1) * 2048] = r[c]["yo"]
        kv_p[0, b, h * 2048:(h + 1) * 2048] = r[c]["kvo"]
        if h == 1:
            win_p[0, b] = r[c]["wino"]
    y_s = np.concatenate([r[c]["ys"] for c in range(8)], 0).reshape(128, 8, 1024)
    kv_s = np.concatenate([r[c]["kvs"] for c in range(8)], 0).reshape(1, 128, 8, 4, 2, 64)
    win_s = np.concatenate([r[c]["wins"] for c in range(8)], 0).reshape(1, 128, 512, 2, 2, 64)
    v_s = np.concatenate([r[c]["vso"] for c in range(8)], 0).reshape(1, 128, 8, 512)
    return (y_p, y_s, kv_p.reshape(1, 4, 4096, 4, 2, 64), kv_s, win_p.reshape(1, 4, 512, 2, 2, 64), win_s, v_s)
```

```python
import contextlib
import numpy as np
import concourse.bass as bass
import concourse.mybir as mybir
from concourse.bass_utils import run_bass_kernel_spmd

F32 = mybir.dt.float32
BF16 = mybir.dt.bfloat16
I32 = mybir.dt.int32
AF = mybir.ActivationFunctionType
ALU = mybir.AluOpType
AX = mybir.AxisListType

ENGS = ("pe", "act", "dve", "pool", "sp")
EPS = 1e-6
NEG = -30000.0


class Op:
    __slots__ = ("eng", "fn", "deps", "idx", "sig", "sem", "val", "n_dma")

    def __init__(self, eng, fn):
        self.eng = eng
        self.fn = fn
        self.deps = []
        self.sig = False
        self.sem = None
        self.val = 0
        self.n_dma = 0


class Prog:
    def __init__(self, nc, stack):
        self.nc = nc
        self.stack = stack
        self.ops = []
        self.lastw = {}
        self.readers = {}
        self.esem = {}
        self.dsem = {}
        self.dcount = {}
        self.dlast = {}
        self.elast = {}
        self.ecount = {e: 0 for e in ENGS}
        for e in ("pe", "act", "dve", "pool"):
            self.esem[e] = stack.enter_context(nc.semaphore("s_" + e))
        self.nps = 0

    def sb(self, name, shape, dt, stack=None):
        return (stack or self.stack).enter_context(self.nc.sbuf_tensor(name, list(shape), dt))

    def _dsem(self, key):
        if key not in self.dsem:
            self.dsem[key] = self.stack.enter_context(self.nc.semaphore("d_" + str(len(self.dsem))))
            self.dcount[key] = 0
        return self.dsem[key]

    def op(self, eng, fn, reads=(), writes=(), dkey=None, ndma=1, extra=()):
        o = Op(eng, fn)
        o.idx = len(self.ops)
        deps = set(extra)
        _psr = [r for r in reads if isinstance(r, tuple) and r and r[0] == "ps"]
        if _psr:
            reads = [r for r in reads if not (isinstance(r, tuple) and r and r[0] == "ps")]
            writes = list(writes) + _psr
        if isinstance(dkey, str) and dkey.startswith("c"):
            dkey = "cser"
            if "cser" in self.dlast:
                deps.add(self.dlast["cser"])
        for r in reads:
            w = self.lastw.get(r)
            if w is not None:
                deps.add(w)
        for r in writes:
            w = self.lastw.get(r)
            if w is not None:
                deps.add(w)
            for rd in self.readers.get(r, ()):
                deps.add(rd)
        for r in reads:
            self.readers.setdefault(r, []).append(o)
        for r in writes:
            self.lastw[r] = o
            self.readers[r] = []
        if dkey is not None:
            o.sem = self._dsem(dkey)
            self.dcount[dkey] += 16 * ndma
            o.val = self.dcount[dkey]
            o.n_dma = ndma
            o.sig = True
            self.dlast[dkey] = o
        else:
            self.elast[eng] = o
        deps.discard(o)
        for d in deps:
            if d.eng == "pe" and eng == "pe" and d.n_dma == 0:
                continue
            o.deps.append(d)
        self.ops.append(o)
        return o

    def barrier(self):
        tg = list(self.elast.values()) + list(self.dlast.values())
        for e in ENGS:
            o = Op(e, lambda eng: None)
            o.deps = [d for d in tg]
            self.ops.append(o)
        self.lastw = {}
        self.readers = {}
        if not hasattr(self, "cuts"):
            self.cuts = []
        self.cuts.append(len(self.ops))

    def emit(self, final_waits=()):
        nc = self.nc
        for o in self.ops:
            for d in o.deps:
                if d.n_dma == 0:
                    d.sig = True
        for o in self.ops:
            if o.n_dma == 0 and o.sig:
                self.ecount[o.eng] += 1
                o.sem = self.esem[o.eng]
                o.val = self.ecount[o.eng]
        handles = {"pe": "tensor", "act": "scalar", "dve": "vector", "pool": "gpsimd", "sp": "sync"}
        cuts = [0] + list(getattr(self, "cuts", [])) + [len(self.ops)]
        seen_all = {e: {} for e in ENGS}
        nseg = len(cuts) - 1
        for si in range(nseg):
            seg = self.ops[cuts[si]:cuts[si + 1]]
            if not seg:
                continue
            per_eng = {e: [o for o in seg if o.eng == e] for e in ENGS}
            last_seg = (si == nseg - 1)
            with nc.Block() as block:
                for e in ENGS:
                    ops = per_eng[e]

                    def body(eng, ops=ops, e=e, last_seg=last_seg):
                        seen = seen_all[e]
                        for o in ops:
                            need = {}
                            for d in o.deps:
                                k = id(d.sem)
                                if seen.get(k, 0) >= d.val:
                                    continue
                                if k not in need or need[k][1] < d.val:
                                    need[k] = (d.sem, d.val)
                            for k, (s, v) in need.items():
                                eng.wait_ge(s, v)
                                seen[k] = v
                            insts = o.fn(eng)
                            if o.n_dma:
                                if not isinstance(insts, (list, tuple)):
                                    insts = [insts]
                                assert len(insts) == o.n_dma, (len(insts), o.n_dma)
                                for i in insts:
                                    i.then_inc(o.sem, 16)
                            elif o.sig:
                                if isinstance(insts, (list, tuple)):
                                    insts = insts[-1]
                                insts.then_inc(o.sem, 1)
                        if e == "sp" and last_seg:
                            for o in final_waits:
                                eng.wait_ge(o.sem, o.val)

                    getattr(block, handles[e])(body)


def I(method, **kw):
    return lambda e: getattr(e, method)(**kw)


def MM(lst):
    def f(e):
        last = None
        for (out, lhsT, rhs, start) in lst:
            last = e.matmul(out, lhsT, rhs, start=start, stop=True, skip_group_check=True)
        return last
    return f


def TR(lst):
    def f(e):
        last = None
        for (out, in_, ident) in lst:
            last = e.transpose(out, in_, ident)
        return last
    return f


class Rot:
    def __init__(self, p, name, n, shape, dt, stack=None):
        self.t = [p.sb("%s%d" % (name, i), shape, dt, stack) for i in range(n)]
        self.name = name
        self.n = n
        self.i = 0

    def get(self):
        k = self.i % self.n
        self.i += 1
        return self.t[k], (self.name, k)


NQ, NKV, NWG, NU, NV = 512, 512, 280, 512, 512
C_Q, C_KV, C_WG, C_U, C_V = 0, 512, 1024, 1304, 1816
INC = 2328


def build(stage, npool=10240):
    nc = bass.Bass("TRN2", target_bir_lowering=False)

    def DI(name, shape, dt=F32):
        return nc.dram_tensor("i_" + name, list(shape), dt, kind="ExternalInput").ap()

    def DO(name, shape, dt=F32):
        return nc.dram_tensor("o_" + name, list(shape), dt, kind="ExternalOutput").ap()

    xo = DI("xo", [2048, 1024]); xc = DI("xc", [2048, 1024]); xs = DI("xs", [128, 1024])
    swin = DI("swin", [16, 512, 256])
    w_in = DI("w_in", [1024, INC]); w_out = DI("w_out", [1024, 1024])
    w_ff1 = DI("w_ff1", [1024, 4096]); w_ff2 = DI("w_ff2", [4096, 1024])
    g1T = DI("g1T", [128, 8]); g2T = DI("g2T", [128, 8]); gfB = DI("gfB", [128, 1024])
    lvg = DI("lvg", [128, 512]); lvb = DI("lvb", [128, 512])
    wsT = DI("wsT", [8, 128, 128]); bs8 = DI("bs8", [8, 128]); bs8s = DI("bs8s", [8, 128])
    identd = DI("ident", [128, 128]); trild = DI("tril", [128, 128]); gindd = DI("gind", [8, 512])

    cw1 = DI("cw1", [2, 2048, 128]); cposd = DI("cpos", [128, 2, 16]); cb1d = DI("cb1", [128, 2])
    cw2 = DI("cw2", [2, 128, 64]); cb2kd = DI("cb2k", [128, 1]); cb2vd = DI("cb2v", [128, 128])
    coverd = DI("cover", [2, 128, 64])
    cmpmd = DI("cmpm", [8, 128, 512]); bandmd = DI("bandm", [8, 128, 512]); negcd = DI("negc", [4, 128, 512])
    Ed = DI("Eexp", [64, 32, 128]); selbd = DI("selb", [128, 16, 64]); visd = DI("vis", [128, 16, 64]); ctxbd = DI("ctxb", [128, 2])

    cache = DI("cache", [npool * 64, 1024]); ptxd = DI("ptx", [128, 16, 32], I32); pcold = DI("pcol", [128, 1])
    covsd = DI("covs", [128, 4, 129]); fbsd = DI("fbs", [64, 129]); sel2d = DI("sel2", [64, 64])
    cpos64d = DI("cpos64", [64, 2, 32]); cm8d = DI("cm8", [64, 8]); wmd = DI("wm", [64, 512]); selgd = DI("selg", [24, 3, 4, 128]); rep16d = DI("rep16", [16, 64])

    yo = DO("yo", [2048, 1024]); ys = DO("ys", [128, 1024])
    kvo = DO("kvo", [2048, 512]); kvs = DO("kvs", [128, 512])
    wino = DO("wino", [512, 256]); wins = DO("wins", [16, 512, 256]); vso = DO("vso", [128, 512])

    outs = []
    with contextlib.ExitStack() as st:
        p = Prog(nc, st)
        ps = [st.enter_context(nc.psum_tensor("ps%d" % i, [128, 512], F32)) for i in range(8)]
        rot = {"g": 0, "a": 0}

        def psg():
            k = rot["g"] % 4
            rot["g"] += 1
            return ps[k], ("ps", k)

        def psa():
            k = 4 + rot["a"] % 4
            rot["a"] += 1
            return ps[k], ("ps", k)

        ident_f = p.sb("ident_f", [128, 128], F32)
        ident = p.sb("ident", [128, 128], BF16)
        p.op("sp", I("dma_start", out=ident_f[:], in_=identd), writes=["ident_f"], dkey="c0")
        p.op("dve", I("tensor_copy", out=ident[:], in_=ident_f[:]), reads=["ident_f"], writes=["ident"])
        g1T_sb = p.sb("g1T_sb", [128, 8], F32); g2T_sb = p.sb("g2T_sb", [128, 8], F32)
        p.op("sp", I("dma_start", out=g1T_sb[:], in_=g1T), writes=["g1T"], dkey="c1")
        p.op("sp", I("dma_start", out=g2T_sb[:], in_=g2T), writes=["g2T"], dkey="c2")
        mixT = p.sb("mixT", [128, 8, 2176], BF16)
        sP = contextlib.ExitStack()
        Qbd = p.sb("Qbd", [128, 16, 64], BF16)
        KnT = p.sb("KnT", [128, 2, 128], BF16)
        p.op("pool", I("memset", ap=Qbd[:], constant=0.0), writes=["Qbd"])
        gates_sb = p.sb("gates_sb", [128, 17, 24], F32)
        KT = p.sb("KT", [128, 4, 4096], BF16, sP)
        vsaug = p.sb("vsaug", [128, 32, 2, 65], BF16, sP)
        vwaug = p.sb("vwaug", [128, 32, 2, 65], BF16, sP)
        qT = p.sb("qT", [128, 4, 2048], BF16, sP)
        p.op("pool", I("memset", ap=vsaug[:, :, :, 64:65], constant=1.0), writes=["vs1"])
        p.op("pool", I("memset", ap=vwaug[:, :, :, 64:65], constant=1.0), writes=["vw1"])

        with contextlib.ExitStack() as s1:
            win_sb = p.sb("win_sb", [128, 8, INC], BF16, s1)
            w_in_v = w_in.rearrange("(kc p) n -> p kc n", p=128)
            for kc in range(8):
                for h in range(2):
                    c0 = h * 1164
                    p.op("pool", I("dma_start", out=win_sb[:, kc, c0:c0 + 1164], in_=w_in_v[:, kc, c0:c0 + 1164]),
                         writes=[("win", kc, h)], dkey=("win", (kc * 2 + h) % 4))
            win_res = [("win", kc, h) for kc in range(8) for h in range(2)]
            lvg_sb = p.sb("lvg_sb", [128, 512], F32, s1); lvb_sb = p.sb("lvb_sb", [128, 512], F32, s1)
            p.op("sp", I("dma_start", out=lvg_sb[:], in_=lvg), writes=["lvg"], dkey="c3")
            p.op("sp", I("dma_start", out=lvb_sb[:], in_=lvb), writes=["lvb"], dkey="c4")
            tril_sb = p.sb("tril_sb", [128, 128], F32, s1)
            p.op("sp", I("dma_start", out=tril_sb[:], in_=trild), writes=["tril"], dkey="c5")
            ws_f = p.sb("ws_f", [128, 8, 128], F32, s1)
            ws_p = p.sb("ws_p", [128, 8, 128], BF16, s1)
            ws_s = p.sb("ws_s", [128, 8, 128], BF16, s1)
            p.op("sp", I("dma_start", out=ws_f[:], in_=wsT.rearrange("g j i -> j g i")), writes=["ws_f"], dkey="c6")
            trb = tril_sb[:, :].unsqueeze(1).broadcast_to([128, 8, 128])
            p.op("dve", I("tensor_tensor", out=ws_p[:], in0=ws_f[:], in1=trb, op=ALU.mult), reads=["ws_f", "tril"], writes=["ws_p"])
            ws_f2 = p.sb("ws_f2", [128, 8, 128], F32, s1)
            p.op("pool", I("memset", ap=ws_f2[:], constant=0.0), writes=["ws_f2"])

            def blk_dma(e, ws_f2=ws_f2):
                r = []
                for b in range(16):
                    r.append(e.dma_start(out=ws_f2[b * 8:(b + 1) * 8, :, b * 8:(b + 1) * 8],
                                         in_=wsT.rearrange("g j i -> j g i")[0:8, :, 0:8],
                                         allow_slow_non_contiguous=True))
                return r
            p.op("sp", blk_dma, reads=[], writes=["ws_f2"], dkey="c7", ndma=16)
            p.op("dve", I("tensor_tensor", out=ws_s[:], in0=ws_f2[:], in1=trb, op=ALU.mult), reads=["ws_f2", "tril"], writes=["ws_s"])
            bs_f = p.sb("bs_f", [8, 2, 128], F32, s1); bs_b = p.sb("bs_b", [8, 2, 128], BF16, s1)
            gind_f = p.sb("gind_f", [8, 512], F32, s1); gind = p.sb("gind", [8, 512], BF16, s1)
            p.op("sp", I("dma_start", out=bs_f[:, 0, :], in_=bs8), writes=["bs_f0"], dkey="c8")
            p.op("sp", I("dma_start", out=bs_f[:, 1, :], in_=bs8s), writes=["bs_f1"], dkey="c9")
            p.op("sp", I("dma_start", out=gind_f[:], in_=gindd), writes=["gind_f"], dkey="c10")
            p.op("dve", I("tensor_copy", out=bs_b[:], in_=bs_f[:]), reads=["bs_f0", "bs_f1"], writes=["bs_b"])
            p.op("dve", I("tensor_copy", out=gind[:], in_=gind_f[:]), reads=["gind_f"], writes=["gind"])

            xbuf = Rot(p, "xb", 2, [128, 1024], F32, s1)
            junk = Rot(p, "junk", 2, [128, 1024], BF16, s1)
            xn = Rot(p, "xn", 2, [128, 1024], BF16, s1)
            hT = Rot(p, "hT", 2, [128, 8, 128], BF16, s1)
            st4 = Rot(p, "st4", 4, [128, 8], F32, s1)
            zkv = Rot(p, "zkv", 2, [128, 512], F32, s1)
            zkvb = Rot(p, "zkvb", 2, [128, 512], BF16, s1)
            zwgb = Rot(p, "zwgb", 2, [128, 256], BF16, s1)
            zwg = Rot(p, "zwg", 2, [128, 280], F32, s1)
            qb = Rot(p, "qb", 2, [128, 512], BF16, s1)
            ub = Rot(p, "ub", 2, [128, 512], BF16, s1)
            vg = Rot(p, "vg", 2, [128, 512], F32, s1)
            vn = Rot(p, "vn", 2, [128, 512], F32, s1)
            vnb = Rot(p, "vnb", 2, [128, 512], BF16, s1)
            ob = Rot(p, "ob", 2, [128, 512], BF16, s1)
            bst = Rot(p, "bst", 2, [128, 6], F32, s1)

            def proj_tile(kind, ti):
                src = {"ctx": xc, "own": xo, "smp": xs}[kind]
                r0 = ti * 128
                xt, xr = xbuf.get()
                p.op("sp", I("dma_start", out=xt[:], in_=src[r0:r0 + 128, :]), writes=[xr], dkey=xr)
                jk, jr = junk.get(); s4, s4r = st4.get()
                p.op("act", I("activation", out=jk[:], in_=xt[:], func=AF.Square, accum_out=s4[:, 0:1]), reads=[xr], writes=[jr, (s4r, 0)])
                p.op("act", I("activation", out=s4[:, 1:2], in_=s4[:, 0:1], func=AF.Sqrt, scale=1.0 / 1024, bias=EPS), reads=[(s4r, 0)], writes=[(s4r, 1)])
                p.op("dve", I("reciprocal", out=s4[:, 2:3], in_=s4[:, 1:2]), reads=[(s4r, 1)], writes=[(s4r, 2)])
                xnt, xnr = xn.get()
                p.op("act", I("activation", out=xnt[:], in_=xt[:], func=AF.Copy, scale=s4[:, 2:3]), reads=[xr, (s4r, 2)], writes=[xnr])
                pt_, ptr = psg()
                ptb = pt_[:].bitcast(BF16)
                p.op("pe", TR([(ptb[:, k * 128:(k + 1) * 128], xnt[:, k * 128:(k + 1) * 128], ident[:]) for k in range(8)]),
                     reads=[xnr, "ident"], writes=[ptr])
                hTt, hTr = hT.get()
                p.op("dve", I("tensor_tensor", out=hTt[:], in0=ptb.rearrange("p (k t) -> p k t", k=8),
                              in1=g1T_sb[:, :].unsqueeze(2).broadcast_to([128, 8, 128]), op=ALU.mult),
                     reads=[ptr, "g1T"], writes=[hTr])

                def zgroup(c0, n):
                    z, zr = psa()
                    p.op("pe", MM([(z[:, 0:n], hTt[:, k, :], win_sb[:, k, c0:c0 + n], k == 0) for k in range(8)]),
                         reads=[hTr] + win_res, writes=[zr])
                    return z, zr

                col = (ti if kind == "ctx" else 16 + ti) * 128
                z, zr = zgroup(C_KV, NKV)
                zk, zkr = zkv.get()
                p.op("act", I("activation", out=zk[:], in_=z[:, :], func=AF.Copy), reads=[zr], writes=[zkr])
                if kind == "own":
                    outs.append(p.op("sp", I("dma_start", out=kvo[r0:r0 + 128, :], in_=zk[:]), reads=[zkr], dkey=("okv", ti % 2)))
                elif kind == "smp":
                    outs.append(p.op("sp", I("dma_start", out=kvs[:, :], in_=zk[:]), reads=[zkr], dkey="okvs"))
                z, zr = zgroup(C_WG, NWG)
                zw, zwr = zwg.get()
                p.op("act", I("activation", out=zw[:, 0:256], in_=z[:, 0:256], func=AF.Copy), reads=[zr], writes=[(zwr, 0)])
                if kind == "own" and ti >= 12:
                    outs.append(p.op("sp", I("dma_start", out=wino[(ti - 12) * 128:(ti - 11) * 128, :], in_=zw[:, 0:256]),
                                     reads=[(zwr, 0)], dkey=("owin", ti % 2)))
                if kind == "smp":
                    def wnew(e):
                        return [e.dma_start(out=wins[b, 504:512, :], in_=zw[b * 8:(b + 1) * 8, 0:256]) for b in range(16)]
                    outs.append(p.op("sp", wnew, reads=[(zwr, 0)], dkey="owins", ndma=16))
                zkbt, zkbr = zkvb.get()
                p.op("pool", I("tensor_copy", out=zkbt[:], in_=zk[:]), reads=[zkr], writes=[zkbr])
                zwbt, zwbr = zwgb.get()
                p.op("pool", I("tensor_copy", out=zwbt[:], in_=zw[:, 0:256]), reads=[(zwr, 0)], writes=[zwbr])
                if kind != "smp":
                    L = ti if kind == "ctx" else 16 + ti
                    pt3, pt3r = psg()
                    pt3b = pt3[:].bitcast(BF16)
                    p.op("pe", TR([(pt3b[:, 0:128], zkbt[:, 0:128], ident[:]), (pt3b[:, 128:256], zkbt[:, 128:256], ident[:]),
                                   (pt3b[:, 256:384], zkbt[:, 256:384], ident[:]), (pt3b[:, 384:512], zwbt[:, 0:128], ident[:])]),
                         reads=[zkbr, zwbr, "ident"], writes=[pt3r])
                    p.op("dve", I("tensor_copy", out=KT[:, :, col:col + 128], in_=pt3b[:, 0:512].rearrange("p (k t) -> p k t", k=4)),
                         reads=[pt3r], writes=[("KT", L)])
                    p.op("pool", I("tensor_copy", out=vsaug[:, L, :, 0:64], in_=zkbt[:, 384:512].rearrange("p (k d) -> p k d", k=2)),
                         reads=[zkbr], writes=[("vsaug", L)])
                    p.op("pool", I("tensor_copy", out=vwaug[:, L, :, 0:64], in_=zwbt[:, 128:256].rearrange("p (k d) -> p k d", k=2)),
                         reads=[zwbr], writes=[("vwaug", L)])
                if kind == "ctx":
                    return
                gi = 16 if kind == "smp" else ti
                p.op("act", I("activation", out=gates_sb[:, gi, :], in_=z[:, 256:280], func=AF.Sigmoid), reads=[zr], writes=[("gates", gi)])
                z, zr = zgroup(C_Q, NQ)
                qbt, qbr = qb.get()
                p.op("act", I("activation", out=qbt[:], in_=z[:, :], func=AF.Copy, scale=0.125), reads=[zr], writes=[qbr])
                pt4, pt4r = psg()
                pt4b = pt4[:].bitcast(BF16)
                p.op("pe", TR([(pt4b[:, k * 128:(k + 1) * 128], qbt[:, k * 128:(k + 1) * 128], ident[:]) for k in range(4)]),
                     reads=[qbr, "ident"], writes=[pt4r])
                if kind == "own":
                    p.op("dve", I("tensor_copy", out=qT[:, :, ti * 128:(ti + 1) * 128], in_=pt4b[:, 0:512].rearrange("p (k t) -> p k t", k=4)),
                         reads=[pt4r], writes=[("qT", ti)])
                else:
                    for k in range(2):
                        hs = slice(k * 64, (k + 1) * 64)
                        for g in range(4):
                            p.op("dve", I("tensor_copy", out=Qbd[hs, :, k * 32 + g * 8:k * 32 + g * 8 + 8],
                                          in_=pt4b[hs, g * 128:(g + 1) * 128].rearrange("p (b q) -> p b q", q=8)),
                                 reads=[pt4r, "Qbd"], writes=["Qbd"])
                    pt5, pt5r = psg()
                    pt5b = pt5[:].bitcast(BF16)
                    p.op("pe", TR([(pt5b[:, 0:128], zkbt[:, 256:384], ident[:]), (pt5b[:, 128:256], zwbt[:, 0:128], ident[:])]),
                         reads=[zkbr, zwbr, "ident"], writes=[pt5r])
                    p.op("dve", I("tensor_copy", out=KnT[:], in_=pt5b[:, 0:256].rearrange("p (k t) -> p k t", k=2)), reads=[pt5r], writes=["KnT"])
                z, zr = zgroup(C_U, NU)
                ut, ur = ub.get()
                p.op("act", I("activation", out=ut[:], in_=z[:, :], func=AF.Gelu_apprx_tanh), reads=[zr], writes=[ur])
                z, zr = zgroup(C_V, NV)
                vgt, vgr = vg.get()
                p.op("act", I("activation", out=vgt[:], in_=z[:, :], func=AF.Gelu_apprx_tanh), reads=[zr], writes=[vgr])
                b6, b6r = bst.get()
                p.op("dve", I("bn_stats", out=b6[:, 0:6], in_=vgt[:]), reads=[vgr], writes=[(b6r, 0)])
                p.op("dve", I("bn_aggr", out=s4[:, 3:5], in_=b6[:, 0:6]), reads=[(b6r, 0)], writes=[(s4r, 3)])
                p.op("act", I("activation", out=s4[:, 5:6], in_=s4[:, 4:5], func=AF.Sqrt, scale=1.0, bias=EPS), reads=[(s4r, 3)], writes=[(s4r, 5)])
                p.op("dve", I("reciprocal", out=s4[:, 6:7], in_=s4[:, 5:6]), reads=[(s4r, 5)], writes=[(s4r, 6)])
                vnt, vnr = vn.get()
                p.op("dve", I("tensor_scalar", out=vnt[:], in0=vgt[:], scalar1=s4[:, 3:4], scalar2=s4[:, 6:7], op0=ALU.subtract, op1=ALU.mult),
                     reads=[vgr, (s4r, 3), (s4r, 6)], writes=[vnr])
                p.op("pool", I("tensor_tensor", out=vnt[:], in0=vnt[:], in1=lvg_sb[:], op=ALU.mult), reads=[vnr, "lvg"], writes=[vnr])
                p.op("pool", I("tensor_tensor", out=vnt[:], in0=vnt[:], in1=lvb_sb[:], op=ALU.add), reads=[vnr, "lvb"], writes=[vnr])
                if kind == "smp":
                    outs.append(p.op("sp", I("dma_start", out=vso[:, :], in_=vnt[:]), reads=[vnr], dkey="ovs"))
                vbt, vbr = vnb.get()
                p.op("pool", I("tensor_copy", out=vbt[:], in_=vnt[:]), reads=[vnr], writes=[vbr])
                sp_, spr = psg()
                wsx = ws_s if kind == "smp" else ws_p
                bi = 1 if kind == "smp" else 0
                lst = [(sp_[:, g * 64:(g + 1) * 64], wsx[:, g, :], vbt[:, g * 64:(g + 1) * 64], g == 0) for g in range(8)]
                lst.append((sp_[:, :], bs_b[:, bi, :], gind[:, :], False))
                p.op("pe", MM(lst), reads=[vbr, "ws_p", "ws_s", "bs_b", "gind"], writes=[spr])
                obt, obr = ob.get()
                p.op("dve", I("tensor_tensor", out=obt[:], in0=sp_[:, :], in1=ut[:], op=ALU.mult), reads=[spr, ur], writes=[obr])
                pt2, pt2r = psg()
                pt2b = pt2[:].bitcast(BF16)
                p.op("pe", TR([(pt2b[:, k * 128:(k + 1) * 128], obt[:, k * 128:(k + 1) * 128], ident[:]) for k in range(4)]),
                     reads=[obr, "ident"], writes=[pt2r])
                mcol = (2048 if kind == "smp" else ti * 128)
                p.op("act", I("activation", out=mixT[:, 4:8, mcol:mcol + 128], in_=pt2b[:, 0:512].rearrange("p (k t) -> p k t", k=4), func=AF.Copy),
                     reads=[pt2r], writes=[("mixT", "b", mcol)])

            proj_tile("smp", 0)
            for ti in range(16):
                proj_tile("ctx", ti)
            for ti in range(16):
                proj_tile("own", ti)
            outs.append(p.op("sp", I("dma_start", out=wins[:, 0:504, :], in_=swin[:, 8:512, :]), dkey="owins2"))
            p.barrier()

        if stage < 2:
            p.op("pool", I("memset", ap=mixT[:, 0:4, :], constant=0.0), writes=[("mixT", "a")])
        elif stage < 3:
            p.op("pool", I("memset", ap=mixT[:, 0:4, 2048:2176], constant=0.0), writes=[("mixT", "a")])

        if stage >= 2:
          with contextlib.ExitStack() as s2:
            W1r = p.sb("W1r", [128, 2, 32, 128], BF16, s2)
            W2c = p.sb("W2c", [128, 2, 16, 128], BF16, s2)
            for stq in range(2):
                v1 = cw1[stq].rearrange("(s d) h -> d s h", d=64)
                for hf in range(2):
                    p.op("pool", I("dma_start", out=W1r[hf * 64:(hf + 1) * 64, stq, :, :], in_=v1), writes=[("W1r", stq, hf)], dkey=("w1r", hf))
                p.op("pool", I("dma_start", out=W2c[:, stq, :, :], in_=cw1[stq].rearrange("(c p) h -> p c h", p=128)), writes=[("W2c", stq)], dkey=("w2c", stq))
            W1r_res = [("W1r", a_, b_) for a_ in range(2) for b_ in range(2)]
            posc = p.sb("posc", [128, 2, 16], BF16, s2)
            p.op("pool", I("dma_start", out=posc[:], in_=cposd), writes=["posc"], dkey="c20")
            b1T = p.sb("b1T", [128, 2], F32, s2)
            p.op("sp", I("dma_start", out=b1T[:], in_=cb1d), writes=["b1T"], dkey="c21")
            w2pad = p.sb("w2pad", [128, 2, 2, 128], BF16, s2)
            p.op("dve", I("memset", ap=w2pad[:], constant=0.0), writes=["w2pad"])

            def w2dma(e, w2pad=w2pad):
                r = []
                for stq in range(2):
                    for va in range(2):
                        r.append(e.dma_start(out=w2pad[:, stq, va, va * 64:(va + 1) * 64], in_=cw2[stq]))
                return r
            p.op("pool", w2dma, writes=["w2pad"], dkey="c22", ndma=4)
            b2k2 = p.sb("b2k2", [128, 1], F32, s2)
            p.op("sp", I("dma_start", out=b2k2[:], in_=cb2kd), writes=["b2k2"], dkey="c23")
            b2vB = p.sb("b2vB", [128, 128], F32, s2)
            p.op("sp", I("dma_start", out=b2vB[:], in_=cb2vd), writes=["b2vB"], dkey="c24")
            kcT = p.sb("kcT", [128, 256], BF16, s2)
            vcaug = p.sb("vcaug", [128, 2, 2, 128], BF16, s2)
            p.op("dve", I("memset", ap=kcT[:], constant=0.0), writes=["kcT"])
            p.op("dve", I("memset", ap=vcaug[:], constant=0.0), writes=["vcaug"])

            def covdma(e, vcaug=vcaug):
                r = []
                for c2 in range(2):
                    for k in range(2):
                        r.append(e.dma_start(out=vcaug[:, c2, k, 64:128], in_=coverd[c2]))
                return r
            p.op("pool", covdma, writes=["vcaug"], dkey="c25", ndma=4)
            b1tot = p.sb("b1tot", [128, 2], F32, s2)
            z, zr = psg()
            lst = []
            for stq in range(2):
                for c in range(16):
                    lst.append((z[:, stq:stq + 1], W2c[:, stq, c, :], posc[:, stq, c:c + 1], (stq == 0 and c == 0)))
            p.op("pe", MM(lst), reads=[("W2c", 0), ("W2c", 1), "posc"], writes=[zr])
            p.op("dve", I("tensor_tensor", out=b1tot[:], in0=z[:, 0:2], in1=b1T[:], op=ALU.add), reads=[zr, "b1T"], writes=["b1tot"])
            ghT = p.sb("ghT", [128, 2, 2, 256], BF16, s2)
            KT_res = [("KT", L) for L in range(32)]
            for stq in range(2):
                for k in range(2):
                    z, zr = psg()
                    hs = slice(k * 64, (k + 1) * 64)
                    p.op("pe", MM([(z[:, 0:255], W1r[hs, stq, s_, :], KT[hs, stq, s_:s_ + 16 * 254 + 1:16], s_ == 0) for s_ in range(32)]),
                         reads=W1r_res + KT_res, writes=[zr])
                    p.op("act", I("activation", out=ghT[:, stq, k, 0:255], in_=z[:, 0:255], func=AF.Gelu_apprx_tanh, bias=b1tot[:, stq:stq + 1]),
                         reads=[zr, "b1tot"], writes=[("ghT", stq, k)])
            z, zr = psg()
            p.op("pe", MM([(z[:, 0:255], w2pad[:, 0, 0, :], ghT[:, 0, 0, 0:255], True), (z[:, 0:255], w2pad[:, 0, 1, :], ghT[:, 0, 1, 0:255], False)]),
                 reads=["w2pad", ("ghT", 0, 0), ("ghT", 0, 1)], writes=[zr])
            p.op("act", I("activation", out=kcT[:, 0:255], in_=z[:, 0:255], func=AF.Identity, bias=b2k2[:, 0:1]), reads=[zr, "b2k2"], writes=["kcT"])
            for c2 in range(2):
                nb = 128 if c2 == 0 else 127
                z, zr = psg()
                p.op("pe", MM([(z[0:nb, k * 64:(k + 1) * 64], ghT[:, 1, k, c2 * 128:c2 * 128 + nb], w2pad[:, 1, 0, 0:64], k == 0) for k in range(2)]),
                     reads=["w2pad", ("ghT", 1, 0), ("ghT", 1, 1)], writes=[zr])
                p.op("dve", I("tensor_tensor", out=vcaug[0:nb, c2, :, 0:64], in0=z[0:nb, 0:128].rearrange("p (k d) -> p k d", k=2),
                              in1=b2vB[0:nb, :].rearrange("p (k d) -> p k d", k=2), op=ALU.add),
                     reads=[zr, "b2vB", "vcaug"], writes=["vcaug"])

            cmpm = p.sb("cmpm", [128, 8, 512], BF16, s2)
            bandm = p.sb("bandm", [128, 8, 512], BF16, s2)
            negc = p.sb("negc", [128, 4, 512], BF16, s2)
            Eexp = p.sb("Eexp", [64, 32, 128], BF16, s2)
            for c in range(8):
                p.op("pool", I("dma_start", out=cmpm[:, c, :], in_=cmpmd[c]), writes=[("cmpm", c)], dkey=("cm", c % 2))
                p.op("pool", I("dma_start", out=bandm[:, c, :], in_=bandmd[c]), writes=[("bandm", c)], dkey=("bm", c % 2))
            for c in range(4):
                p.op("pool", I("dma_start", out=negc[:, c, :], in_=negcd[c]), writes=[("negc", c)], dkey=("nm", c % 2))
                p.op("pool", I("dma_start", out=Eexp[:, c * 8:(c + 1) * 8, :], in_=Ed[:, c * 8:(c + 1) * 8, :]), writes=[("E", c)], dkey=("em", c % 2))
            selb_sb = p.sb("selb_sb", [128, 16, 64], F32, s2)
            vis_sb = p.sb("vis_sb", [128, 16, 64], F32, s2)
            ctxb = p.sb("ctxb", [128, 2], F32, s2)
            p.op("sp", I("dma_start", out=selb_sb[:], in_=selbd), writes=["selb"], dkey="c26")
            p.op("sp", I("dma_start", out=vis_sb[:], in_=visd), writes=["vis"], dkey="c27")
            p.op("sp", I("dma_start", out=ctxb[:], in_=ctxbd), writes=["ctxb"], dkey="c28")
            pT = Rot(p, "pT", 4, [128, 512], BF16, s2)
            pTm = Rot(p, "pTm", 5, [128, 512], BF16, s2)
            maskb = Rot(p, "maskb", 2, [128, 512], BF16, s2)
            pcm = p.sb("pcm", [128, 8, 512], BF16, s2)
            selT = Rot(p, "selT", 2, [64, 512], BF16, s2)
            selbf = Rot(p, "selbf", 2, [128, 64], BF16, s2)
            sc_r = Rot(p, "sc", 2, [128, 64], F32, s2)
            scr_r = Rot(p, "scr", 2, [128, 64], F32, s2)
            sm = Rot(p, "sm", 4, [128, 32], F32, s2)
            oacc = Rot(p, "oacc", 8, [128, 4, 64], F32, s2)
            otmp = Rot(p, "otmp", 2, [128, 4, 64], F32, s2)
            oa_tok = p.sb("oa_tok", [128, 4, 512], BF16, s2)
            mres = [("cmpm", c) for c in range(8)] + [("bandm", c) for c in range(8)] + [("negc", c) for c in range(4)] + [("E", c) for c in range(4)]

            def sT_exp(kstream, kcol, k, g, qt, use_ctx):
                hs = slice(k * 64, (k + 1) * 64)
                z, zr = psg()
                if kstream is None:
                    lhsT = kcT[hs, kcol:kcol + 128]
                    rd = ["kcT"]
                else:
                    lhsT = KT[hs, kstream, kcol:kcol + 128]
                    rd = [("KT", kcol // 128)]
                p.op("pe", MM([(z[:, :], lhsT, qT[hs, g, qt * 512:(qt + 1) * 512], True)]),
                     reads=rd + [("qT", qt * 4 + j) for j in range(4)], writes=[zr])
                t, tr = pT.get()
                bi = 0 if use_ctx else 1
                p.op("act", I("activation", out=t[:], in_=z[:, :], func=AF.Exp, bias=ctxb[:, bi:bi + 1]), reads=[zr, "ctxb"], writes=[tr])
                return t, tr

            def finish_branch(obanks, br, k, qt, accs, width, first):
                for sub in range(4):
                    ti = qt * 4 + sub
                    o, orr = obanks[sub]
                    o3 = o[:, 0:4 * width].rearrange("p (g w) -> p g w", g=4)
                    m, mr = sm.get()
                    if width == 128:
                        p.op("dve", I("tensor_reduce", out=m[:, 0:4], in_=o3[:, :, 64:128], axis=AX.X, op=ALU.add), reads=[orr], writes=[(mr, 0)])
                    else:
                        p.op("dve", I("tensor_copy", out=m[:, 0:4], in_=o3[:, :, 64]), reads=[orr], writes=[(mr, 0)])
                    p.op("dve", I("tensor_scalar", out=m[:, 12:16], in0=m[:, 0:4], scalar1=1e-30, scalar2=None, op0=ALU.max), reads=[(mr, 0)], writes=[(mr, 5)])
                    p.op("dve", I("reciprocal", out=m[:, 4:8], in_=m[:, 12:16]), reads=[(mr, 5)], writes=[(mr, 1)])
                    gc = br * 8 + k * 4
                    p.op("dve", I("tensor_tensor", out=m[:, 8:12], in0=m[:, 4:8], in1=gates_sb[:, ti, gc:gc + 4], op=ALU.mult),
                         reads=[(mr, 1), ("gates", ti)], writes=[(mr, 2)])
                    coefb = m[:, 8:12].unsqueeze(2).broadcast_to([128, 4, 64])
                    a, ar = accs[sub]
                    if first:
                        p.op("dve", I("tensor_tensor", out=a[:], in0=o3[:, :, 0:64], in1=coefb, op=ALU.mult), reads=[orr, (mr, 2)], writes=[ar])
                    else:
                        tt, ttr = otmp.get()
                        p.op("dve", I("tensor_tensor", out=tt[:], in0=o3[:, :, 0:64], in1=coefb, op=ALU.mult), reads=[orr, (mr, 2)], writes=[ttr])
                        p.op("pool", I("tensor_tensor", out=a[:], in0=a[:], in1=tt[:], op=ALU.add), reads=[ttr, ar], writes=[ar])
                    if width == 128:
                        sc, scr = sc_r.get()
                        for g in range(4):
                            in1 = selb_sb[:, ti, :] if g == 0 else sc[:]
                            p.op("dve", I("scalar_tensor_tensor", out=sc[:], in0=o3[:, g, 64:128], scalar=m[:, 4 + g:5 + g], in1=in1, op0=ALU.mult, op1=ALU.add),
                                 reads=[orr, (mr, 1), "selb", scr], writes=[scr])
                        p.op("dve", I("max", out=m[:, 16:24], in_=sc[:]), reads=[scr], writes=[(mr, 3)])
                        s2_, s2r = scr_r.get()
                        p.op("dve", I("match_replace", out=s2_[:], in_to_replace=m[:, 16:24], in_values=sc[:], imm_value=-3.0e38), reads=[scr, (mr, 3)], writes=[s2r])
                        p.op("dve", I("max", out=m[:, 24:32], in_=s2_[:]), reads=[s2r], writes=[(mr, 4)])
                        sb_, sbr = selbf.get()
                        p.op("dve", I("scalar_tensor_tensor", out=sb_[:], in0=sc[:], scalar=m[:, 31:32], in1=vis_sb[:, ti, :], op0=ALU.is_ge, op1=ALU.mult),
                             reads=[scr, (mr, 4), "vis"], writes=[sbr])
                        z, zr = psg()
                        zb = z[:].bitcast(BF16)
                        p.op("pe", TR([(zb[0:64, 0:128], sb_[:, :], ident[:])]), reads=[sbr, "ident"], writes=[zr])
                        p.op("act", I("activation", out=cur_selT[0][0:64, sub * 128:(sub + 1) * 128], in_=zb[0:64, 0:128], func=AF.Copy),
                             reads=[zr], writes=[(cur_selT[1], sub)])

            cur_selT = [None, None]
            for qt in range(4):
                for k in range(2):
                    hs = slice(k * 64, (k + 1) * 64)
                    accs = [oacc.get() for _ in range(4)]
                    cur_selT[0], cur_selT[1] = selT.get()
                    for c2 in range(2):
                        for g in range(4):
                            t, tr = sT_exp(None, c2 * 128, k, g, qt, c2 == 0)
                            p.op("pool", I("tensor_tensor", out=pcm[:, c2 * 4 + g, :], in0=t[:], in1=cmpm[:, qt * 2 + c2, :], op=ALU.mult),
                                 reads=[tr] + mres, writes=[("pcm", c2 * 4 + g)])
                    ob = [psa() for _ in range(4)]
                    for sub in range(4):
                        o, orr = ob[sub]
                        lst = []
                        for g in range(4):
                            for c2 in range(2):
                                lst.append((o[:, g * 128:(g + 1) * 128], pcm[:, c2 * 4 + g, sub * 128:(sub + 1) * 128], vcaug[:, c2, k, :], (g == 0 and c2 == 0)))
                        p.op("pe", MM(lst), reads=[("pcm", j) for j in range(8)] + ["vcaug"], writes=[orr])
                    finish_branch(ob, 0, k, qt, accs, 128, True)
                    nch = 16 + 4 * (qt + 1)
                    ob = [psa() for _ in range(4)]
                    obr = [r_ for (_, r_) in ob]
                    pend = []
                    LAG = 2

                    def pv_slc(it, ob=ob, obr=obr, k=k):
                        tm, tmr, c, g = it
                        p.op("pe", MM([(ob[sub][0][:, g * 65:(g + 1) * 65], tm[:, sub * 128:(sub + 1) * 128], vsaug[:, c, k, :], (c == 0 and g == 0))
                                       for sub in range(4)]),
                             reads=[tmr, ("vsaug", c), "vs1"], writes=obr)
                    for c in range(nch):
                        z, zr = psg()
                        d = c - (16 + 4 * qt)
                        lst = [(z[:, :], Eexp[:, c, :], cur_selT[0][:, :], True)]
                        if d >= 0:
                            lst.append((z[:, :], ident[:], negc[:, d, :], False))
                        p.op("pe", MM(lst), reads=[(cur_selT[1], j) for j in range(4)] + mres + ["ident"], writes=[zr])
                        mb, mbr = maskb.get()
                        p.op("dve", I("tensor_scalar", out=mb[:], in0=z[:, :], scalar1=0.0, scalar2=None, op0=ALU.max), reads=[zr], writes=[mbr])
                        for g in range(4):
                            t, tr = sT_exp(2, c * 128, k, g, qt, c < 16)
                            tm, tmr = pTm.get()
                            p.op("dve" if g % 2 == 0 else "pool", I("tensor_tensor", out=tm[:], in0=t[:], in1=mb[:], op=ALU.mult), reads=[tr, mbr], writes=[tmr])
                            pend.append((tm, tmr, c, g))
                            if len(pend) > LAG:
                                pv_slc(pend.pop(0))
                    while pend:
                        pv_slc(pend.pop(0))
                    finish_branch(ob, 1, k, qt, accs, 65, False)
                    ob = [psa() for _ in range(4)]
                    obr = [r_ for (_, r_) in ob]
                    pend = []

                    def pv_win(it, ob=ob, obr=obr, k=k):
                        tm, tmr, c, g, kc_ = it
                        p.op("pe", MM([(ob[sub][0][:, g * 65:(g + 1) * 65], tm[:, sub * 128:(sub + 1) * 128], vwaug[:, kc_, k, :], (c == 0 and g == 0))
                                       for sub in range(4)]),
                             reads=[tmr, ("vwaug", kc_), "vw1"], writes=obr)
                    for c in range(8):
                        kc_ = 16 + 4 * qt - 4 + c
                        for g in range(4):
                            t, tr = sT_exp(3, kc_ * 128, k, g, qt, kc_ < 16)
                            tm, tmr = pTm.get()
                            p.op("dve" if g % 2 == 0 else "pool", I("tensor_tensor", out=tm[:], in0=t[:], in1=bandm[:, c, :], op=ALU.mult), reads=[tr] + mres, writes=[tmr])
                            pend.append((tm, tmr, c, g, kc_))
                            if len(pend) > LAG:
                                pv_win(pend.pop(0))
                    while pend:
                        pv_win(pend.pop(0))
                    finish_branch(ob, 2, k, qt, accs, 65, False)
                    for sub in range(4):
                        a, ar = accs[sub]
                        p.op("pool", I("tensor_copy", out=oa_tok[:, sub, k * 256:(k + 1) * 256], in_=a[:].rearrange("p g d -> p (g d)")),
                             reads=[ar], writes=[("oa_tok", sub, k)])
                for sub in range(4):
                    ti = qt * 4 + sub
                    z, zr = psg()
                    zb = z[:].bitcast(BF16)
                    p.op("pe", TR([(zb[:, j * 128:(j + 1) * 128], oa_tok[:, sub, j * 128:(j + 1) * 128], ident[:]) for j in range(4)]),
                         reads=[("oa_tok", sub, 0), ("oa_tok", sub, 1), "ident"], writes=[zr])
                    p.op("act", I("activation", out=mixT[:, 0:4, ti * 128:(ti + 1) * 128], in_=zb[:, 0:512].rearrange("p (k t) -> p k t", k=4), func=AF.Copy),
                         reads=[zr], writes=[("mixT", "a", ti * 128)])
            p.barrier()

        sP.close()
        import os as _os0
        if stage >= 3 and _os0.environ.get('DBG_NOS') != '1':
          with contextlib.ExitStack() as sS:
            idx_i = p.sb("idx_i", [128, 512], I32, sS); idx_f = p.sb("idx_f", [128, 512], F32, sS); idx = p.sb("idx", [128, 512], I32, sS)
            pcol = p.sb("pcol", [128, 1], F32, sS)
            p.op("sp", I("dma_start", out=idx_i[:], in_=ptxd.rearrange("p b c -> p (b c)")), writes=["idx_i"], dkey="c30")
            p.op("sp", I("dma_start", out=pcol[:], in_=pcold), writes=["pcol"], dkey="c31")
            p.op("dve", I("tensor_copy", out=idx_f[:], in_=idx_i[:]), reads=["idx_i"], writes=["idx_f"])
            p.op("dve", I("tensor_scalar", out=idx_f[:], in0=idx_f[:], scalar1=64.0, scalar2=pcol[:, 0:1], op0=ALU.mult, op1=ALU.add),
                 reads=["idx_f", "pcol"], writes=["idx_f"])
            p.op("dve", I("tensor_copy", out=idx[:], in_=idx_f[:]), reads=["idx_f"], writes=["idx"])
            W1s = p.sb("W1s", [128, 2, 32, 128], BF16, sS)
            for stq in range(2):
                v1 = cw1[stq].rearrange("(s d) h -> d s h", d=64)
                for hf in range(2):
                    p.op("pool", I("dma_start", out=W1s[hf * 64:(hf + 1) * 64, stq, :, :], in_=v1), writes=[("W1s", stq, hf)], dkey=("w1r", hf))
            W1s_res = [("W1s", a_, b_) for a_ in range(2) for b_ in range(2)]
            posd_sb = p.sb("posd_sb", [64, 2, 32], BF16, sS)
            p.op("pool", I("dma_start", out=posd_sb[:], in_=cpos64d), writes=["posd"], dkey="c20")
            b1T = p.sb("b1Ts", [128, 2], F32, sS)
            p.op("sp", I("dma_start", out=b1T[:], in_=cb1d), writes=["b1T"], dkey="c21")
            w2pad = p.sb("w2pads", [128, 2, 2, 128], BF16, sS)
            p.op("dve", I("memset", ap=w2pad[:], constant=0.0), writes=["w2pad"])

            def w2dma_s(e, w2pad=w2pad):
                r = []
                for stq in range(2):
                    for va in range(2):
                        r.append(e.dma_start(out=w2pad[:, stq, va, va * 64:(va + 1) * 64], in_=cw2[stq]))
                return r
            p.op("pool", w2dma_s, writes=["w2pad"], dkey="c22", ndma=4)
            b2k2 = p.sb("b2k2s", [128, 1], F32, sS)
            p.op("sp", I("dma_start", out=b2k2[:], in_=cb2kd), writes=["b2k2"], dkey="c23")
            b2vB = p.sb("b2vBs", [128, 128], F32, sS)
            p.op("sp", I("dma_start", out=b2vB[:], in_=cb2vd), writes=["b2vB"], dkey="c24")
            b1tot = p.sb("b1tots", [128, 2], F32, sS)
            z, zr = psg()
            lst = []
            for stq in range(2):
                for c in range(32):
                    lst.append((z[:, stq:stq + 1], W1s[0:64, stq, c, :], posd_sb[:, stq, c:c + 1], (stq == 0 and c == 0)))
            p.op("pe", MM(lst), reads=W1s_res + ["posd"], writes=[zr])
            p.op("dve", I("tensor_tensor", out=b1tot[:], in0=z[:, 0:2], in1=b1T[:], op=ALU.add), reads=[zr, "b1T"], writes=["b1tot"])
            fbs = p.sb("fbs", [64, 129], F32, sS); sel2 = p.sb("sel2", [64, 64], F32, sS)
            cm8f = p.sb("cm8f", [64, 8], F32, sS); cm8 = p.sb("cm8", [64, 8], BF16, sS)
            wm = p.sb("wm", [64, 512], BF16, sS); selg = p.sb("selg", [24, 3, 4, 128], F32, sS)
            p.op("sp", I("dma_start", out=fbs[:], in_=fbsd), writes=["fbs"], dkey="c32")
            p.op("sp", I("dma_start", out=sel2[:], in_=sel2d), writes=["sel2"], dkey="c33")
            p.op("sp", I("dma_start", out=cm8f[:], in_=cm8d), writes=["cm8f"], dkey="c34")
            p.op("dve", I("tensor_copy", out=cm8[:], in_=cm8f[:]), reads=["cm8f"], writes=["cm8"])
            p.op("pool", I("dma_start", out=wm[:], in_=wmd), writes=["wm"], dkey="c35")
            p.op("sp", I("dma_start", out=selg[:], in_=selgd), writes=["selg"], dkey="c36")
            vcs = p.sb("vcs", [128, 4, 257], BF16, sS)
            p.op("dve", I("memset", ap=vcs[:], constant=0.0), writes=["vcs"])
            p.op("pool", I("dma_start", out=vcs[:, :, 128:257], in_=covsd), writes=["vcs"], reads=["vcs"], dkey="c37")
            Vn = p.sb("Vn", [8, 2, 16, 129], BF16, sS)
            p.op("dve", I("memset", ap=Vn[:], constant=1.0), writes=["Vn"])
            p.op("pool", I("dma_start", out=Vn[:, 0, :, 0:128], in_=kvs.rearrange("(b q) c -> q b c", q=8)[:, :, 384:512]), reads=["Vn"], writes=["Vn"], dkey="c38")
            p.op("pool", I("dma_start", out=Vn[:, 1, :, 0:128], in_=wins[:, 504:512, 128:256].rearrange("b q c -> q b c")), reads=["Vn"], writes=["Vn"], dkey="c39")
            X2T = p.sb("X2T", [128, 2, 2, 8, 512], BF16, sS)
            KsT = p.sb("KsT", [128, 2, 4096], BF16, sS)
            Vs = p.sb("Vs", [128, 32, 2, 129], BF16, sS)
            p.op("pool", I("memset", ap=Vs[:, :, :, 128:129], constant=1.0), writes=["Vs1"])
            G = Rot(p, "G", 4, [128, 1024], BF16, sS)
            gh = p.sb("gh", [128, 2, 2, 512], BF16, sS)
            kcTs = p.sb("kcTs", [128, 512], BF16, sS)
            Pc = Rot(p, "Pc", 2, [64, 512], BF16, sS)
            for t_ in Pc.t:
                p.op("dve", I("memset", ap=t_[:], constant=0.0), writes=[("Pc", Pc.t.index(t_))])
            PcT = Rot(p, "PcT", 2, [128, 4, 64], BF16, sS)
            Pq = Rot(p, "Pq", 3, [64, 512], BF16, sS)
            Pqm = Rot(p, "Pqm", 4, [64, 512], BF16, sS)
            PsT = Rot(p, "PsT", 4, [128, 4, 64], BF16, sS)
            Pn = Rot(p, "Pn", 2, [64, 8], BF16, sS); Pnm = Rot(p, "Pnm", 2, [64, 8], BF16, sS); PnT = Rot(p, "PnT", 2, [8, 64], BF16, sS)
            on_r = Rot(p, "on", 3, [64, 128], BF16, sS)
            smm = Rot(p, "smm", 6, [64, 32], F32, sS)
            Pe32 = Rot(p, "Pe32", 2, [64, 512], F32, sS)
            Pg = Rot(p, "Pg", 2, [128, 4, 16], BF16, sS)
            Pg32 = Rot(p, "Pg32", 2, [128, 4, 16], F32, sS)
            sel16 = Rot(p, "sel16", 2, [16, 129], BF16, sS)
            rep16f = p.sb("rep16f", [16, 64], F32, sS); rep16 = p.sb("rep16", [16, 64], BF16, sS)
            p.op("sp", I("dma_start", out=rep16f[:], in_=rep16d), writes=["rep16f"], dkey="c41")
            p.op("dve", I("tensor_copy", out=rep16[:], in_=rep16f[:]), reads=["rep16f"], writes=["rep16"])
            scs = Rot(p, "scs", 2, [64, 129], F32, sS); scs2 = Rot(p, "scs2", 2, [64, 129], F32, sS)
            sel_r = Rot(p, "sel", 2, [64, 129], BF16, sS)
            SW = Rot(p, "SW", 2, [128, 4, 256], BF16, sS)
            SWv = Rot(p, "SWv", 2, [128, 4, 129], BF16, sS)
            for i_, t_ in enumerate(SWv.t):
                p.op("pool", I("memset", ap=t_[:, :, 128:129], constant=1.0), writes=[("SWv1", i_)])
            KwT = Rot(p, "KwT", 2, [128, 512], BF16, sS)
            OBR = p.sb("OBR", [128, 3, 4, 128], F32, sS)
            id64 = ident[0:64, 0:64]
            G_all = [("G", i_) for i_ in range(4)]

            def to_obr(ont, onr, br, b):
                z, zr = psg()
                zb = z[:].bitcast(BF16)
                p.op("pe", TR([(zb[:, 0:64], ont[:, :], id64)]), reads=[onr, "ident"], writes=[zr])
                for k in range(2):
                    hs = slice(k * 64, (k + 1) * 64)
                    p.op("act", I("activation", out=OBR[hs, br, :, b * 8:(b + 1) * 8], in_=zb[hs, k * 32:(k + 1) * 32].rearrange("p (g q) -> p g q", q=8), func=AF.Copy),
                         reads=[zr], writes=[("OBR", br, b, k)])

            def finish_s(o, orr, br, b, rs_from_cover):
                m, mr = smm.get()
                p.op("dve", I("tensor_scalar", out=m[:, 1:2], in0=o[0:64, 128:129], scalar1=1e-30, scalar2=None, op0=ALU.max), reads=[orr], writes=[(mr, 1)])
                p.op("dve", I("reciprocal", out=m[:, 2:3], in_=m[:, 1:2]), reads=[(mr, 1)], writes=[(mr, 2)])
                ont, onr = on_r.get()
                p.op("dve", I("tensor_scalar", out=ont[:], in0=o[0:64, 0:128], scalar1=m[:, 2:3], scalar2=None, op0=ALU.mult), reads=[orr, (mr, 2)], writes=[onr])
                to_obr(ont, onr, br, b)
                return m, mr

            def new_keys(o, orr, which, b, tag):
                z, zr = psg()
                p.op("pe", MM([(z[0:64, 0:8], Qbd[:, b, :], KnT[:, which, b * 8:(b + 1) * 8], True)]), reads=["Qbd", "KnT"], writes=[zr])
                t, tr = Pn.get()
                p.op("act", I("activation", out=t[:], in_=z[0:64, 0:8], func=AF.Exp), reads=[zr], writes=[tr])
                tm, tmr = Pnm.get()
                p.op("dve", I("tensor_tensor", out=tm[:], in0=t[:], in1=cm8[:], op=ALU.mult), reads=[tr, "cm8"], writes=[tmr])
                z2, z2r = psg()
                z2b = z2[:].bitcast(BF16)
                p.op("pe", TR([(z2b[0:8, 0:64], tm[:, :], id64)]), reads=[tmr, "ident"], writes=[z2r])
                tt, ttr = PnT.get()
                p.op("act", I("activation", out=tt[:], in_=z2b[0:8, 0:64], func=AF.Copy), reads=[z2r], writes=[ttr])
                p.op("pe", MM([(o[0:64, 0:129], tt[:, :], Vn[:, which, b, :], False)]), reads=[ttr, "Vn"], writes=[orr])

            def dk_A(kT_ap, kT_res, mask_fn):
                z, zr = psg()
                p.op("pe", MM([(z[0:64, :], Qbd[:, b_cur[0], :], kT_ap, True)]), reads=["Qbd"] + kT_res, writes=[zr])
                t, tr = Pq.get()
                p.op("act", I("activation", out=t[:], in_=z[0:64, :], func=AF.Exp), reads=[zr], writes=[tr])
                tm, tmr = Pqm.get()
                mask_fn(tm, tmr, t, tr)
                return tm, tmr

            def dk_B(tm, tmr):
                z2, z2r = psg()
                z2b = z2[:].bitcast(BF16)
                p.op("pe", TR([(z2b[:, j * 64:(j + 1) * 64], tm[:, j * 128:(j + 1) * 128], id64) for j in range(4)]), reads=[tmr, "ident"], writes=[z2r])
                tt, ttr = PsT.get()
                p.op("act", I("activation", out=tt[:], in_=z2b[:, 0:256].rearrange("p (j c) -> p j c", j=4), func=AF.Copy), reads=[z2r], writes=[ttr])
                return tt, ttr

            def dk_C(o, orr, tt, ttr, v_fn, v_res, first):
                p.op("pe", MM([(o[0:64, 0:129], tt[:, j, :], v_fn(j), first and j == 0) for j in range(4)]), reads=[ttr] + v_res, writes=[orr])

            def dense_keys(o, orr, kT_ap, kT_res, mask_fn, v_fn, v_res, first):
                tm, tmr = dk_A(kT_ap, kT_res, mask_fn)
                tt, ttr = dk_B(tm, tmr)
                dk_C(o, orr, tt, ttr, v_fn, v_res, first)

            b_cur = [0]
            cache_v = cache
            import os as _os
            SPART = int(_os.environ.get('DBG_SPART', '5'))
            for b in range(int(_os.environ.get('DBG_NB', '16'))):
                b_cur[0] = b
                for c in range(32):
                    Gt, Gr = G.get()
                    p.op("pool", I("indirect_dma_start", out=Gt[:], out_offset=None, in_=cache_v,
                                   in_offset=bass.IndirectOffsetOnAxis(ap=idx[:, b * 32 + c:b * 32 + c + 1], axis=0)),
                         reads=["idx"], writes=[Gr], dkey=Gr)
                    if _os.environ.get('DBG_NOTR') == '1':
                        p.op("pool", I("tensor_copy", out=Vs[:, c, :, 0:128], in_=Gt[:].rearrange("p (t r) -> p t r", t=2)[:, :, 384:512]), reads=[Gr], writes=[("Vs", c)])
                        continue
                    z, zr = psg()
                    zb = z[:].bitcast(BF16)
                    lst = []
                    for stq in range(3):
                        for t2 in range(2):
                            o_ = t2 * 512 + stq * 128
                            lst.append((zb[:, (stq * 2 + t2) * 128:(stq * 2 + t2 + 1) * 128], Gt[:, o_:o_ + 128], ident[:]))
                    p.op("pe", TR(lst), reads=[Gr, "ident"], writes=[zr])
                    for stq in range(2):
                        for t2 in range(2):
                            o_ = (stq * 2 + t2) * 128
                            p.op("act" if t2 == 0 else "dve",
                                 I("activation", out=X2T[:, stq, t2, :, 16 * c:16 * c + 16], in_=zb[:, o_:o_ + 128].rearrange("p (i c) -> p c i", i=16, c=8), func=AF.Copy)
                                 if t2 == 0 else
                                 I("tensor_copy", out=X2T[:, stq, t2, :, 16 * c:16 * c + 16], in_=zb[:, o_:o_ + 128].rearrange("p (i c) -> p c i", i=16, c=8)),
                                 reads=[zr], writes=[("X2T", c, stq, t2)])
                    p.op("dve", I("tensor_copy", out=KsT[:, :, c * 128:(c + 1) * 128], in_=zb[:, 512:768].rearrange("p (k t) -> p k t", k=2)),
                         reads=[zr], writes=[("KsT", c)])
                    p.op("pool", I("tensor_copy", out=Vs[:, c, :, 0:128], in_=Gt[:].rearrange("p (t r) -> p t r", t=2)[:, :, 384:512]), reads=[Gr], writes=[("Vs", c)])
                X2T_res = [("X2T", c, q_, t_) for c in range(32) for q_ in range(2) for t_ in range(2)]
                if SPART < 2:
                    continue
                for stq in range(2):
                    zz = [psg(), psg()]
                    lst = []
                    for s_ in range(32):
                        for k in range(2):
                            hs = slice(k * 64, (k + 1) * 64)
                            cc = s_ // 2
                            rhs_ = X2T[hs, stq, s_ % 2, cc, 0:511] if cc < 8 else X2T[hs, stq, s_ % 2, cc - 8, 1:512]
                            lst.append((zz[k][0][:, 0:511], W1s[hs, stq, s_, :], rhs_, s_ == 0))
                    p.op("pe", MM(lst), reads=W1s_res + X2T_res, writes=[zz[0][1], zz[1][1]])
                    for k in range(2):
                        p.op("act", I("activation", out=gh[:, stq, k, 0:511], in_=zz[k][0][:, 0:511], func=AF.Gelu_apprx_tanh, bias=b1tot[:, stq:stq + 1]),
                             reads=[zz[k][1], "b1tot"], writes=[("gh", stq, k)])
                z, zr = psg()
                p.op("pe", MM([(z[:, 0:511], w2pad[:, 0, 0, :], gh[:, 0, 0, 0:511], True), (z[:, 0:511], w2pad[:, 0, 1, :], gh[:, 0, 1, 0:511], False)]),
                     reads=["w2pad", ("gh", 0, 0), ("gh", 0, 1)], writes=[zr])
                p.op("act", I("activation", out=kcTs[:, 0:511], in_=z[:, 0:511], func=AF.Identity, bias=b2k2[:, 0:1]), reads=[zr, "b2k2"], writes=["kcTs"])
                for c4 in range(4):
                    nb = 128 if c4 < 3 else 127
                    z, zr = psg()
                    p.op("pe", MM([(z[0:nb, k * 64:(k + 1) * 64], gh[:, 1, k, c4 * 128:c4 * 128 + nb], w2pad[:, 1, 0, 0:64], k == 0) for k in range(2)]),
                         reads=["w2pad", ("gh", 1, 0), ("gh", 1, 1)], writes=[zr])
                    p.op("dve", I("tensor_tensor", out=vcs[0:nb, c4, 0:128], in0=z[0:nb, 0:128], in1=b2vB[0:nb, :], op=ALU.add),
                         reads=[zr, "b2vB", "vcs"], writes=[("vcs", c4)])
                if SPART < 3:
                    continue
                z, zr = psg()
                p.op("pe", MM([(z[0:64, 0:511], Qbd[:, b, :], kcTs[:, 0:511], True)]), reads=["Qbd", "kcTs"], writes=[zr])
                m, mr = smm.get()
                pe_, per = Pe32.get()
                p.op("act", I("activation", out=pe_[:, 0:511], in_=z[0:64, 0:511], func=AF.Exp, accum_out=m[:, 0:1]), reads=[zr], writes=[per, (mr, 0)])
                p.op("dve", I("tensor_scalar", out=m[:, 1:2], in0=m[:, 0:1], scalar1=1e-30, scalar2=None, op0=ALU.max), reads=[(mr, 0)], writes=[(mr, 1)])
                p.op("dve", I("reciprocal", out=m[:, 2:3], in_=m[:, 1:2]), reads=[(mr, 1)], writes=[(mr, 2)])
                pc, pcr = Pc.get()
                p.op("dve", I("tensor_scalar", out=pc[:, 0:511], in0=pe_[:, 0:511], scalar1=m[:, 2:3], scalar2=None, op0=ALU.mult), reads=[per, (mr, 2)], writes=[pcr])
                z2, z2r = psg()
                z2b = z2[:].bitcast(BF16)
                p.op("pe", TR([(z2b[:, j * 64:(j + 1) * 64], pc[:, j * 128:(j + 1) * 128], id64) for j in range(4)]), reads=[pcr, "ident"], writes=[z2r])
                pct, pctr = PcT.get()
                p.op("act", I("activation", out=pct[:], in_=z2b[:, 0:256].rearrange("p (j c) -> p j c", j=4), func=AF.Copy), reads=[z2r], writes=[pctr])
                pg32, pg32r = Pg32.get()
                p.op("dve", I("tensor_reduce", out=pg32[:].rearrange("p j (k q) -> p j k q", k=2),
                              in_=pct[:].rearrange("p j (k g q) -> p j k q g", k=2, g=4), axis=AX.X, op=ALU.add), reads=[pctr], writes=[pg32r])
                pg, pgr = Pg.get()
                p.op("dve", I("tensor_copy", out=pg[:], in_=pg32[:]), reads=[pg32r], writes=[pgr])
                oc, ocr = psa()
                p.op("pe", MM([(oc[0:64, 0:128], pct[:, j, :], vcs[:, j, 0:128], j == 0) for j in range(4)]),
                     reads=[pctr, "vcs"] + [("vcs", j) for j in range(4)], writes=[ocr])
                ont, onr = on_r.get()
                p.op("act", I("activation", out=ont[:], in_=oc[0:64, 0:128], func=AF.Copy), reads=[ocr], writes=[onr])
                to_obr(ont, onr, 0, b)
                z, zr = psg()
                p.op("pe", MM([(z[0:16, 0:129], pg[:, j, :], vcs[:, j, 128:257], j == 0) for j in range(4)]), reads=[pgr, "vcs"], writes=[zr])
                sc, scr = scs.get()
                p.op("dve", I("tensor_tensor", out=sc[0:16, :], in0=z[0:16, 0:129], in1=fbs[0:16, :], op=ALU.add), reads=[zr, "fbs"], writes=[scr])
                p.op("dve", I("max", out=m[0:16, 8:16], in_=sc[0:16, :]), reads=[scr], writes=[(mr, 3)])
                sc2, sc2r = scs2.get()
                p.op("dve", I("match_replace", out=sc2[0:16, :], in_to_replace=m[0:16, 8:16], in_values=sc[0:16, :], imm_value=-3.0e38), reads=[scr, (mr, 3)], writes=[sc2r])
                p.op("dve", I("max", out=m[0:16, 16:24], in_=sc2[0:16, :]), reads=[sc2r], writes=[(mr, 4)])
                s16, s16r = sel16.get()
                p.op("dve", I("tensor_scalar", out=s16[:], in0=sc[0:16, :], scalar1=m[0:16, 23:24], scalar2=None, op0=ALU.is_ge), reads=[scr, (mr, 4)], writes=[s16r])
                z, zr = psg()
                p.op("pe", MM([(z[0:64, 0:129], rep16[:, :], s16[:, :], True)]), reads=[s16r, "rep16"], writes=[zr])
                sel, selr = sel_r.get()
                p.op("act", I("activation", out=sel[:], in_=z[0:64, 0:129], func=AF.Copy), reads=[zr], writes=[selr])
                if SPART < 4:
                    continue
                osl, oslr = psa()
                qa, qb_ = [], []
                nC = [0]

                def run_C(it):
                    tt, ttr, t2, mm_ = it
                    dk_C(osl, oslr, tt, ttr, lambda j, t2=t2, mm_=mm_: Vs[:, 4 * mm_ + j, t2, :], [("Vs", 4 * mm_ + j) for j in range(4)] + ["Vs1"], nC[0] == 0)
                    nC[0] += 1

                def run_B(it):
                    tm, tmr, t2, mm_ = it
                    tt, ttr = dk_B(tm, tmr)
                    qb_.append((tt, ttr, t2, mm_))
                    if len(qb_) > 1:
                        run_C(qb_.pop(0))
                for t2 in range(2):
                    for mm_ in range(8):
                        def mask_sel(tm, tmr, t, tr, mm_=mm_):
                            p.op("dve", I("tensor_tensor", out=tm[:].rearrange("p (j r) -> p j r", r=32), in0=t[:].rearrange("p (j r) -> p j r", r=32),
                                          in1=sel[:, 16 * mm_:16 * mm_ + 16].unsqueeze(2).broadcast_to([64, 16, 32]), op=ALU.mult),
                                 reads=[tr, selr], writes=[tmr])
                        tm, tmr = dk_A(KsT[:, t2, mm_ * 512:(mm_ + 1) * 512], [("KsT", 4 * mm_ + j) for j in range(4)], mask_sel)
                        qa.append((tm, tmr, t2, mm_))
                        if len(qa) > 1:
                            run_B(qa.pop(0))
                while qa:
                    run_B(qa.pop(0))
                while qb_:
                    run_C(qb_.pop(0))
                new_keys(osl, oslr, 0, b, "s")
                finish_s(osl, oslr, 1, b, False)
                if SPART < 5:
                    continue
                swt, swr = SW.get()
                p.op("pool", I("dma_start", out=swt[:], in_=swin[b].rearrange("(c p) f -> p c f", p=128)), writes=[swr], dkey=swr)
                z, zr = psg()
                zb = z[:].bitcast(BF16)
                p.op("pe", TR([(zb[:, c * 128:(c + 1) * 128], swt[:, c, 0:128], ident[:]) for c in range(4)]), reads=[swr, "ident"], writes=[zr])
                kw, kwr = KwT.get()
                p.op("act", I("activation", out=kw[:], in_=zb[:, 0:512], func=AF.Copy), reads=[zr], writes=[kwr])
                sv, svr = SWv.get()
                p.op("pool", I("tensor_copy", out=sv[:, :, 0:128], in_=swt[:, :, 128:256]), reads=[swr], writes=[svr])
                ow, owr = psa()

                def mask_win(tm, tmr, t, tr):
                    p.op("dve", I("tensor_tensor", out=tm[:], in0=t[:], in1=wm[:], op=ALU.mult), reads=[tr, "wm"], writes=[tmr])
                dense_keys(ow, owr, kw[:, :], [kwr], mask_win, lambda j, sv=sv: sv[:, j, :], [svr], True)
                new_keys(ow, owr, 1, b, "w")
                finish_s(ow, owr, 2, b, False)
            ghi = p.sb("ghi", [128, 24], BF16, sS); ghi32 = p.sb("ghi32", [128, 24], F32, sS); glo = p.sb("glo", [128, 24], BF16, sS)
            p.op("dve", I("tensor_copy", out=ghi[:], in_=gates_sb[:, 16, :]), reads=[("gates", 16)], writes=["ghi"])
            p.op("dve", I("tensor_copy", out=ghi32[:], in_=ghi[:]), reads=["ghi"], writes=["ghi32"])
            p.op("dve", I("tensor_tensor", out=glo[:], in0=gates_sb[:, 16, :], in1=ghi32[:], op=ALU.subtract), reads=[("gates", 16), "ghi32"], writes=["glo"])
            z, zr = psg()
            zb = z[:].bitcast(BF16)
            p.op("pe", TR([(zb[0:24, 0:128], ghi[:, :], ident[:]), (zb[0:24, 128:256], glo[:, :], ident[:])]), reads=["ghi", "glo", "ident"], writes=[zr])
            gT = p.sb("gT", [24, 2, 128], BF16, sS)
            p.op("act", I("activation", out=gT[:], in_=zb[0:24, 0:256].rearrange("p (a t) -> p a t", a=2), func=AF.Copy), reads=[zr], writes=["gT"])
            selgb = p.sb("selgb", [24, 3, 4, 128], BF16, sS)
            p.op("dve", I("tensor_copy", out=selgb[:], in_=selg[:]), reads=["selg"], writes=["selgb"])
            oaT = p.sb("oaT", [128, 512], F32, sS)
            oat2 = p.sb("oat2", [128, 512], F32, sS)
            obr_res = [("OBR", br, b, k) for br in range(3) for b in range(16) for k in range(2)]
            for br in range(3):
                z, zr = psg()
                lst = []
                for g in range(4):
                    lst.append((z[:, g * 128:(g + 1) * 128], selgb[:, br, g, :], gT[:, 0, :], g == 0))
                    lst.append((z[:, g * 128:(g + 1) * 128], selgb[:, br, g, :], gT[:, 1, :], False))
                p.op("pe", MM(lst), reads=["selgb", "gT"], writes=[zr])
                src = OBR[:, br, :, :].rearrange("p g t -> p (g t)")
                if br == 0:
                    p.op("dve", I("tensor_tensor", out=oaT[:], in0=z[:, :], in1=src, op=ALU.mult), reads=[zr] + obr_res, writes=["oaT"])
                else:
                    p.op("dve", I("tensor_tensor", out=oat2[:], in0=z[:, :], in1=src, op=ALU.mult), reads=[zr] + obr_res, writes=["oat2"])
                    p.op("pool", I("tensor_tensor", out=oaT[:], in0=oaT[:], in1=oat2[:], op=ALU.add), reads=["oaT", "oat2"], writes=["oaT"])
            if _os.environ.get('DBG_NOFIN') != '1':
                p.op("act", I("activation", out=mixT[:, 0:4, 2048:2176], in_=oaT[:].rearrange("p (g t) -> p g t", g=4), func=AF.Copy), reads=["oaT"], writes=[("mixT", "a")])
            p.barrier()
        with contextlib.ExitStack() as s3:
            wout_sb = p.sb("wout_sb", [128, 8, 1024], BF16, s3)
            w_out_v = w_out.rearrange("(kc p) n -> p kc n", p=128)
            for kc in range(8):
                p.op("pool", I("dma_start", out=wout_sb[:, kc, :], in_=w_out_v[:, kc, :]), writes=[("wout", kc)], dkey=("wout", kc % 4))
            wouts_sb = p.sb("wouts_sb", [128, 4, 1024], BF16, s3)

            def wouts_dma(e, wouts_sb=wouts_sb):
                r = []
                for g in range(4):
                    for k in range(2):
                        h_ = 4 * k + g
                        r.append(e.dma_start(out=wouts_sb[k * 64:(k + 1) * 64, g, :], in_=w_out[h_ * 64:(h_ + 1) * 64, :]))
                return r
            p.op("pool", wouts_dma, writes=["wouts"], dkey="c40", ndma=8)
            wout_res = [("wout", kc) for kc in range(8)] + ["wouts"]
            gf_sb = p.sb("gf_sb", [128, 1024], F32, s3)
            p.op("sp", I("dma_start", out=gf_sb[:], in_=gfB), writes=["gf"], dkey="c11")
            w1s = Rot(p, "w1s", 2, [128, 8, 512], BF16, s3)
            w2s = Rot(p, "w2s", 3, [128, 4, 512], BF16, s3)
            fT = p.sb("fT", [128, 32, 512], BF16, s3)
            hnT = p.sb("hnT", [128, 8, 512], BF16, s3)
            h2 = p.sb("h2", [128, 4, 1024], F32, s3)
            xbuf = Rot(p, "xb3", 2, [128, 1024], F32, s3)
            junk = Rot(p, "junk3", 2, [128, 1024], BF16, s3)
            hnb = Rot(p, "hnb", 2, [128, 1024], BF16, s3)
            st4 = Rot(p, "st43", 4, [128, 8], F32, s3)
            rl = Rot(p, "rl", 3, [128, 512], F32, s3)
            yb = Rot(p, "yb", 4, [128, 1024], F32, s3)
            w_ff1_v = w_ff1.rearrange("(kc p) f -> p kc f", p=128)
            w_ff2_v = w_ff2.rearrange("(fc p) n -> p fc n", p=128)

            groups = [[("own", t) for t in range(0, 4)], [("own", t) for t in range(4, 8)],
                      [("own", t) for t in range(8, 12)], [("own", t) for t in range(12, 16)], [("smp", 0)]]
            for grp in groups:
                nt = len(grp)
                ntok = nt * 128
                for si, (kind, ti) in enumerate(grp):
                    src = xo if kind == "own" else xs
                    r0 = ti * 128
                    mcol = 2048 if kind == "smp" else ti * 128
                    xt, xr = xbuf.get()
                    p.op("sp", I("dma_start", out=xt[:], in_=src[r0:r0 + 128, :]), writes=[xr], dkey=xr)
                    for half in range(2):
                        z, zr = psa()
                        p.op("pe", MM([(z[:, :], mixT[:, k, mcol:mcol + 128],
                                        (wouts_sb if (kind == "smp" and k < 4 and stage >= 3) else wout_sb)[:, k, half * 512:(half + 1) * 512], k == 0) for k in range(8)]),
                             reads=wout_res + [("mixT", "a"), ("mixT", "a", mcol), ("mixT", "b", mcol)], writes=[zr])
                        p.op("dve", I("tensor_tensor", out=h2[:, si, half * 512:(half + 1) * 512], in0=z[:, :], in1=xt[:, half * 512:(half + 1) * 512], op=ALU.add),
                             reads=[zr, xr], writes=[("h2", si, half)])
                    jk, jr = junk.get(); s4, s4r = st4.get()
                    p.op("act", I("activation", out=jk[:], in_=h2[:, si, :], func=AF.Square, accum_out=s4[:, 0:1]),
                         reads=[("h2", si, 0), ("h2", si, 1)], writes=[jr, (s4r, 0)])
                    p.op("act", I("activation", out=s4[:, 1:2], in_=s4[:, 0:1], func=AF.Sqrt, scale=1.0 / 1024, bias=EPS), reads=[(s4r, 0)], writes=[(s4r, 1)])
                    p.op("dve", I("reciprocal", out=s4[:, 2:3], in_=s4[:, 1:2]), reads=[(s4r, 1)], writes=[(s4r, 2)])
                    hb, hbr = hnb.get()
                    p.op("act", I("activation", out=hb[:], in_=h2[:, si, :], func=AF.Copy, scale=s4[:, 2:3]),
                         reads=[("h2", si, 0), ("h2", si, 1), (s4r, 2)], writes=[hbr])
                    pt_, ptr = psg()
                    ptb = pt_[:].bitcast(BF16)
                    p.op("pe", TR([(ptb[:, k * 128:(k + 1) * 128], hb[:, k * 128:(k + 1) * 128], ident[:]) for k in range(8)]),
                         reads=[hbr, "ident"], writes=[ptr])
                    p.op("dve", I("tensor_tensor", out=hnT[:, :, si * 128:(si + 1) * 128], in0=ptb.rearrange("p (k t) -> p k t", k=8),
                                  in1=g2T_sb[:, :].unsqueeze(2).broadcast_to([128, 8, 128]), op=ALU.mult),
                         reads=[ptr, "g2T"], writes=[("hnT", si)])
                hn_res = [("hnT", si) for si in range(nt)]
                for c in range(8):
                    w1t, w1r = w1s.get()
                    p.op("pool", I("dma_start", out=w1t[:], in_=w_ff1_v[:, :, c * 512:(c + 1) * 512]), writes=[w1r], dkey=w1r)
                    for f4 in range(4):
                        fc = c * 4 + f4
                        for (t0, tn) in [(0, ntok)]:
                            z, zr = psg()
                            p.op("pe", MM([(z[:, 0:tn], w1t[:, k, f4 * 128:(f4 + 1) * 128], hnT[:, k, t0:t0 + tn], k == 0) for k in range(8)]),
                                 reads=[w1r] + hn_res, writes=[zr])
                            rt, rr = rl.get()
                            p.op("act", I("activation", out=rt[:, 0:tn], in_=z[:, 0:tn], func=AF.Relu), reads=[zr], writes=[rr])
                            p.op("pool", I("tensor_tensor", out=fT[:, fc, t0:t0 + tn], in0=rt[:, 0:tn], in1=rt[:, 0:tn], op=ALU.mult),
                                 reads=[rr], writes=[("fT", fc, t0)])
                yts = [yb.get() for _ in range(nt)]
                for half in range(2):
                    accs = [psa() for _ in range(nt)]
                    for c in range(8):
                        w2t, w2r = w2s.get()
                        p.op("pool", I("dma_start", out=w2t[:], in_=w_ff2_v[:, c * 4:(c + 1) * 4, half * 512:(half + 1) * 512]), writes=[w2r], dkey=w2r)
                        for si in range(nt):
                            z, zr = accs[si]
                            p.op("pe", MM([(z[:, :], fT[:, c * 4 + f4, si * 128:(si + 1) * 128], w2t[:, f4, :], (c == 0 and f4 == 0)) for f4 in range(4)]),
                                 reads=[w2r] + [("fT", c * 4 + f4, 0) for f4 in range(4)], writes=[zr])
                    for si in range(nt):
                        z, zr = accs[si]
                        yt, yr = yts[si]
                        p.op("dve", I("tensor_tensor", out=yt[:, half * 512:(half + 1) * 512], in0=z[:, :], in1=h2[:, si, half * 512:(half + 1) * 512], op=ALU.add),
                             reads=[zr, ("h2", si, half)], writes=[(yr, half)])
                for si, (kind, ti) in enumerate(grp):
                    dst = yo if kind == "own" else ys
                    r0 = ti * 128
                    yt, yr = yts[si]
                    jk, jr = junk.get(); s4, s4r = st4.get()
                    p.op("act", I("activation", out=jk[:], in_=yt[:], func=AF.Square, accum_out=s4[:, 0:1]), reads=[(yr, 0), (yr, 1)], writes=[jr, (s4r, 0)])
                    p.op("act", I("activation", out=s4[:, 1:2], in_=s4[:, 0:1], func=AF.Sqrt, scale=1.0 / 1024, bias=EPS), reads=[(s4r, 0)], writes=[(s4r, 1)])
                    p.op("dve", I("reciprocal", out=s4[:, 2:3], in_=s4[:, 1:2]), reads=[(s4r, 1)], writes=[(s4r, 2)])
                    p.op("dve", I("scalar_tensor_tensor", out=yt[:], in0=yt[:], scalar=s4[:, 2:3], in1=gf_sb[:], op0=ALU.mult, op1=ALU.mult),
                         reads=[(yr, 0), (yr, 1), (s4r, 2), "gf"], writes=[(yr, 0), (yr, 1)])
                    outs.append(p.op("sp", I("dma_start", out=dst[r0:r0 + 128, :], in_=yt[:]), reads=[(yr, 0), (yr, 1)], dkey=("oy", si)))
        p.emit(final_waits=outs)
    return nc


def _consts():
    I_ = np.arange(128)[:, None]
    q_ = np.arange(512)[None, :]
    cover = np.zeros((2, 128, 64), np.float32)
    for c2 in range(2):
        for i in range(128):
            bi = c2 * 128 + i
            if bi > 254:
                continue
            for j in range(64):
                ov = min(16 * bi + 32, 64 * j + 64) - max(16 * bi, 64 * j)
                if ov > 0:
                    cover[c2, i, j] = ov / 32.0
    cmpm = np.zeros((8, 128, 512), np.float32)
    for qt in range(4):
        for c2 in range(2):
            bi = c2 * 128 + I_
            cmpm[qt * 2 + c2] = ((bi <= 254) & (16 * bi + 31 <= 2048 + 512 * qt + q_)).astype(np.float32)
    bandm = np.zeros((8, 128, 512), np.float32)
    for c in range(8):
        kk = 128 * c + I_
        bandm[c] = ((kk > q_) & (kk <= q_ + 512)).astype(np.float32)
    negc = np.zeros((4, 128, 512), np.float32)
    for d in range(4):
        negc[d] = -((128 * d + I_) > q_).astype(np.float32)
    E = np.zeros((64, 32, 128), np.float32)
    for c in range(32):
        for m_ in range(128):
            E[2 * c + m_ // 64, c, m_] = 1.0
    return dict(cover=cover, cmpm=cmpm, bandm=bandm, negc=negc, Eexp=E)


def _sample_consts():
    covs = np.zeros((128, 4, 129), np.float32)
    for c4 in range(4):
        for i in range(128):
            bi = c4 * 128 + i
            if bi > 510:
                continue
            for j in range(129):
                ov = min(16 * bi + 32, 64 * j + 64) - max(16 * bi, 64 * j)
                if ov > 0:
                    covs[i, c4, j] = ov / 32.0
    fbs = np.zeros((64, 129), np.float32)
    fbs[:, [0, 127, 128]] = 1e9
    r = np.arange(64)
    k_, g_, q_ = r // 32, (r // 8) % 4, r % 8
    sel2 = ((k_[:, None] == k_[None, :]) & (q_[:, None] == q_[None, :])).astype(np.float32)
    cm8 = (np.arange(8)[None, :] <= q_[:, None]).astype(np.float32)
    wm = (np.arange(512)[None, :] > q_[:, None]).astype(np.float32)
    selg = np.zeros((24, 3, 4, 128), np.float32)
    for br in range(3):
        for g in range(4):
            for k in range(2):
                selg[br * 8 + 4 * k + g, br, g, k * 64:(k + 1) * 64] = 1.0
    pcol = (np.arange(128) % 64).astype(np.float32)[:, None]
    r16 = np.arange(16)
    rep16 = ((r16[:, None] // 8 == k_[None, :]) & (r16[:, None] % 8 == q_[None, :])).astype(np.float32)
    return dict(covs=covs, fbs=fbs, sel2=sel2, cm8=cm8, wm=wm, selg=selg, pcol=pcol, rep16=rep16)


def _core_consts(h):
    first = 0 if h == 1 else 32
    selb = np.zeros((128, 16, 64), np.float32)
    vis = np.zeros((128, 16, 64), np.float32)
    j = np.arange(64)[None, :]
    for t in range(16):
        tl = 2048 + t * 128 + np.arange(128)[:, None]
        cur = tl // 64
        visible = (j >= first) & (j <= cur)
        forced = (j == first) | (j == cur) | (j == cur - 1)
        b = np.where(forced, 1e9, 0.0)
        b = np.where(visible, b, -1e30)
        selb[:, t, :] = b
        vis[:, t, :] = visible
    ctxb = np.zeros((128, 2), np.float32)
    if h == 0:
        ctxb[:, 0] = NEG
    return dict(selb=selb, vis=vis, ctxb=ctxb)


def _host_inputs(inp):
    f = lambda a: np.ascontiguousarray(np.asarray(a, dtype=np.float32))
    w_in = f(inp["w_in"][0])
    qperm = []
    for g in range(4):
        for k in range(2):
            h = k * 4 + g
            qperm += list(range(h * 64, (h + 1) * 64))
    perm = np.array(qperm + list(range(512, INC)))
    w_in_p = np.ascontiguousarray(w_in[:, perm])
    rep = lambda v, n=128: np.ascontiguousarray(np.broadcast_to(np.asarray(v, np.float32)[None, :], (n, len(v))))
    colT = lambda v: np.ascontiguousarray(np.asarray(v, np.float32).reshape(8, 128).T)
    b_s = f(inp["b_s"][0])
    gind = np.zeros((8, 512), np.float32)
    for g in range(8):
        gind[g, g * 64:(g + 1) * 64] = 1.0
    ii = np.arange(128)
    common = dict(
        w_in=w_in_p, w_out=f(inp["w_out"][0]), w_ff1=f(inp["w_ff1"][0]), w_ff2=f(inp["w_ff2"][0]),
        g1T=colT(inp["ln1_g"][0]), g2T=colT(inp["ln2_g"][0]), gfB=rep(inp["ln_f_g"]),
        lvg=rep(inp["ln_v_g"][0]), lvb=rep(inp["ln_v_b"][0]),
        wsT=np.ascontiguousarray(f(inp["w_s"][0]).transpose(0, 2, 1)),
        bs8=b_s, bs8s=np.ascontiguousarray(np.tile(b_s[:, 0:8], (1, 16))),
        ident=np.eye(128, dtype=np.float32),
        tril=(ii[:, None] <= ii[None, :]).astype(np.float32),
        gind=gind,
    )
    common.update(_consts())
    common.update(_sample_consts())
    cache = np.asarray(inp["cache_kv"], dtype=np.float32).reshape(-1, 1024)
    common["cache"] = cache
    pt = np.asarray(inp["page_table"]).astype(np.int32)
    pp = np.arange(128) // 64
    cw1 = f(inp["cmp_w1"][0]); cpos = f(inp["cmp_pos"][0]).reshape(2, 16, 128)
    cb2 = f(inp["cmp_b2"][0])
    common.update(dict(
        cw1=cw1, cpos=np.ascontiguousarray(cpos.transpose(2, 0, 1)),
        cpos64=np.ascontiguousarray(f(inp["cmp_pos"][0]).transpose(2, 0, 1)), cb1=np.ascontiguousarray(f(inp["cmp_b1"][0]).T),
        cw2=f(inp["cmp_w2"][0]), cb2k=np.ascontiguousarray(np.tile(cb2[0], 2)[:, None]),
        cb2v=rep(np.tile(cb2[1], 2)),
    ))
    xp = f(inp["x_prompt"]); xs = f(inp["x_sample"]).reshape(1024, 1024)
    swin = f(inp["state_win_kv"][0]).reshape(128, 512, 256)
    maps = []
    for c in range(8):
        b, h = c // 2, c % 2
        m = dict(common)
        m["xo"] = np.ascontiguousarray(xp[b, h * 2048:(h + 1) * 2048])
        m["xc"] = np.ascontiguousarray(xp[b, 0:2048])
        m["xs"] = np.ascontiguousarray(xs[c * 128:(c + 1) * 128])
        m["swin"] = np.ascontiguousarray(swin[c * 16:(c + 1) * 16])
        m.update(_core_consts(h))
        ptc = pt[c * 16:(c + 1) * 16].reshape(16, 32, 2)
        m["ptx"] = np.ascontiguousarray(ptc[:, :, pp].transpose(2, 0, 1))
        maps.append(m)
    return maps


STAGE = 3
_NC = {}


def kernel(**inp):
    return _run(_host_inputs(inp))


def _run(maps):
    npool = maps[0]["cache"].shape[0] // 64
    if (STAGE, npool) not in _NC:
        _NC[(STAGE, npool)] = build(STAGE, npool)
    nc = _NC[(STAGE, npool)]
    maps = [{"i_" + k: v for k, v in m.items()} for m in maps]
    res = run_bass_kernel_spmd(nc, maps, core_ids=list(range(8)))
    r = [{k[2:]: v for k, v in d.items()} for d in res.results]
    y_p = np.zeros((4, 4096, 1024), np.float32)
    kv_p = np.zeros((1, 4, 4096, 512), np.float32)
    win_p = np.zeros((1, 4, 512, 256), np.float32)
    for c in range(8):
        b, h = c // 2, c % 2
        y_p[b, h * 2048:(h + 1) * 2048] = r[c]["yo"]
        kv_p[0, b, h * 2048:(h + 1) * 2048] = r[c]["kvo"]
        if h == 1:
            win_p[0, b] = r[c]["wino"]
    y_s = np.concatenate([r[c]["ys"] for c in range(8)], 0).reshape(128, 8, 1024)
    kv_s = np.concatenate([r[c]["kvs"] for c in range(8)], 0).reshape(1, 128, 8, 4, 2, 64)
    win_s = np.concatenate([r[c]["wins"] for c in range(8)], 0).reshape(1, 128, 512, 2, 2, 64)
    v_s = np.concatenate([r[c]["vso"] for c in range(8)], 0).reshape(1, 128, 8, 512)
    return (y_p, y_s, kv_p.reshape(1, 4, 4096, 4, 2, 64), kv_s, win_p.reshape(1, 4, 512, 2, 2, 64), win_s, v_s)
```

```python
import contextlib
import numpy as np
import concourse.bass as bass
import concourse.mybir as mybir
from concourse.bass_utils import run_bass_kernel_spmd

F32 = mybir.dt.float32
BF16 = mybir.dt.bfloat16
I32 = mybir.dt.int32
AF = mybir.ActivationFunctionType
ALU = mybir.AluOpType
AX = mybir.AxisListType

ENGS = ("pe", "act", "dve", "pool", "sp")
EPS = 1e-6
NEG = -30000.0


class Op:
    __slots__ = ("eng", "fn", "deps", "idx", "sig", "sem", "val", "n_dma")

    def __init__(self, eng, fn):
        self.eng = eng
        self.fn = fn
        self.deps = []
        self.sig = False
        self.sem = None
        self.val = 0
        self.n_dma = 0


class Prog:
    def __init__(self, nc, stack):
        self.nc = nc
        self.stack = stack
        self.ops = []
        self.lastw = {}
        self.readers = {}
        self.esem = {}
        self.dsem = {}
        self.dcount = {}
        self.dlast = {}
        self.elast = {}
        self.ecount = {e: 0 for e in ENGS}
        for e in ("pe", "act", "dve", "pool"):
            self.esem[e] = stack.enter_context(nc.semaphore("s_" + e))
        self.nps = 0

    def sb(self, name, shape, dt, stack=None):
        return (stack or self.stack).enter_context(self.nc.sbuf_tensor(name, list(shape), dt))

    def _dsem(self, key):
        if key not in self.dsem:
            self.dsem[key] = self.stack.enter_context(self.nc.semaphore("d_" + str(len(self.dsem))))
            self.dcount[key] = 0
        return self.dsem[key]

    def op(self, eng, fn, reads=(), writes=(), dkey=None, ndma=1, extra=()):
        o = Op(eng, fn)
        o.idx = len(self.ops)
        deps = set(extra)
        _psr = [r for r in reads if isinstance(r, tuple) and r and r[0] == "ps"]
        if _psr:
            reads = [r for r in reads if not (isinstance(r, tuple) and r and r[0] == "ps")]
            writes = list(writes) + _psr
        if isinstance(dkey, str) and dkey.startswith("c"):
            dkey = "cser"
            if "cser" in self.dlast:
                deps.add(self.dlast["cser"])
        for r in reads:
            w = self.lastw.get(r)
            if w is not None:
                deps.add(w)
        for r in writes:
            w = self.lastw.get(r)
            if w is not None:
                deps.add(w)
            for rd in self.readers.get(r, ()):
                deps.add(rd)
        for r in reads:
            self.readers.setdefault(r, []).append(o)
        for r in writes:
            self.lastw[r] = o
            self.readers[r] = []
        if dkey is not None:
            o.sem = self._dsem(dkey)
            self.dcount[dkey] += 16 * ndma
            o.val = self.dcount[dkey]
            o.n_dma = ndma
            o.sig = True
            self.dlast[dkey] = o
        else:
            self.elast[eng] = o
        deps.discard(o)
        for d in deps:
            if d.eng == "pe" and eng == "pe" and d.n_dma == 0:
                continue
            o.deps.append(d)
        self.ops.append(o)
        return o

    def barrier(self):
        tg = list(self.elast.values()) + list(self.dlast.values())
        for e in ENGS:
            o = Op(e, lambda eng: None)
            o.deps = [d for d in tg]
            self.ops.append(o)
        self.lastw = {}
        self.readers = {}
        if not hasattr(self, "cuts"):
            self.cuts = []
        self.cuts.append(len(self.ops))

    def emit(self, final_waits=()):
        nc = self.nc
        for o in self.ops:
            for d in o.deps:
                if d.n_dma == 0:
                    d.sig = True
        for o in self.ops:
            if o.n_dma == 0 and o.sig:
                self.ecount[o.eng] += 1
                o.sem = self.esem[o.eng]
                o.val = self.ecount[o.eng]
        handles = {"pe": "tensor", "act": "scalar", "dve": "vector", "pool": "gpsimd", "sp": "sync"}
        cuts = [0] + list(getattr(self, "cuts", [])) + [len(self.ops)]
        seen_all = {e: {} for e in ENGS}
        nseg = len(cuts) - 1
        for si in range(nseg):
            seg = self.ops[cuts[si]:cuts[si + 1]]
            if not seg:
                continue
            per_eng = {e: [o for o in seg if o.eng == e] for e in ENGS}
            last_seg = (si == nseg - 1)
            with nc.Block() as block:
                for e in ENGS:
                    ops = per_eng[e]

                    def body(eng, ops=ops, e=e, last_seg=last_seg):
                        seen = seen_all[e]
                        for o in ops:
                            need = {}
                            for d in o.deps:
                                k = id(d.sem)
                                if seen.get(k, 0) >= d.val:
                                    continue
                                if k not in need or need[k][1] < d.val:
                                    need[k] = (d.sem, d.val)
                            for k, (s, v) in need.items():
                                eng.wait_ge(s, v)
                                seen[k] = v
                            insts = o.fn(eng)
                            if o.n_dma:
                                if not isinstance(insts, (list, tuple)):
                                    insts = [insts]
                                assert len(insts) == o.n_dma, (len(insts), o.n_dma)
                                for i in insts:
                                    i.then_inc(o.sem, 16)
                            elif o.sig:
                                if isinstance(insts, (list, tuple)):
                                    insts = insts[-1]
                                insts.then_inc(o.sem, 1)
                        if e == "sp" and last_seg:
                            for o in final_waits:
                                eng.wait_ge(o.sem, o.val)

                    getattr(block, handles[e])(body)


def I(method, **kw):
    return lambda e: getattr(e, method)(**kw)


def MM(lst):
    def f(e):
        last = None
        for (out, lhsT, rhs, start) in lst:
            last = e.matmul(out, lhsT, rhs, start=start, stop=True, skip_group_check=True)
        return last
    return f


def TR(lst):
    def f(e):
        last = None
        for (out, in_, ident) in lst:
            last = e.transpose(out, in_, ident)
        return last
    return f


class Rot:
    def __init__(self, p, name, n, shape, dt, stack=None):
        self.t = [p.sb("%s%d" % (name, i), shape, dt, stack) for i in range(n)]
        self.name = name
        self.n = n
        self.i = 0

    def get(self):
        k = self.i % self.n
        self.i += 1
        return self.t[k], (self.name, k)


NQ, NKV, NWG, NU, NV = 512, 512, 280, 512, 512
C_Q, C_KV, C_WG, C_U, C_V = 0, 512, 1024, 1304, 1816
INC = 2328


def build(stage, npool=10240):
    nc = bass.Bass("TRN2", target_bir_lowering=False)

    def DI(name, shape, dt=F32):
        return nc.dram_tensor("i_" + name, list(shape), dt, kind="ExternalInput").ap()

    def DO(name, shape, dt=F32):
        return nc.dram_tensor("o_" + name, list(shape), dt, kind="ExternalOutput").ap()

    xo = DI("xo", [2048, 1024]); xc = DI("xc", [2048, 1024]); xs = DI("xs", [128, 1024])
    swin = DI("swin", [16, 512, 256])
    w_in = DI("w_in", [1024, INC]); w_out = DI("w_out", [1024, 1024])
    w_ff1 = DI("w_ff1", [1024, 4096]); w_ff2 = DI("w_ff2", [4096, 1024])
    g1T = DI("g1T", [128, 8]); g2T = DI("g2T", [128, 8]); gfB = DI("gfB", [128, 1024])
    lvg = DI("lvg", [128, 512]); lvb = DI("lvb", [128, 512])
    wsT = DI("wsT", [8, 128, 128]); bs8 = DI("bs8", [8, 128]); bs8s = DI("bs8s", [8, 128])
    identd = DI("ident", [128, 128]); trild = DI("tril", [128, 128]); gindd = DI("gind", [8, 512])

    cw1 = DI("cw1", [2, 2048, 128]); cposd = DI("cpos", [128, 2, 16]); cb1d = DI("cb1", [128, 2])
    cw2 = DI("cw2", [2, 128, 64]); cb2kd = DI("cb2k", [128, 1]); cb2vd = DI("cb2v", [128, 128])
    coverd = DI("cover", [2, 128, 64])
    cmpmd = DI("cmpm", [8, 128, 512]); bandmd = DI("bandm", [8, 128, 512]); negcd = DI("negc", [4, 128, 512])
    Ed = DI("Eexp", [64, 32, 128]); selbd = DI("selb", [128, 16, 64]); visd = DI("vis", [128, 16, 64]); ctxbd = DI("ctxb", [128, 2])

    cache = DI("cache", [npool * 64, 1024]); ptxd = DI("ptx", [128, 16, 32], I32); pcold = DI("pcol", [128, 1])
    covsd = DI("covs", [128, 4, 129]); fbsd = DI("fbs", [64, 129]); sel2d = DI("sel2", [64, 64])
    cpos64d = DI("cpos64", [64, 2, 32]); cm8d = DI("cm8", [64, 8]); wmd = DI("wm", [64, 512]); selgd = DI("selg", [24, 3, 4, 128]); rep16d = DI("rep16", [16, 64])

    yo = DO("yo", [2048, 1024]); ys = DO("ys", [128, 1024])
    kvo = DO("kvo", [2048, 512]); kvs = DO("kvs", [128, 512])
    wino = DO("wino", [512, 256]); wins = DO("wins", [16, 512, 256]); vso = DO("vso", [128, 512])

    outs = []
    with contextlib.ExitStack() as st:
        p = Prog(nc, st)
        ps = [st.enter_context(nc.psum_tensor("ps%d" % i, [128, 512], F32)) for i in range(8)]
        rot = {"g": 0, "a": 0}

        def psg():
            k = rot["g"] % 4
            rot["g"] += 1
            return ps[k], ("ps", k)

        def psa():
            k = 4 + rot["a"] % 4
            rot["a"] += 1
            return ps[k], ("ps", k)

        ident_f = p.sb("ident_f", [128, 128], F32)
        ident = p.sb("ident", [128, 128], BF16)
        p.op("sp", I("dma_start", out=ident_f[:], in_=identd), writes=["ident_f"], dkey="c0")
        p.op("dve", I("tensor_copy", out=ident[:], in_=ident_f[:]), reads=["ident_f"], writes=["ident"])
        g1T_sb = p.sb("g1T_sb", [128, 8], F32); g2T_sb = p.sb("g2T_sb", [128, 8], F32)
        p.op("sp", I("dma_start", out=g1T_sb[:], in_=g1T), writes=["g1T"], dkey="c1")
        p.op("sp", I("dma_start", out=g2T_sb[:], in_=g2T), writes=["g2T"], dkey="c2")
        mixT = p.sb("mixT", [128, 8, 2176], BF16)
        sP = contextlib.ExitStack()
        Qbd = p.sb("Qbd", [128, 16, 64], BF16)
        KnT = p.sb("KnT", [128, 2, 128], BF16)
        p.op("pool", I("memset", ap=Qbd[:], constant=0.0), writes=["Qbd"])
        gates_sb = p.sb("gates_sb", [128, 17, 24], F32)
        KT = p.sb("KT", [128, 4, 4096], BF16, sP)
        vsaug = p.sb("vsaug", [128, 32, 2, 65], BF16, sP)
        vwaug = p.sb("vwaug", [128, 32, 2, 65], BF16, sP)
        qT = p.sb("qT", [128, 4, 2048], BF16, sP)
        p.op("pool", I("memset", ap=vsaug[:, :, :, 64:65], constant=1.0), writes=["vs1"])
        p.op("pool", I("memset", ap=vwaug[:, :, :, 64:65], constant=1.0), writes=["vw1"])

        with contextlib.ExitStack() as s1:
            win_sb = p.sb("win_sb", [128, 8, INC], BF16, s1)
            w_in_v = w_in.rearrange("(kc p) n -> p kc n", p=128)
            for kc in range(8):
                for h in range(2):
                    c0 = h * 1164
                    p.op("pool", I("dma_start", out=win_sb[:, kc, c0:c0 + 1164], in_=w_in_v[:, kc, c0:c0 + 1164]),
                         writes=[("win", kc, h)], dkey=("win", (kc * 2 + h) % 4))
            win_res = [("win", kc, h) for kc in range(8) for h in range(2)]
            lvg_sb = p.sb("lvg_sb", [128, 512], F32, s1); lvb_sb = p.sb("lvb_sb", [128, 512], F32, s1)
            p.op("sp", I("dma_start", out=lvg_sb[:], in_=lvg), writes=["lvg"], dkey="c3")
            p.op("sp", I("dma_start", out=lvb_sb[:], in_=lvb), writes=["lvb"], dkey="c4")
            tril_sb = p.sb("tril_sb", [128, 128], F32, s1)
            p.op("sp", I("dma_start", out=tril_sb[:], in_=trild), writes=["tril"], dkey="c5")
            ws_f = p.sb("ws_f", [128, 8, 128], F32, s1)
            ws_p = p.sb("ws_p", [128, 8, 128], BF16, s1)
            ws_s = p.sb("ws_s", [128, 8, 128], BF16, s1)
            p.op("sp", I("dma_start", out=ws_f[:], in_=wsT.rearrange("g j i -> j g i")), writes=["ws_f"], dkey="c6")
            trb = tril_sb[:, :].unsqueeze(1).broadcast_to([128, 8, 128])
            p.op("dve", I("tensor_tensor", out=ws_p[:], in0=ws_f[:], in1=trb, op=ALU.mult), reads=["ws_f", "tril"], writes=["ws_p"])
            ws_f2 = p.sb("ws_f2", [128, 8, 128], F32, s1)
            p.op("pool", I("memset", ap=ws_f2[:], constant=0.0), writes=["ws_f2"])

            def blk_dma(e, ws_f2=ws_f2):
                r = []
                for b in range(16):
                    r.append(e.dma_start(out=ws_f2[b * 8:(b + 1) * 8, :, b * 8:(b + 1) * 8],
                                         in_=wsT.rearrange("g j i -> j g i")[0:8, :, 0:8],
                                         allow_slow_non_contiguous=True))
                return r
            p.op("sp", blk_dma, reads=[], writes=["ws_f2"], dkey="c7", ndma=16)
            p.op("dve", I("tensor_tensor", out=ws_s[:], in0=ws_f2[:], in1=trb, op=ALU.mult), reads=["ws_f2", "tril"], writes=["ws_s"])
            bs_f = p.sb("bs_f", [8, 2, 128], F32, s1); bs_b = p.sb("bs_b", [8, 2, 128], BF16, s1)
            gind_f = p.sb("gind_f", [8, 512], F32, s1); gind = p.sb("gind", [8, 512], BF16, s1)
            p.op("sp", I("dma_start", out=bs_f[:, 0, :], in_=bs8), writes=["bs_f0"], dkey="c8")
            p.op("sp", I("dma_start", out=bs_f[:, 1, :], in_=bs8s), writes=["bs_f1"], dkey="c9")
            p.op("sp", I("dma_start", out=gind_f[:], in_=gindd), writes=["gind_f"], dkey="c10")
            p.op("dve", I("tensor_copy", out=bs_b[:], in_=bs_f[:]), reads=["bs_f0", "bs_f1"], writes=["bs_b"])
            p.op("dve", I("tensor_copy", out=gind[:], in_=gind_f[:]), reads=["gind_f"], writes=["gind"])

            xbuf = Rot(p, "xb", 2, [128, 1024], F32, s1)
            junk = Rot(p, "junk", 2, [128, 1024], BF16, s1)
            xn = Rot(p, "xn", 2, [128, 1024], BF16, s1)
            hT = Rot(p, "hT", 2, [128, 8, 128], BF16, s1)
            st4 = Rot(p, "st4", 4, [128, 8], F32, s1)
            zkv = Rot(p, "zkv", 2, [128, 512], F32, s1)
            zkvb = Rot(p, "zkvb", 2, [128, 512], BF16, s1)
            zwgb = Rot(p, "zwgb", 2, [128, 256], BF16, s1)
            zwg = Rot(p, "zwg", 2, [128, 280], F32, s1)
            qb = Rot(p, "qb", 2, [128, 512], BF16, s1)
            ub = Rot(p, "ub", 2, [128, 512], BF16, s1)
            vg = Rot(p, "vg", 2, [128, 512], F32, s1)
            vn = Rot(p, "vn", 2, [128, 512], F32, s1)
            vnb = Rot(p, "vnb", 2, [128, 512], BF16, s1)
            ob = Rot(p, "ob", 2, [128, 512], BF16, s1)
            bst = Rot(p, "bst", 2, [128, 6], F32, s1)

            def proj_tile(kind, ti):
                src = {"ctx": xc, "own": xo, "smp": xs}[kind]
                r0 = ti * 128
                xt, xr = xbuf.get()
                p.op("sp", I("dma_start", out=xt[:], in_=src[r0:r0 + 128, :]), writes=[xr], dkey=xr)
                jk, jr = junk.get(); s4, s4r = st4.get()
                p.op("act", I("activation", out=jk[:], in_=xt[:], func=AF.Square, accum_out=s4[:, 0:1]), reads=[xr], writes=[jr, (s4r, 0)])
                p.op("act", I("activation", out=s4[:, 1:2], in_=s4[:, 0:1], func=AF.Sqrt, scale=1.0 / 1024, bias=EPS), reads=[(s4r, 0)], writes=[(s4r, 1)])
                p.op("dve", I("reciprocal", out=s4[:, 2:3], in_=s4[:, 1:2]), reads=[(s4r, 1)], writes=[(s4r, 2)])
                xnt, xnr = xn.get()
                p.op("act", I("activation", out=xnt[:], in_=xt[:], func=AF.Copy, scale=s4[:, 2:3]), reads=[xr, (s4r, 2)], writes=[xnr])
                pt_, ptr = psg()
                ptb = pt_[:].bitcast(BF16)
                p.op("pe", TR([(ptb[:, k * 128:(k + 1) * 128], xnt[:, k * 128:(k + 1) * 128], ident[:]) for k in range(8)]),
                     reads=[xnr, "ident"], writes=[ptr])
                hTt, hTr = hT.get()
                p.op("dve", I("tensor_tensor", out=hTt[:], in0=ptb.rearrange("p (k t) -> p k t", k=8),
                              in1=g1T_sb[:, :].unsqueeze(2).broadcast_to([128, 8, 128]), op=ALU.mult),
                     reads=[ptr, "g1T"], writes=[hTr])

                def zgroup(c0, n):
                    z, zr = psa()
                    p.op("pe", MM([(z[:, 0:n], hTt[:, k, :], win_sb[:, k, c0:c0 + n], k == 0) for k in range(8)]),
                         reads=[hTr] + win_res, writes=[zr])
                    return z, zr

                col = (ti if kind == "ctx" else 16 + ti) * 128
                z, zr = zgroup(C_KV, NKV)
                zk, zkr = zkv.get()
                p.op("act", I("activation", out=zk[:], in_=z[:, :], func=AF.Copy), reads=[zr], writes=[zkr])
                if kind == "own":
                    outs.append(p.op("sp", I("dma_start", out=kvo[r0:r0 + 128, :], in_=zk[:]), reads=[zkr], dkey=("okv", ti % 2)))
                elif kind == "smp":
                    outs.append(p.op("sp", I("dma_start", out=kvs[:, :], in_=zk[:]), reads=[zkr], dkey="okvs"))
                z, zr = zgroup(C_WG, NWG)
                zw, zwr = zwg.get()
                p.op("act", I("activation", out=zw[:, 0:256], in_=z[:, 0:256], func=AF.Copy), reads=[zr], writes=[(zwr, 0)])
                if kind == "own" and ti >= 12:
                    outs.append(p.op("sp", I("dma_start", out=wino[(ti - 12) * 128:(ti - 11) * 128, :], in_=zw[:, 0:256]),
                                     reads=[(zwr, 0)], dkey=("owin", ti % 2)))
                if kind == "smp":
                    def wnew(e):
                        return [e.dma_start(out=wins[b, 504:512, :], in_=zw[b * 8:(b + 1) * 8, 0:256]) for b in range(16)]
                    outs.append(p.op("sp", wnew, reads=[(zwr, 0)], dkey="owins", ndma=16))
                zkbt, zkbr = zkvb.get()
                p.op("pool", I("tensor_copy", out=zkbt[:], in_=zk[:]), reads=[zkr], writes=[zkbr])
                zwbt, zwbr = zwgb.get()
                p.op("pool", I("tensor_copy", out=zwbt[:], in_=zw[:, 0:256]), reads=[(zwr, 0)], writes=[zwbr])
                if kind != "smp":
                    L = ti if kind == "ctx" else 16 + ti
                    pt3, pt3r = psg()
                    pt3b = pt3[:].bitcast(BF16)
                    p.op("pe", TR([(pt3b[:, 0:128], zkbt[:, 0:128], ident[:]), (pt3b[:, 128:256], zkbt[:, 128:256], ident[:]),
                                   (pt3b[:, 256:384], zkbt[:, 256:384], ident[:]), (pt3b[:, 384:512], zwbt[:, 0:128], ident[:])]),
                         reads=[zkbr, zwbr, "ident"], writes=[pt3r])
                    p.op("dve", I("tensor_copy", out=KT[:, :, col:col + 128], in_=pt3b[:, 0:512].rearrange("p (k t) -> p k t", k=4)),
                         reads=[pt3r], writes=[("KT", L)])
                    p.op("pool", I("tensor_copy", out=vsaug[:, L, :, 0:64], in_=zkbt[:, 384:512].rearrange("p (k d) -> p k d", k=2)),
                         reads=[zkbr], writes=[("vsaug", L)])
                    p.op("pool", I("tensor_copy", out=vwaug[:, L, :, 0:64], in_=zwbt[:, 128:256].rearrange("p (k d) -> p k d", k=2)),
                         reads=[zwbr], writes=[("vwaug", L)])
                if kind == "ctx":
                    return
                gi = 16 if kind == "smp" else ti
                p.op("act", I("activation", out=gates_sb[:, gi, :], in_=z[:, 256:280], func=AF.Sigmoid), reads=[zr], writes=[("gates", gi)])
                z, zr = zgroup(C_Q, NQ)
                qbt, qbr = qb.get()
                p.op("act", I("activation", out=qbt[:], in_=z[:, :], func=AF.Copy, scale=0.125), reads=[zr], writes=[qbr])
                pt4, pt4r = psg()
                pt4b = pt4[:].bitcast(BF16)
                p.op("pe", TR([(pt4b[:, k * 128:(k + 1) * 128], qbt[:, k * 128:(k + 1) * 128], ident[:]) for k in range(4)]),
                     reads=[qbr, "ident"], writes=[pt4r])
                if kind == "own":
                    p.op("dve", I("tensor_copy", out=qT[:, :, ti * 128:(ti + 1) * 128], in_=pt4b[:, 0:512].rearrange("p (k t) -> p k t", k=4)),
                         reads=[pt4r], writes=[("qT", ti)])
                else:
                    for k in range(2):
                        hs = slice(k * 64, (k + 1) * 64)
                        for g in range(4):
                            p.op("dve", I("tensor_copy", out=Qbd[hs, :, k * 32 + g * 8:k * 32 + g * 8 + 8],
                                          in_=pt4b[hs, g * 128:(g + 1) * 128].rearrange("p (b q) -> p b q", q=8)),
                                 reads=[pt4r, "Qbd"], writes=["Qbd"])
                    pt5, pt5r = psg()
                    pt5b = pt5[:].bitcast(BF16)
                    p.op("pe", TR([(pt5b[:, 0:128], zkbt[:, 256:384], ident[:]), (pt5b[:, 128:256], zwbt[:, 0:128], ident[:])]),
                         reads=[zkbr, zwbr, "ident"], writes=[pt5r])
                    p.op("dve", I("tensor_copy", out=KnT[:], in_=pt5b[:, 0:256].rearrange("p (k t) -> p k t", k=2)), reads=[pt5r], writes=["KnT"])
                z, zr = zgroup(C_U, NU)
                ut, ur = ub.get()
                p.op("act", I("activation", out=ut[:], in_=z[:, :], func=AF.Gelu_apprx_tanh), reads=[zr], writes=[ur])
                z, zr = zgroup(C_V, NV)
                vgt, vgr = vg.get()
                p.op("act", I("activation", out=vgt[:], in_=z[:, :], func=AF.Gelu_apprx_tanh), reads=[zr], writes=[vgr])
                b6, b6r = bst.get()
                p.op("dve", I("bn_stats", out=b6[:, 0:6], in_=vgt[:]), reads=[vgr], writes=[(b6r, 0)])
                p.op("dve", I("bn_aggr", out=s4[:, 3:5], in_=b6[:, 0:6]), reads=[(b6r, 0)], writes=[(s4r, 3)])
                p.op("act", I("activation", out=s4[:, 5:6], in_=s4[:, 4:5], func=AF.Sqrt, scale=1.0, bias=EPS), reads=[(s4r, 3)], writes=[(s4r, 5)])
                p.op("dve", I("reciprocal", out=s4[:, 6:7], in_=s4[:, 5:6]), reads=[(s4r, 5)], writes=[(s4r, 6)])
                vnt, vnr = vn.get()
                p.op("dve", I("tensor_scalar", out=vnt[:], in0=vgt[:], scalar1=s4[:, 3:4], scalar2=s4[:, 6:7], op0=ALU.subtract, op1=ALU.mult),
                     reads=[vgr, (s4r, 3), (s4r, 6)], writes=[vnr])
                p.op("pool", I("tensor_tensor", out=vnt[:], in0=vnt[:], in1=lvg_sb[:], op=ALU.mult), reads=[vnr, "lvg"], writes=[vnr])
                p.op("pool", I("tensor_tensor", out=vnt[:], in0=vnt[:], in1=lvb_sb[:], op=ALU.add), reads=[vnr, "lvb"], writes=[vnr])
                if kind == "smp":
                    outs.append(p.op("sp", I("dma_start", out=vso[:, :], in_=vnt[:]), reads=[vnr], dkey="ovs"))
                vbt, vbr = vnb.get()
                p.op("pool", I("tensor_copy", out=vbt[:], in_=vnt[:]), reads=[vnr], writes=[vbr])
                sp_, spr = psg()
                wsx = ws_s if kind == "smp" else ws_p
                bi = 1 if kind == "smp" else 0
                lst = [(sp_[:, g * 64:(g + 1) * 64], wsx[:, g, :], vbt[:, g * 64:(g + 1) * 64], g == 0) for g in range(8)]
                lst.append((sp_[:, :], bs_b[:, bi, :], gind[:, :], False))
                p.op("pe", MM(lst), reads=[vbr, "ws_p", "ws_s", "bs_b", "gind"], writes=[spr])
                obt, obr = ob.get()
                p.op("dve", I("tensor_tensor", out=obt[:], in0=sp_[:, :], in1=ut[:], op=ALU.mult), reads=[spr, ur], writes=[obr])
                pt2, pt2r = psg()
                pt2b = pt2[:].bitcast(BF16)
                p.op("pe", TR([(pt2b[:, k * 128:(k + 1) * 128], obt[:, k * 128:(k + 1) * 128], ident[:]) for k in range(4)]),
                     reads=[obr, "ident"], writes=[pt2r])
                mcol = (2048 if kind == "smp" else ti * 128)
                p.op("act", I("activation", out=mixT[:, 4:8, mcol:mcol + 128], in_=pt2b[:, 0:512].rearrange("p (k t) -> p k t", k=4), func=AF.Copy),
                     reads=[pt2r], writes=[("mixT", "b", mcol)])

            proj_tile("smp", 0)
            for ti in range(16):
                proj_tile("ctx", ti)
            for ti in range(16):
                proj_tile("own", ti)
            outs.append(p.op("sp", I("dma_start", out=wins[:, 0:504, :], in_=swin[:, 8:512, :]), dkey="owins2"))
            p.barrier()

        if stage < 2:
            p.op("pool", I("memset", ap=mixT[:, 0:4, :], constant=0.0), writes=[("mixT", "a")])
        elif stage < 3:
            p.op("pool", I("memset", ap=mixT[:, 0:4, 2048:2176], constant=0.0), writes=[("mixT", "a")])

        if stage >= 2:
          with contextlib.ExitStack() as s2:
            W1r = p.sb("W1r", [128, 2, 32, 128], BF16, s2)
            W2c = p.sb("W2c", [128, 2, 16, 128], BF16, s2)
            for stq in range(2):
                v1 = cw1[stq].rearrange("(s d) h -> d s h", d=64)
                for hf in range(2):
                    p.op("pool", I("dma_start", out=W1r[hf * 64:(hf + 1) * 64, stq, :, :], in_=v1), writes=[("W1r", stq, hf)], dkey=("w1r", hf))
                p.op("pool", I("dma_start", out=W2c[:, stq, :, :], in_=cw1[stq].rearrange("(c p) h -> p c h", p=128)), writes=[("W2c", stq)], dkey=("w2c", stq))
            W1r_res = [("W1r", a_, b_) for a_ in range(2) for b_ in range(2)]
            posc = p.sb("posc", [128, 2, 16], BF16, s2)
            p.op("pool", I("dma_start", out=posc[:], in_=cposd), writes=["posc"], dkey="c20")
            b1T = p.sb("b1T", [128, 2], F32, s2)
            p.op("sp", I("dma_start", out=b1T[:], in_=cb1d), writes=["b1T"], dkey="c21")
            w2pad = p.sb("w2pad", [128, 2, 2, 128], BF16, s2)
            p.op("dve", I("memset", ap=w2pad[:], constant=0.0), writes=["w2pad"])

            def w2dma(e, w2pad=w2pad):
                r = []
                for stq in range(2):
                    for va in range(2):
                        r.append(e.dma_start(out=w2pad[:, stq, va, va * 64:(va + 1) * 64], in_=cw2[stq]))
                return r
            p.op("pool", w2dma, writes=["w2pad"], dkey="c22", ndma=4)
            b2k2 = p.sb("b2k2", [128, 1], F32, s2)
            p.op("sp", I("dma_start", out=b2k2[:], in_=cb2kd), writes=["b2k2"], dkey="c23")
            b2vB = p.sb("b2vB", [128, 128], F32, s2)
            p.op("sp", I("dma_start", out=b2vB[:], in_=cb2vd), writes=["b2vB"], dkey="c24")
            kcT = p.sb("kcT", [128, 256], BF16, s2)
            vcaug = p.sb("vcaug", [128, 2, 2, 128], BF16, s2)
            p.op("dve", I("memset", ap=kcT[:], constant=0.0), writes=["kcT"])
            p.op("dve", I("memset", ap=vcaug[:], constant=0.0), writes=["vcaug"])

            def covdma(e, vcaug=vcaug):
                r = []
                for c2 in range(2):
                    for k in range(2):
                        r.append(e.dma_start(out=vcaug[:, c2, k, 64:128], in_=coverd[c2]))
                return r
            p.op("pool", covdma, writes=["vcaug"], dkey="c25", ndma=4)
            b1tot = p.sb("b1tot", [128, 2], F32, s2)
            z, zr = psg()
            lst = []
            for stq in range(2):
                for c in range(16):
                    lst.append((z[:, stq:stq + 1], W2c[:, stq, c, :], posc[:, stq, c:c + 1], (stq == 0 and c == 0)))
            p.op("pe", MM(lst), reads=[("W2c", 0), ("W2c", 1), "posc"], writes=[zr])
            p.op("dve", I("tensor_tensor", out=b1tot[:], in0=z[:, 0:2], in1=b1T[:], op=ALU.add), reads=[zr, "b1T"], writes=["b1tot"])
            ghT = p.sb("ghT", [128, 2, 2, 256], BF16, s2)
            KT_res = [("KT", L) for L in range(32)]
            for stq in range(2):
                for k in range(2):
                    z, zr = psg()
                    hs = slice(k * 64, (k + 1) * 64)
                    p.op("pe", MM([(z[:, 0:255], W1r[hs, stq, s_, :], KT[hs, stq, s_:s_ + 16 * 254 + 1:16], s_ == 0) for s_ in range(32)]),
                         reads=W1r_res + KT_res, writes=[zr])
                    p.op("act", I("activation", out=ghT[:, stq, k, 0:255], in_=z[:, 0:255], func=AF.Gelu_apprx_tanh, bias=b1tot[:, stq:stq + 1]),
                         reads=[zr, "b1tot"], writes=[("ghT", stq, k)])
            z, zr = psg()
            p.op("pe", MM([(z[:, 0:255], w2pad[:, 0, 0, :], ghT[:, 0, 0, 0:255], True), (z[:, 0:255], w2pad[:, 0, 1, :], ghT[:, 0, 1, 0:255], False)]),
                 reads=["w2pad", ("ghT", 0, 0), ("ghT", 0, 1)], writes=[zr])
            p.op("act", I("activation", out=kcT[:, 0:255], in_=z[:, 0:255], func=AF.Identity, bias=b2k2[:, 0:1]), reads=[zr, "b2k2"], writes=["kcT"])
            for c2 in range(2):
                nb = 128 if c2 == 0 else 127
                z, zr = psg()
                p.op("pe", MM([(z[0:nb, k * 64:(k + 1) * 64], ghT[:, 1, k, c2 * 128:c2 * 128 + nb], w2pad[:, 1, 0, 0:64], k == 0) for k in range(2)]),
                     reads=["w2pad", ("ghT", 1, 0), ("ghT", 1, 1)], writes=[zr])
                p.op("dve", I("tensor_tensor", out=vcaug[0:nb, c2, :, 0:64], in0=z[0:nb, 0:128].rearrange("p (k d) -> p k d", k=2),
                              in1=b2vB[0:nb, :].rearrange("p (k d) -> p k d", k=2), op=ALU.add),
                     reads=[zr, "b2vB", "vcaug"], writes=["vcaug"])

            cmpm = p.sb("cmpm", [128, 8, 512], BF16, s2)
            bandm = p.sb("bandm", [128, 8, 512], BF16, s2)
            negc = p.sb("negc", [128, 4, 512], BF16, s2)
            Eexp = p.sb("Eexp", [64, 32, 128], BF16, s2)
            for c in range(8):
                p.op("pool", I("dma_start", out=cmpm[:, c, :], in_=cmpmd[c]), writes=[("cmpm", c)], dkey=("cm", c % 2))
                p.op("pool", I("dma_start", out=bandm[:, c, :], in_=bandmd[c]), writes=[("bandm", c)], dkey=("bm", c % 2))
            for c in range(4):
                p.op("pool", I("dma_start", out=negc[:, c, :], in_=negcd[c]), writes=[("negc", c)], dkey=("nm", c % 2))
                p.op("pool", I("dma_start", out=Eexp[:, c * 8:(c + 1) * 8, :], in_=Ed[:, c * 8:(c + 1) * 8, :]), writes=[("E", c)], dkey=("em", c % 2))
            selb_sb = p.sb("selb_sb", [128, 16, 64], F32, s2)
            vis_sb = p.sb("vis_sb", [128, 16, 64], F32, s2)
            ctxb = p.sb("ctxb", [128, 2], F32, s2)
            p.op("sp", I("dma_start", out=selb_sb[:], in_=selbd), writes=["selb"], dkey="c26")
            p.op("sp", I("dma_start", out=vis_sb[:], in_=visd), writes=["vis"], dkey="c27")
            p.op("sp", I("dma_start", out=ctxb[:], in_=ctxbd), writes=["ctxb"], dkey="c28")
            pT = Rot(p, "pT", 4, [128, 512], BF16, s2)
            pTm = Rot(p, "pTm", 5, [128, 512], BF16, s2)
            maskb = Rot(p, "maskb", 2, [128, 512], BF16, s2)
            pcm = p.sb("pcm", [128, 8, 512], BF16, s2)
            selT = Rot(p, "selT", 2, [64, 512], BF16, s2)
            selbf = Rot(p, "selbf", 2, [128, 64], BF16, s2)
            sc_r = Rot(p, "sc", 2, [128, 64], F32, s2)
            scr_r = Rot(p, "scr", 2, [128, 64], F32, s2)
            sm = Rot(p, "sm", 4, [128, 32], F32, s2)
            oacc = Rot(p, "oacc", 8, [128, 4, 64], F32, s2)
            otmp = Rot(p, "otmp", 2, [128, 4, 64], F32, s2)
            oa_tok = p.sb("oa_tok", [128, 4, 512], BF16, s2)
            mres = [("cmpm", c) for c in range(8)] + [("bandm", c) for c in range(8)] + [("negc", c) for c in range(4)] + [("E", c) for c in range(4)]

            def sT_exp(kstream, kcol, k, g, qt, use_ctx):
                hs = slice(k * 64, (k + 1) * 64)
                z, zr = psg()
                if kstream is None:
                    lhsT = kcT[hs, kcol:kcol + 128]
                    rd = ["kcT"]
                else:
                    lhsT = KT[hs, kstream, kcol:kcol + 128]
                    rd = [("KT", kcol // 128)]
                p.op("pe", MM([(z[:, :], lhsT, qT[hs, g, qt * 512:(qt + 1) * 512], True)]),
                     reads=rd + [("qT", qt * 4 + j) for j in range(4)], writes=[zr])
                t, tr = pT.get()
                bi = 0 if use_ctx else 1
                p.op("act", I("activation", out=t[:], in_=z[:, :], func=AF.Exp, bias=ctxb[:, bi:bi + 1]), reads=[zr, "ctxb"], writes=[tr])
                return t, tr

            def finish_branch(obanks, br, k, qt, accs, width, first):
                for sub in range(4):
                    ti = qt * 4 + sub
                    o, orr = obanks[sub]
                    o3 = o[:, 0:4 * width].rearrange("p (g w) -> p g w", g=4)
                    m, mr = sm.get()
                    if width == 128:
                        p.op("dve", I("tensor_reduce", out=m[:, 0:4], in_=o3[:, :, 64:128], axis=AX.X, op=ALU.add), reads=[orr], writes=[(mr, 0)])
                    else:
                        p.op("dve", I("tensor_copy", out=m[:, 0:4], in_=o3[:, :, 64]), reads=[orr], writes=[(mr, 0)])
                    p.op("dve", I("tensor_scalar", out=m[:, 12:16], in0=m[:, 0:4], scalar1=1e-30, scalar2=None, op0=ALU.max), reads=[(mr, 0)], writes=[(mr, 5)])
                    p.op("dve", I("reciprocal", out=m[:, 4:8], in_=m[:, 12:16]), reads=[(mr, 5)], writes=[(mr, 1)])
                    gc = br * 8 + k * 4
                    p.op("dve", I("tensor_tensor", out=m[:, 8:12], in0=m[:, 4:8], in1=gates_sb[:, ti, gc:gc + 4], op=ALU.mult),
                         reads=[(mr, 1), ("gates", ti)], writes=[(mr, 2)])
                    coefb = m[:, 8:12].unsqueeze(2).broadcast_to([128, 4, 64])
                    a, ar = accs[sub]
                    if first:
                        p.op("dve", I("tensor_tensor", out=a[:], in0=o3[:, :, 0:64], in1=coefb, op=ALU.mult), reads=[orr, (mr, 2)], writes=[ar])
                    else:
                        tt, ttr = otmp.get()
                        p.op("dve", I("tensor_tensor", out=tt[:], in0=o3[:, :, 0:64], in1=coefb, op=ALU.mult), reads=[orr, (mr, 2)], writes=[ttr])
                        p.op("pool", I("tensor_tensor", out=a[:], in0=a[:], in1=tt[:], op=ALU.add), reads=[ttr, ar], writes=[ar])
                    if width == 128:
                        sc, scr = sc_r.get()
                        for g in range(4):
                            in1 = selb_sb[:, ti, :] if g == 0 else sc[:]
                            p.op("dve", I("scalar_tensor_tensor", out=sc[:], in0=o3[:, g, 64:128], scalar=m[:, 4 + g:5 + g], in1=in1, op0=ALU.mult, op1=ALU.add),
                                 reads=[orr, (mr, 1), "selb", scr], writes=[scr])
                        p.op("dve", I("max", out=m[:, 16:24], in_=sc[:]), reads=[scr], writes=[(mr, 3)])
                        s2_, s2r = scr_r.get()
                        p.op("dve", I("match_replace", out=s2_[:], in_to_replace=m[:, 16:24], in_values=sc[:], imm_value=-3.0e38), reads=[scr, (mr, 3)], writes=[s2r])
                        p.op("dve", I("max", out=m[:, 24:32], in_=s2_[:]), reads=[s2r], writes=[(mr, 4)])
                        sb_, sbr = selbf.get()
                        p.op("dve", I("scalar_tensor_tensor", out=sb_[:], in0=sc[:], scalar=m[:, 31:32], in1=vis_sb[:, ti, :], op0=ALU.is_ge, op1=ALU.mult),
                             reads=[scr, (mr, 4), "vis"], writes=[sbr])
                        z, zr = psg()
                        zb = z[:].bitcast(BF16)
                        p.op("pe", TR([(zb[0:64, 0:128], sb_[:, :], ident[:])]), reads=[sbr, "ident"], writes=[zr])
                        p.op("act", I("activation", out=cur_selT[0][0:64, sub * 128:(sub + 1) * 128], in_=zb[0:64, 0:128], func=AF.Copy),
                             reads=[zr], writes=[(cur_selT[1], sub)])

            cur_selT = [None, None]
            for qt in range(4):
                for k in range(2):
                    hs = slice(k * 64, (k + 1) * 64)
                    accs = [oacc.get() for _ in range(4)]
                    cur_selT[0], cur_selT[1] = selT.get()
                    for c2 in range(2):
                        for g in range(4):
                            t, tr = sT_exp(None, c2 * 128, k, g, qt, c2 == 0)
                            p.op("pool", I("tensor_tensor", out=pcm[:, c2 * 4 + g, :], in0=t[:], in1=cmpm[:, qt * 2 + c2, :], op=ALU.mult),
                                 reads=[tr] + mres, writes=[("pcm", c2 * 4 + g)])
                    ob = [psa() for _ in range(4)]
                    for sub in range(4):
                        o, orr = ob[sub]
                        lst = []
                        for g in range(4):
                            for c2 in range(2):
                                lst.append((o[:, g * 128:(g + 1) * 128], pcm[:, c2 * 4 + g, sub * 128:(sub + 1) * 128], vcaug[:, c2, k, :], (g == 0 and c2 == 0)))
                        p.op("pe", MM(lst), reads=[("pcm", j) for j in range(8)] + ["vcaug"], writes=[orr])
                    finish_branch(ob, 0, k, qt, accs, 128, True)
                    nch = 16 + 4 * (qt + 1)
                    ob = [psa() for _ in range(4)]
                    obr = [r_ for (_, r_) in ob]
                    pend = []
                    LAG = 2

                    def pv_slc(it, ob=ob, obr=obr, k=k):
                        tm, tmr, c, g = it
                        p.op("pe", MM([(ob[sub][0][:, g * 65:(g + 1) * 65], tm[:, sub * 128:(sub + 1) * 128], vsaug[:, c, k, :], (c == 0 and g == 0))
                                       for sub in range(4)]),
                             reads=[tmr, ("vsaug", c), "vs1"], writes=obr)
                    for c in range(nch):
                        z, zr = psg()
                        d = c - (16 + 4 * qt)
                        lst = [(z[:, :], Eexp[:, c, :], cur_selT[0][:, :], True)]
                        if d >= 0:
                            lst.append((z[:, :], ident[:], negc[:, d, :], False))
                        p.op("pe", MM(lst), reads=[(cur_selT[1], j) for j in range(4)] + mres + ["ident"], writes=[zr])
                        mb, mbr = maskb.get()
                        p.op("dve", I("tensor_scalar", out=mb[:], in0=z[:, :], scalar1=0.0, scalar2=None, op0=ALU.max), reads=[zr], writes=[mbr])
                        for g in range(4):
                            t, tr = sT_exp(2, c * 128, k, g, qt, c < 16)
                            tm, tmr = pTm.get()
                            p.op("dve" if g % 2 == 0 else "pool", I("tensor_tensor", out=tm[:], in0=t[:], in1=mb[:], op=ALU.mult), reads=[tr, mbr], writes=[tmr])
                            pend.append((tm, tmr, c, g))
                            if len(pend) > LAG:
                                pv_slc(pend.pop(0))
                    while pend:
                        pv_slc(pend.pop(0))
                    finish_branch(ob, 1, k, qt, accs, 65, False)
                    ob = [psa() for _ in range(4)]
                    obr = [r_ for (_, r_) in ob]
                    pend = []

                    def pv_win(it, ob=ob, obr=obr, k=k):
                        tm, tmr, c, g, kc_ = it
                        p.op("pe", MM([(ob[sub][0][:, g * 65:(g + 1) * 65], tm[:, sub * 128:(sub + 1) * 128], vwaug[:, kc_, k, :], (c == 0 and g == 0))
                                       for sub in range(4)]),
                             reads=[tmr, ("vwaug", kc_), "vw1"], writes=obr)
                    for c in range(8):
                        kc_ = 16 + 4 * qt - 4 + c
                        for g in range(4):
                            t, tr = sT_exp(3, kc_ * 128, k, g, qt, kc_ < 16)
                            tm, tmr = pTm.get()
                            p.op("dve" if g % 2 == 0 else "pool", I("tensor_tensor", out=tm[:], in0=t[:], in1=bandm[:, c, :], op=ALU.mult), reads=[tr] + mres, writes=[tmr])
                            pend.append((tm, tmr, c, g, kc_))
                            if len(pend) > LAG:
                                pv_win(pend.pop(0))
                    while pend:
                        pv_win(pend.pop(0))
                    finish_branch(ob, 2, k, qt, accs, 65, False)
                    for sub in range(4):
                        a, ar = accs[sub]
                        p.op("pool", I("tensor_copy", out=oa_tok[:, sub, k * 256:(k + 1) * 256], in_=a[:].rearrange("p g d -> p (g d)")),
                             reads=[ar], writes=[("oa_tok", sub, k)])
                for sub in range(4):
                    ti = qt * 4 + sub
                    z, zr = psg()
                    zb = z[:].bitcast(BF16)
                    p.op("pe", TR([(zb[:, j * 128:(j + 1) * 128], oa_tok[:, sub, j * 128:(j + 1) * 128], ident[:]) for j in range(4)]),
                         reads=[("oa_tok", sub, 0), ("oa_tok", sub, 1), "ident"], writes=[zr])
                    p.op("act", I("activation", out=mixT[:, 0:4, ti * 128:(ti + 1) * 128], in_=zb[:, 0:512].rearrange("p (k t) -> p k t", k=4), func=AF.Copy),
                         reads=[zr], writes=[("mixT", "a", ti * 128)])
            p.barrier()

        sP.close()
        import os as _os0
        if stage >= 3 and _os0.environ.get('DBG_NOS') != '1':
          with contextlib.ExitStack() as sS:
            idx_i = p.sb("idx_i", [128, 512], I32, sS); idx_f = p.sb("idx_f", [128, 512], F32, sS); idx = p.sb("idx", [128, 512], I32, sS)
            pcol = p.sb("pcol", [128, 1], F32, sS)
            p.op("sp", I("dma_start", out=idx_i[:], in_=ptxd.rearrange("p b c -> p (b c)")), writes=["idx_i"], dkey="c30")
            p.op("sp", I("dma_start", out=pcol[:], in_=pcold), writes=["pcol"], dkey="c31")
            p.op("dve", I("tensor_copy", out=idx_f[:], in_=idx_i[:]), reads=["idx_i"], writes=["idx_f"])
            p.op("dve", I("tensor_scalar", out=idx_f[:], in0=idx_f[:], scalar1=64.0, scalar2=pcol[:, 0:1], op0=ALU.mult, op1=ALU.add),
                 reads=["idx_f", "pcol"], writes=["idx_f"])
            p.op("dve", I("tensor_copy", out=idx[:], in_=idx_f[:]), reads=["idx_f"], writes=["idx"])
            W1s = p.sb("W1s", [128, 2, 32, 128], BF16, sS)
            for stq in range(2):
                v1 = cw1[stq].rearrange("(s d) h -> d s h", d=64)
                for hf in range(2):
                    p.op("pool", I("dma_start", out=W1s[hf * 64:(hf + 1) * 64, stq, :, :], in_=v1), writes=[("W1s", stq, hf)], dkey=("w1r", hf))
            W1s_res = [("W1s", a_, b_) for a_ in range(2) for b_ in range(2)]
            posd_sb = p.sb("posd_sb", [64, 2, 32], BF16, sS)
            p.op("pool", I("dma_start", out=posd_sb[:], in_=cpos64d), writes=["posd"], dkey="c20")
            b1T = p.sb("b1Ts", [128, 2], F32, sS)
            p.op("sp", I("dma_start", out=b1T[:], in_=cb1d), writes=["b1T"], dkey="c21")
            w2pad = p.sb("w2pads", [128, 2, 2, 128], BF16, sS)
            p.op("dve", I("memset", ap=w2pad[:], constant=0.0), writes=["w2pad"])

            def w2dma_s(e, w2pad=w2pad):
                r = []
                for stq in range(2):
                    for va in range(2):
                        r.append(e.dma_start(out=w2pad[:, stq, va, va * 64:(va + 1) * 64], in_=cw2[stq]))
                return r
            p.op("pool", w2dma_s, writes=["w2pad"], dkey="c22", ndma=4)
            b2k2 = p.sb("b2k2s", [128, 1], F32, sS)
            p.op("sp", I("dma_start", out=b2k2[:], in_=cb2kd), writes=["b2k2"], dkey="c23")
            b2vB = p.sb("b2vBs", [128, 128], F32, sS)
            p.op("sp", I("dma_start", out=b2vB[:], in_=cb2vd), writes=["b2vB"], dkey="c24")
            b1tot = p.sb("b1tots", [128, 2], F32, sS)
            z, zr = psg()
            lst = []
            for stq in range(2):
                for c in range(32):
                    lst.append((z[:, stq:stq + 1], W1s[0:64, stq, c, :], posd_sb[:, stq, c:c + 1], (stq == 0 and c == 0)))
            p.op("pe", MM(lst), reads=W1s_res + ["posd"], writes=[zr])
            p.op("dve", I("tensor_tensor", out=b1tot[:], in0=z[:, 0:2], in1=b1T[:], op=ALU.add), reads=[zr, "b1T"], writes=["b1tot"])
            fbs = p.sb("fbs", [64, 129], F32, sS); sel2 = p.sb("sel2", [64, 64], F32, sS)
            cm8f = p.sb("cm8f", [64, 8], F32, sS); cm8 = p.sb("cm8", [64, 8], BF16, sS)
            wm = p.sb("wm", [64, 512], BF16, sS); selg = p.sb("selg", [24, 3, 4, 128], F32, sS)
            p.op("sp", I("dma_start", out=fbs[:], in_=fbsd), writes=["fbs"], dkey="c32")
            p.op("sp", I("dma_start", out=sel2[:], in_=sel2d), writes=["sel2"], dkey="c33")
            p.op("sp", I("dma_start", out=cm8f[:], in_=cm8d), writes=["cm8f"], dkey="c34")
            p.op("dve", I("tensor_copy", out=cm8[:], in_=cm8f[:]), reads=["cm8f"], writes=["cm8"])
            p.op("pool", I("dma_start", out=wm[:], in_=wmd), writes=["wm"], dkey="c35")
            p.op("sp", I("dma_start", out=selg[:], in_=selgd), writes=["selg"], dkey="c36")
            vcs = p.sb("vcs", [128, 4, 257], BF16, sS)
            p.op("dve", I("memset", ap=vcs[:], constant=0.0), writes=["vcs"])
            p.op("pool", I("dma_start", out=vcs[:, :, 128:257], in_=covsd), writes=["vcs"], reads=["vcs"], dkey="c37")
            Vn = p.sb("Vn", [8, 2, 16, 129], BF16, sS)
            p.op("dve", I("memset", ap=Vn[:], constant=1.0), writes=["Vn"])
            p.op("pool", I("dma_start", out=Vn[:, 0, :, 0:128], in_=kvs.rearrange("(b q) c -> q b c", q=8)[:, :, 384:512]), reads=["Vn"], writes=["Vn"], dkey="c38")
            p.op("pool", I("dma_start", out=Vn[:, 1, :, 0:128], in_=wins[:, 504:512, 128:256].rearrange("b q c -> q b c")), reads=["Vn"], writes=["Vn"], dkey="c39")
            X2T = p.sb("X2T", [128, 2, 2, 8, 512], BF16, sS)
            KsT = p.sb("KsT", [128, 2, 4096], BF16, sS)
            Vs = p.sb("Vs", [128, 32, 2, 129], BF16, sS)
            p.op("pool", I("memset", ap=Vs[:, :, :, 128:129], constant=1.0), writes=["Vs1"])
            G = Rot(p, "G", 6, [128, 1024], BF16, sS)
            gh = p.sb("gh", [128, 2, 2, 512], BF16, sS)
            kcTs = p.sb("kcTs", [128, 512], BF16, sS)
            Pc = Rot(p, "Pc", 2, [64, 512], BF16, sS)
            for t_ in Pc.t:
                p.op("dve", I("memset", ap=t_[:], constant=0.0), writes=[("Pc", Pc.t.index(t_))])
            PcT = Rot(p, "PcT", 2, [128, 4, 64], BF16, sS)
            Pq = Rot(p, "Pq", 3, [64, 512], BF16, sS)
            Pqm = Rot(p, "Pqm", 4, [64, 512], BF16, sS)
            PsT = Rot(p, "PsT", 4, [128, 4, 64], BF16, sS)
            Pn = Rot(p, "Pn", 2, [64, 8], BF16, sS); Pnm = Rot(p, "Pnm", 2, [64, 8], BF16, sS); PnT = Rot(p, "PnT", 2, [8, 64], BF16, sS)
            on_r = Rot(p, "on", 3, [64, 128], BF16, sS)
            smm = Rot(p, "smm", 6, [64, 32], F32, sS)
            Pe32 = Rot(p, "Pe32", 2, [64, 512], F32, sS)
            Pg = Rot(p, "Pg", 2, [128, 4, 16], BF16, sS)
            Pg32 = Rot(p, "Pg32", 2, [128, 4, 16], F32, sS)
            sel16 = Rot(p, "sel16", 2, [16, 129], BF16, sS)
            rep16f = p.sb("rep16f", [16, 64], F32, sS); rep16 = p.sb("rep16", [16, 64], BF16, sS)
            p.op("sp", I("dma_start", out=rep16f[:], in_=rep16d), writes=["rep16f"], dkey="c41")
            p.op("dve", I("tensor_copy", out=rep16[:], in_=rep16f[:]), reads=["rep16f"], writes=["rep16"])
            scs = Rot(p, "scs", 2, [64, 129], F32, sS); scs2 = Rot(p, "scs2", 2, [64, 129], F32, sS)
            sel_r = Rot(p, "sel", 2, [64, 129], BF16, sS)
            SW = Rot(p, "SW", 2, [128, 4, 256], BF16, sS)
            SWv = Rot(p, "SWv", 2, [128, 4, 129], BF16, sS)
            for i_, t_ in enumerate(SWv.t):
                p.op("pool", I("memset", ap=t_[:, :, 128:129], constant=1.0), writes=[("SWv1", i_)])
            KwT = Rot(p, "KwT", 2, [128, 512], BF16, sS)
            OBR = p.sb("OBR", [128, 3, 4, 128], F32, sS)
            id64 = ident[0:64, 0:64]
            G_all = [("G", i_) for i_ in range(6)]

            def to_obr(ont, onr, br, b):
                z, zr = psg()
                zb = z[:].bitcast(BF16)
                p.op("pe", TR([(zb[:, 0:64], ont[:, :], id64)]), reads=[onr, "ident"], writes=[zr])
                for k in range(2):
                    hs = slice(k * 64, (k + 1) * 64)
                    p.op("act", I("activation", out=OBR[hs, br, :, b * 8:(b + 1) * 8], in_=zb[hs, k * 32:(k + 1) * 32].rearrange("p (g q) -> p g q", q=8), func=AF.Copy),
                         reads=[zr], writes=[("OBR", br, b, k)])

            def finish_s(o, orr, br, b, rs_from_cover):
                m, mr = smm.get()
                p.op("dve", I("tensor_scalar", out=m[:, 1:2], in0=o[0:64, 128:129], scalar1=1e-30, scalar2=None, op0=ALU.max), reads=[orr], writes=[(mr, 1)])
                p.op("dve", I("reciprocal", out=m[:, 2:3], in_=m[:, 1:2]), reads=[(mr, 1)], writes=[(mr, 2)])
                ont, onr = on_r.get()
                p.op("dve", I("tensor_scalar", out=ont[:], in0=o[0:64, 0:128], scalar1=m[:, 2:3], scalar2=None, op0=ALU.mult), reads=[orr, (mr, 2)], writes=[onr])
                to_obr(ont, onr, br, b)
                return m, mr

            def new_keys(o, orr, which, b, tag):
                z, zr = psg()
                p.op("pe", MM([(z[0:64, 0:8], Qbd[:, b, :], KnT[:, which, b * 8:(b + 1) * 8], True)]), reads=["Qbd", "KnT"], writes=[zr])
                t, tr = Pn.get()
                p.op("act", I("activation", out=t[:], in_=z[0:64, 0:8], func=AF.Exp), reads=[zr], writes=[tr])
                tm, tmr = Pnm.get()
                p.op("dve", I("tensor_tensor", out=tm[:], in0=t[:], in1=cm8[:], op=ALU.mult), reads=[tr, "cm8"], writes=[tmr])
                z2, z2r = psg()
                z2b = z2[:].bitcast(BF16)
                p.op("pe", TR([(z2b[0:8, 0:64], tm[:, :], id64)]), reads=[tmr, "ident"], writes=[z2r])
                tt, ttr = PnT.get()
                p.op("act", I("activation", out=tt[:], in_=z2b[0:8, 0:64], func=AF.Copy), reads=[z2r], writes=[ttr])
                p.op("pe", MM([(o[0:64, 0:129], tt[:, :], Vn[:, which, b, :], False)]), reads=[ttr, "Vn"], writes=[orr])

            def dk_A(kT_ap, kT_res, mask_fn):
                z, zr = psg()
                p.op("pe", MM([(z[0:64, :], Qbd[:, b_cur[0], :], kT_ap, True)]), reads=["Qbd"] + kT_res, writes=[zr])
                t, tr = Pq.get()
                p.op("act", I("activation", out=t[:], in_=z[0:64, :], func=AF.Exp), reads=[zr], writes=[tr])
                tm, tmr = Pqm.get()
                mask_fn(tm, tmr, t, tr)
                return tm, tmr

            def dk_B(tm, tmr):
                z2, z2r = psg()
                z2b = z2[:].bitcast(BF16)
                p.op("pe", TR([(z2b[:, j * 64:(j + 1) * 64], tm[:, j * 128:(j + 1) * 128], id64) for j in range(4)]), reads=[tmr, "ident"], writes=[z2r])
                tt, ttr = PsT.get()
                p.op("act", I("activation", out=tt[:], in_=z2b[:, 0:256].rearrange("p (j c) -> p j c", j=4), func=AF.Copy), reads=[z2r], writes=[ttr])
                return tt, ttr

            def dk_C(o, orr, tt, ttr, v_fn, v_res, first):
                p.op("pe", MM([(o[0:64, 0:129], tt[:, j, :], v_fn(j), first and j == 0) for j in range(4)]), reads=[ttr] + v_res, writes=[orr])

            def dense_keys(o, orr, kT_ap, kT_res, mask_fn, v_fn, v_res, first):
                tm, tmr = dk_A(kT_ap, kT_res, mask_fn)
                tt, ttr = dk_B(tm, tmr)
                dk_C(o, orr, tt, ttr, v_fn, v_res, first)

            b_cur = [0]
            cache_v = cache
            import os as _os
            SPART = int(_os.environ.get('DBG_SPART', '5'))
            for b in range(int(_os.environ.get('DBG_NB', '16'))):
                b_cur[0] = b
                for c in range(32):
                    Gt, Gr = G.get()
                    p.op("pool", I("indirect_dma_start", out=Gt[:], out_offset=None, in_=cache_v,
                                   in_offset=bass.IndirectOffsetOnAxis(ap=idx[:, b * 32 + c:b * 32 + c + 1], axis=0)),
                         reads=["idx"], writes=[Gr], dkey=Gr)
                    if _os.environ.get('DBG_NOTR') == '1':
                        p.op("pool", I("tensor_copy", out=Vs[:, c, :, 0:128], in_=Gt[:].rearrange("p (t r) -> p t r", t=2)[:, :, 384:512]), reads=[Gr], writes=[("Vs", c)])
                        continue
                    z, zr = psg()
                    zb = z[:].bitcast(BF16)
                    lst = []
                    for stq in range(3):
                        for t2 in range(2):
                            o_ = t2 * 512 + stq * 128
                            lst.append((zb[:, (stq * 2 + t2) * 128:(stq * 2 + t2 + 1) * 128], Gt[:, o_:o_ + 128], ident[:]))
                    p.op("pe", TR(lst), reads=[Gr, "ident"], writes=[zr])
                    for stq in range(2):
                        for t2 in range(2):
                            o_ = (stq * 2 + t2) * 128
                            p.op("act" if t2 == 0 else "dve",
                                 I("activation", out=X2T[:, stq, t2, :, 16 * c:16 * c + 16], in_=zb[:, o_:o_ + 128].rearrange("p (i c) -> p c i", i=16, c=8), func=AF.Copy)
                                 if t2 == 0 else
                                 I("tensor_copy", out=X2T[:, stq, t2, :, 16 * c:16 * c + 16], in_=zb[:, o_:o_ + 128].rearrange("p (i c) -> p c i", i=16, c=8)),
                                 reads=[zr], writes=[("X2T", c, stq, t2)])
                    p.op("dve", I("tensor_copy", out=KsT[:, :, c * 128:(c + 1) * 128], in_=zb[:, 512:768].rearrange("p (k t) -> p k t", k=2)),
                         reads=[zr], writes=[("KsT", c)])
                    p.op("pool", I("tensor_copy", out=Vs[:, c, :, 0:128], in_=Gt[:].rearrange("p (t r) -> p t r", t=2)[:, :, 384:512]), reads=[Gr], writes=[("Vs", c)])
                X2T_res = [("X2T", c, q_, t_) for c in range(32) for q_ in range(2) for t_ in range(2)]
                if SPART < 2:
                    continue
                for stq in range(2):
                    zz = [psg(), psg()]
                    lst = []
                    for s_ in range(32):
                        for k in range(2):
                            hs = slice(k * 64, (k + 1) * 64)
                            cc = s_ // 2
                            rhs_ = X2T[hs, stq, s_ % 2, cc, 0:511] if cc < 8 else X2T[hs, stq, s_ % 2, cc - 8, 1:512]
                            lst.append((zz[k][0][:, 0:511], W1s[hs, stq, s_, :], rhs_, s_ == 0))
                    p.op("pe", MM(lst), reads=W1s_res + X2T_res, writes=[zz[0][1], zz[1][1]])
                    for k in range(2):
                        p.op("act", I("activation", out=gh[:, stq, k, 0:511], in_=zz[k][0][:, 0:511], func=AF.Gelu_apprx_tanh, bias=b1tot[:, stq:stq + 1]),
                             reads=[zz[k][1], "b1tot"], writes=[("gh", stq, k)])
                z, zr = psg()
                p.op("pe", MM([(z[:, 0:511], w2pad[:, 0, 0, :], gh[:, 0, 0, 0:511], True), (z[:, 0:511], w2pad[:, 0, 1, :], gh[:, 0, 1, 0:511], False)]),
                     reads=["w2pad", ("gh", 0, 0), ("gh", 0, 1)], writes=[zr])
                p.op("act", I("activation", out=kcTs[:, 0:511], in_=z[:, 0:511], func=AF.Identity, bias=b2k2[:, 0:1]), reads=[zr, "b2k2"], writes=["kcTs"])
                for c4 in range(4):
                    nb = 128 if c4 < 3 else 127
                    z, zr = psg()
                    p.op("pe", MM([(z[0:nb, k * 64:(k + 1) * 64], gh[:, 1, k, c4 * 128:c4 * 128 + nb], w2pad[:, 1, 0, 0:64], k == 0) for k in range(2)]),
                         reads=["w2pad", ("gh", 1, 0), ("gh", 1, 1)], writes=[zr])
                    p.op("dve", I("tensor_tensor", out=vcs[0:nb, c4, 0:128], in0=z[0:nb, 0:128], in1=b2vB[0:nb, :], op=ALU.add),
                         reads=[zr, "b2vB", "vcs"], writes=[("vcs", c4)])
                if SPART < 3:
                    continue
                z, zr = psg()
                p.op("pe", MM([(z[0:64, 0:511], Qbd[:, b, :], kcTs[:, 0:511], True)]), reads=["Qbd", "kcTs"], writes=[zr])
                m, mr = smm.get()
                pe_, per = Pe32.get()
                p.op("act", I("activation", out=pe_[:, 0:511], in_=z[0:64, 0:511], func=AF.Exp, accum_out=m[:, 0:1]), reads=[zr], writes=[per, (mr, 0)])
                p.op("dve", I("tensor_scalar", out=m[:, 1:2], in0=m[:, 0:1], scalar1=1e-30, scalar2=None, op0=ALU.max), reads=[(mr, 0)], writes=[(mr, 1)])
                p.op("dve", I("reciprocal", out=m[:, 2:3], in_=m[:, 1:2]), reads=[(mr, 1)], writes=[(mr, 2)])
                pc, pcr = Pc.get()
                p.op("dve", I("tensor_scalar", out=pc[:, 0:511], in0=pe_[:, 0:511], scalar1=m[:, 2:3], scalar2=None, op0=ALU.mult), reads=[per, (mr, 2)], writes=[pcr])
                z2, z2r = psg()
                z2b = z2[:].bitcast(BF16)
                p.op("pe", TR([(z2b[:, j * 64:(j + 1) * 64], pc[:, j * 128:(j + 1) * 128], id64) for j in range(4)]), reads=[pcr, "ident"], writes=[z2r])
                pct, pctr = PcT.get()
                p.op("act", I("activation", out=pct[:], in_=z2b[:, 0:256].rearrange("p (j c) -> p j c", j=4), func=AF.Copy), reads=[z2r], writes=[pctr])
                pg32, pg32r = Pg32.get()
                p.op("dve", I("tensor_reduce", out=pg32[:].rearrange("p j (k q) -> p j k q", k=2),
                              in_=pct[:].rearrange("p j (k g q) -> p j k q g", k=2, g=4), axis=AX.X, op=ALU.add), reads=[pctr], writes=[pg32r])
                pg, pgr = Pg.get()
                p.op("dve", I("tensor_copy", out=pg[:], in_=pg32[:]), reads=[pg32r], writes=[pgr])
                oc, ocr = psa()
                p.op("pe", MM([(oc[0:64, 0:128], pct[:, j, :], vcs[:, j, 0:128], j == 0) for j in range(4)]),
                     reads=[pctr, "vcs"] + [("vcs", j) for j in range(4)], writes=[ocr])
                ont, onr = on_r.get()
                p.op("act", I("activation", out=ont[:], in_=oc[0:64, 0:128], func=AF.Copy), reads=[ocr], writes=[onr])
                to_obr(ont, onr, 0, b)
                z, zr = psg()
                p.op("pe", MM([(z[0:16, 0:129], pg[:, j, :], vcs[:, j, 128:257], j == 0) for j in range(4)]), reads=[pgr, "vcs"], writes=[zr])
                sc, scr = scs.get()
                p.op("dve", I("tensor_tensor", out=sc[0:16, :], in0=z[0:16, 0:129], in1=fbs[0:16, :], op=ALU.add), reads=[zr, "fbs"], writes=[scr])
                p.op("dve", I("max", out=m[0:16, 8:16], in_=sc[0:16, :]), reads=[scr], writes=[(mr, 3)])
                sc2, sc2r = scs2.get()
                p.op("dve", I("match_replace", out=sc2[0:16, :], in_to_replace=m[0:16, 8:16], in_values=sc[0:16, :], imm_value=-3.0e38), reads=[scr, (mr, 3)], writes=[sc2r])
                p.op("dve", I("max", out=m[0:16, 16:24], in_=sc2[0:16, :]), reads=[sc2r], writes=[(mr, 4)])
                s16, s16r = sel16.get()
                p.op("dve", I("tensor_scalar", out=s16[:], in0=sc[0:16, :], scalar1=m[0:16, 23:24], scalar2=None, op0=ALU.is_ge), reads=[scr, (mr, 4)], writes=[s16r])
                z, zr = psg()
                p.op("pe", MM([(z[0:64, 0:129], rep16[:, :], s16[:, :], True)]), reads=[s16r, "rep16"], writes=[zr])
                sel, selr = sel_r.get()
                p.op("act", I("activation", out=sel[:], in_=z[0:64, 0:129], func=AF.Copy), reads=[zr], writes=[selr])
                if SPART < 4:
                    continue
                osl, oslr = psa()
                qa, qb_ = [], []
                nC = [0]

                def run_C(it):
                    tt, ttr, t2, mm_ = it
                    dk_C(osl, oslr, tt, ttr, lambda j, t2=t2, mm_=mm_: Vs[:, 4 * mm_ + j, t2, :], [("Vs", 4 * mm_ + j) for j in range(4)] + ["Vs1"], nC[0] == 0)
                    nC[0] += 1

                def run_B(it):
                    tm, tmr, t2, mm_ = it
                    tt, ttr = dk_B(tm, tmr)
                    qb_.append((tt, ttr, t2, mm_))
                    if len(qb_) > 1:
                        run_C(qb_.pop(0))
                for mm_ in range(8):
                    for t2 in range(2):
                        def mask_sel(tm, tmr, t, tr, mm_=mm_):
                            p.op("dve", I("tensor_tensor", out=tm[:].rearrange("p (j r) -> p j r", r=32), in0=t[:].rearrange("p (j r) -> p j r", r=32),
                                          in1=sel[:, 16 * mm_:16 * mm_ + 16].unsqueeze(2).broadcast_to([64, 16, 32]), op=ALU.mult),
                                 reads=[tr, selr], writes=[tmr])
                        tm, tmr = dk_A(KsT[:, t2, mm_ * 512:(mm_ + 1) * 512], [("KsT", 4 * mm_ + j) for j in range(4)], mask_sel)
                        qa.append((tm, tmr, t2, mm_))
                        if len(qa) > 1:
                            run_B(qa.pop(0))
                while qa:
                    run_B(qa.pop(0))
                while qb_:
                    run_C(qb_.pop(0))
                new_keys(osl, oslr, 0, b, "s")
                finish_s(osl, oslr, 1, b, False)
                if SPART < 5:
                    continue
                swt, swr = SW.get()
                p.op("pool", I("dma_start", out=swt[:], in_=swin[b].rearrange("(c p) f -> p c f", p=128)), writes=[swr], dkey=swr)
                z, zr = psg()
                zb = z[:].bitcast(BF16)
                p.op("pe", TR([(zb[:, c * 128:(c + 1) * 128], swt[:, c, 0:128], ident[:]) for c in range(4)]), reads=[swr, "ident"], writes=[zr])
                kw, kwr = KwT.get()
                p.op("act", I("activation", out=kw[:], in_=zb[:, 0:512], func=AF.Copy), reads=[zr], writes=[kwr])
                sv, svr = SWv.get()
                p.op("pool", I("tensor_copy", out=sv[:, :, 0:128], in_=swt[:, :, 128:256]), reads=[swr], writes=[svr])
                ow, owr = psa()

                def mask_win(tm, tmr, t, tr):
                    p.op("dve", I("tensor_tensor", out=tm[:], in0=t[:], in1=wm[:], op=ALU.mult), reads=[tr, "wm"], writes=[tmr])
                dense_keys(ow, owr, kw[:, :], [kwr], mask_win, lambda j, sv=sv: sv[:, j, :], [svr], True)
                new_keys(ow, owr, 1, b, "w")
                finish_s(ow, owr, 2, b, False)
            ghi = p.sb("ghi", [128, 24], BF16, sS); ghi32 = p.sb("ghi32", [128, 24], F32, sS); glo = p.sb("glo", [128, 24], BF16, sS)
            p.op("dve", I("tensor_copy", out=ghi[:], in_=gates_sb[:, 16, :]), reads=[("gates", 16)], writes=["ghi"])
            p.op("dve", I("tensor_copy", out=ghi32[:], in_=ghi[:]), reads=["ghi"], writes=["ghi32"])
            p.op("dve", I("tensor_tensor", out=glo[:], in0=gates_sb[:, 16, :], in1=ghi32[:], op=ALU.subtract), reads=[("gates", 16), "ghi32"], writes=["glo"])
            z, zr = psg()
            zb = z[:].bitcast(BF16)
            p.op("pe", TR([(zb[0:24, 0:128], ghi[:, :], ident[:]), (zb[0:24, 128:256], glo[:, :], ident[:])]), reads=["ghi", "glo", "ident"], writes=[zr])
            gT = p.sb("gT", [24, 2, 128], BF16, sS)
            p.op("act", I("activation", out=gT[:], in_=zb[0:24, 0:256].rearrange("p (a t) -> p a t", a=2), func=AF.Copy), reads=[zr], writes=["gT"])
            selgb = p.sb("selgb", [24, 3, 4, 128], BF16, sS)
            p.op("dve", I("tensor_copy", out=selgb[:], in_=selg[:]), reads=["selg"], writes=["selgb"])
            oaT = p.sb("oaT", [128, 512], F32, sS)
            oat2 = p.sb("oat2", [128, 512], F32, sS)
            obr_res = [("OBR", br, b, k) for br in range(3) for b in range(16) for k in range(2)]
            for br in range(3):
                z, zr = psg()
                lst = []
                for g in range(4):
                    lst.append((z[:, g * 128:(g + 1) * 128], selgb[:, br, g, :], gT[:, 0, :], g == 0))
                    lst.append((z[:, g * 128:(g + 1) * 128], selgb[:, br, g, :], gT[:, 1, :], False))
                p.op("pe", MM(lst), reads=["selgb", "gT"], writes=[zr])
                src = OBR[:, br, :, :].rearrange("p g t -> p (g t)")
                if br == 0:
                    p.op("dve", I("tensor_tensor", out=oaT[:], in0=z[:, :], in1=src, op=ALU.mult), reads=[zr] + obr_res, writes=["oaT"])
                else:
                    p.op("dve", I("tensor_tensor", out=oat2[:], in0=z[:, :], in1=src, op=ALU.mult), reads=[zr] + obr_res, writes=["oat2"])
                    p.op("pool", I("tensor_tensor", out=oaT[:], in0=oaT[:], in1=oat2[:], op=ALU.add), reads=["oaT", "oat2"], writes=["oaT"])
            if _os.environ.get('DBG_NOFIN') != '1':
                p.op("act", I("activation", out=mixT[:, 0:4, 2048:2176], in_=oaT[:].rearrange("p (g t) -> p g t", g=4), func=AF.Copy), reads=["oaT"], writes=[("mixT", "a")])
            p.barrier()
        with contextlib.ExitStack() as s3:
            wout_sb = p.sb("wout_sb", [128, 8, 1024], BF16, s3)
            w_out_v = w_out.rearrange("(kc p) n -> p kc n", p=128)
            for kc in range(8):
                p.op("pool", I("dma_start", out=wout_sb[:, kc, :], in_=w_out_v[:, kc, :]), writes=[("wout", kc)], dkey=("wout", kc % 4))
            wouts_sb = p.sb("wouts_sb", [128, 4, 1024], BF16, s3)

            def wouts_dma(e, wouts_sb=wouts_sb):
                r = []
                for g in range(4):
                    for k in range(2):
                        h_ = 4 * k + g
                        r.append(e.dma_start(out=wouts_sb[k * 64:(k + 1) * 64, g, :], in_=w_out[h_ * 64:(h_ + 1) * 64, :]))
                return r
            p.op("pool", wouts_dma, writes=["wouts"], dkey="c40", ndma=8)
            wout_res = [("wout", kc) for kc in range(8)] + ["wouts"]
            gf_sb = p.sb("gf_sb", [128, 1024], F32, s3)
            p.op("sp", I("dma_start", out=gf_sb[:], in_=gfB), writes=["gf"], dkey="c11")
            w1s = Rot(p, "w1s", 2, [128, 8, 512], BF16, s3)
            w2s = Rot(p, "w2s", 3, [128, 4, 512], BF16, s3)
            fT = p.sb("fT", [128, 32, 512], BF16, s3)
            hnT = p.sb("hnT", [128, 8, 512], BF16, s3)
            h2 = p.sb("h2", [128, 4, 1024], F32, s3)
            xbuf = Rot(p, "xb3", 2, [128, 1024], F32, s3)
            junk = Rot(p, "junk3", 2, [128, 1024], BF16, s3)
            hnb = Rot(p, "hnb", 2, [128, 1024], BF16, s3)
            st4 = Rot(p, "st43", 4, [128, 8], F32, s3)
            rl = Rot(p, "rl", 3, [128, 512], F32, s3)
            yb = Rot(p, "yb", 4, [128, 1024], F32, s3)
            w_ff1_v = w_ff1.rearrange("(kc p) f -> p kc f", p=128)
            w_ff2_v = w_ff2.rearrange("(fc p) n -> p fc n", p=128)

            groups = [[("own", t) for t in range(0, 4)], [("own", t) for t in range(4, 8)],
                      [("own", t) for t in range(8, 12)], [("own", t) for t in range(12, 16)], [("smp", 0)]]
            for grp in groups:
                nt = len(grp)
                ntok = nt * 128
                for si, (kind, ti) in enumerate(grp):
                    src = xo if kind == "own" else xs
                    r0 = ti * 128
                    mcol = 2048 if kind == "smp" else ti * 128
                    xt, xr = xbuf.get()
                    p.op("sp", I("dma_start", out=xt[:], in_=src[r0:r0 + 128, :]), writes=[xr], dkey=xr)
                    for half in range(2):
                        z, zr = psa()
                        p.op("pe", MM([(z[:, :], mixT[:, k, mcol:mcol + 128],
                                        (wouts_sb if (kind == "smp" and k < 4 and stage >= 3) else wout_sb)[:, k, half * 512:(half + 1) * 512], k == 0) for k in range(8)]),
                             reads=wout_res + [("mixT", "a"), ("mixT", "a", mcol), ("mixT", "b", mcol)], writes=[zr])
                        p.op("dve", I("tensor_tensor", out=h2[:, si, half * 512:(half + 1) * 512], in0=z[:, :], in1=xt[:, half * 512:(half + 1) * 512], op=ALU.add),
                             reads=[zr, xr], writes=[("h2", si, half)])
                    jk, jr = junk.get(); s4, s4r = st4.get()
                    p.op("act", I("activation", out=jk[:], in_=h2[:, si, :], func=AF.Square, accum_out=s4[:, 0:1]),
                         reads=[("h2", si, 0), ("h2", si, 1)], writes=[jr, (s4r, 0)])
                    p.op("act", I("activation", out=s4[:, 1:2], in_=s4[:, 0:1], func=AF.Sqrt, scale=1.0 / 1024, bias=EPS), reads=[(s4r, 0)], writes=[(s4r, 1)])
                    p.op("dve", I("reciprocal", out=s4[:, 2:3], in_=s4[:, 1:2]), reads=[(s4r, 1)], writes=[(s4r, 2)])
                    hb, hbr = hnb.get()
                    p.op("act", I("activation", out=hb[:], in_=h2[:, si, :], func=AF.Copy, scale=s4[:, 2:3]),
                         reads=[("h2", si, 0), ("h2", si, 1), (s4r, 2)], writes=[hbr])
                    pt_, ptr = psg()
                    ptb = pt_[:].bitcast(BF16)
                    p.op("pe", TR([(ptb[:, k * 128:(k + 1) * 128], hb[:, k * 128:(k + 1) * 128], ident[:]) for k in range(8)]),
                         reads=[hbr, "ident"], writes=[ptr])
                    p.op("dve", I("tensor_tensor", out=hnT[:, :, si * 128:(si + 1) * 128], in0=ptb.rearrange("p (k t) -> p k t", k=8),
                                  in1=g2T_sb[:, :].unsqueeze(2).broadcast_to([128, 8, 128]), op=ALU.mult),
                         reads=[ptr, "g2T"], writes=[("hnT", si)])
                hn_res = [("hnT", si) for si in range(nt)]
                for c in range(8):
                    w1t, w1r = w1s.get()
                    p.op("pool", I("dma_start", out=w1t[:], in_=w_ff1_v[:, :, c * 512:(c + 1) * 512]), writes=[w1r], dkey=w1r)
                    for f4 in range(4):
                        fc = c * 4 + f4
                        for (t0, tn) in [(0, ntok)]:
                            z, zr = psg()
                            p.op("pe", MM([(z[:, 0:tn], w1t[:, k, f4 * 128:(f4 + 1) * 128], hnT[:, k, t0:t0 + tn], k == 0) for k in range(8)]),
                                 reads=[w1r] + hn_res, writes=[zr])
                            rt, rr = rl.get()
                            p.op("act", I("activation", out=rt[:, 0:tn], in_=z[:, 0:tn], func=AF.Relu), reads=[zr], writes=[rr])
                            p.op("pool", I("tensor_tensor", out=fT[:, fc, t0:t0 + tn], in0=rt[:, 0:tn], in1=rt[:, 0:tn], op=ALU.mult),
                                 reads=[rr], writes=[("fT", fc, t0)])
                yts = [yb.get() for _ in range(nt)]
                for half in range(2):
                    accs = [psa() for _ in range(nt)]
                    for c in range(8):
                        w2t, w2r = w2s.get()
                        p.op("pool", I("dma_start", out=w2t[:], in_=w_ff2_v[:, c * 4:(c + 1) * 4, half * 512:(half + 1) * 512]), writes=[w2r], dkey=w2r)
                        for si in range(nt):
                            z, zr = accs[si]
                            p.op("pe", MM([(z[:, :], fT[:, c * 4 + f4, si * 128:(si + 1) * 128], w2t[:, f4, :], (c == 0 and f4 == 0)) for f4 in range(4)]),
                                 reads=[w2r] + [("fT", c * 4 + f4, 0) for f4 in range(4)], writes=[zr])
                    for si in range(nt):
                        z, zr = accs[si]
                        yt, yr = yts[si]
                        p.op("dve", I("tensor_tensor", out=yt[:, half * 512:(half + 1) * 512], in0=z[:, :], in1=h2[:, si, half * 512:(half + 1) * 512], op=ALU.add),
                             reads=[zr, ("h2", si, half)], writes=[(yr, half)])
                for si, (kind, ti) in enumerate(grp):
                    dst = yo if kind == "own" else ys
                    r0 = ti * 128
                    yt, yr = yts[si]
                    jk, jr = junk.get(); s4, s4r = st4.get()
                    p.op("act", I("activation", out=jk[:], in_=yt[:], func=AF.Square, accum_out=s4[:, 0:1]), reads=[(yr, 0), (yr, 1)], writes=[jr, (s4r, 0)])
                    p.op("act", I("activation", out=s4[:, 1:2], in_=s4[:, 0:1], func=AF.Sqrt, scale=1.0 / 1024, bias=EPS), reads=[(s4r, 0)], writes=[(s4r, 1)])
                    p.op("dve", I("reciprocal", out=s4[:, 2:3], in_=s4[:, 1:2]), reads=[(s4r, 1)], writes=[(s4r, 2)])
                    p.op("dve", I("scalar_tensor_tensor", out=yt[:], in0=yt[:], scalar=s4[:, 2:3], in1=gf_sb[:], op0=ALU.mult, op1=ALU.mult),
                         reads=[(yr, 0), (yr, 1), (s4r, 2), "gf"], writes=[(yr, 0), (yr, 1)])
                    outs.append(p.op("sp", I("dma_start", out=dst[r0:r0 + 128, :], in_=yt[:]), reads=[(yr, 0), (yr, 1)], dkey=("oy", si)))
        p.emit(final_waits=outs)
    return nc


def _consts():
    I_ = np.arange(128)[:, None]
    q_ = np.arange(512)[None, :]
    cover = np.zeros((2, 128, 64), np.float32)
    for c2 in range(2):
        for i in range(128):
            bi = c2 * 128 + i
            if bi > 254:
                continue
            for j in range(64):
                ov = min(16 * bi + 32, 64 * j + 64) - max(16 * bi, 64 * j)
                if ov > 0:
                    cover[c2, i, j] = ov / 32.0
    cmpm = np.zeros((8, 128, 512), np.float32)
    for qt in range(4):
        for c2 in range(2):
            bi = c2 * 128 + I_
            cmpm[qt * 2 + c2] = ((bi <= 254) & (16 * bi + 31 <= 2048 + 512 * qt + q_)).astype(np.float32)
    bandm = np.zeros((8, 128, 512), np.float32)
    for c in range(8):
        kk = 128 * c + I_
        bandm[c] = ((kk > q_) & (kk <= q_ + 512)).astype(np.float32)
    negc = np.zeros((4, 128, 512), np.float32)
    for d in range(4):
        negc[d] = -((128 * d + I_) > q_).astype(np.float32)
    E = np.zeros((64, 32, 128), np.float32)
    for c in range(32):
        for m_ in range(128):
            E[2 * c + m_ // 64, c, m_] = 1.0
    return dict(cover=cover, cmpm=cmpm, bandm=bandm, negc=negc, Eexp=E)


def _sample_consts():
    covs = np.zeros((128, 4, 129), np.float32)
    for c4 in range(4):
        for i in range(128):
            bi = c4 * 128 + i
            if bi > 510:
                continue
            for j in range(129):
                ov = min(16 * bi + 32, 64 * j + 64) - max(16 * bi, 64 * j)
                if ov > 0:
                    covs[i, c4, j] = ov / 32.0
    fbs = np.zeros((64, 129), np.float32)
    fbs[:, [0, 127, 128]] = 1e9
    r = np.arange(64)
    k_, g_, q_ = r // 32, (r // 8) % 4, r % 8
    sel2 = ((k_[:, None] == k_[None, :]) & (q_[:, None] == q_[None, :])).astype(np.float32)
    cm8 = (np.arange(8)[None, :] <= q_[:, None]).astype(np.float32)
    wm = (np.arange(512)[None, :] > q_[:, None]).astype(np.float32)
    selg = np.zeros((24, 3, 4, 128), np.float32)
    for br in range(3):
        for g in range(4):
            for k in range(2):
                selg[br * 8 + 4 * k + g, br, g, k * 64:(k + 1) * 64] = 1.0
    pcol = (np.arange(128) % 64).astype(np.float32)[:, None]
    r16 = np.arange(16)
    rep16 = ((r16[:, None] // 8 == k_[None, :]) & (r16[:, None] % 8 == q_[None, :])).astype(np.float32)
    return dict(covs=covs, fbs=fbs, sel2=sel2, cm8=cm8, wm=wm, selg=selg, pcol=pcol, rep16=rep16)


def _core_consts(h):
    first = 0 if h == 1 else 32
    selb = np.zeros((128, 16, 64), np.float32)
    vis = np.zeros((128, 16, 64), np.float32)
    j = np.arange(64)[None, :]
    for t in range(16):
        tl = 2048 + t * 128 + np.arange(128)[:, None]
        cur = tl // 64
        visible = (j >= first) & (j <= cur)
        forced = (j == first) | (j == cur) | (j == cur - 1)
        b = np.where(forced, 1e9, 0.0)
        b = np.where(visible, b, -1e30)
        selb[:, t, :] = b
        vis[:, t, :] = visible
    ctxb = np.zeros((128, 2), np.float32)
    if h == 0:
        ctxb[:, 0] = NEG
    return dict(selb=selb, vis=vis, ctxb=ctxb)


def _host_inputs(inp):
    f = lambda a: np.ascontiguousarray(np.asarray(a, dtype=np.float32))
    w_in = f(inp["w_in"][0])
    qperm = []
    for g in range(4):
        for k in range(2):
            h = k * 4 + g
            qperm += list(range(h * 64, (h + 1) * 64))
    perm = np.array(qperm + list(range(512, INC)))
    w_in_p = np.ascontiguousarray(w_in[:, perm])
    rep = lambda v, n=128: np.ascontiguousarray(np.broadcast_to(np.asarray(v, np.float32)[None, :], (n, len(v))))
    colT = lambda v: np.ascontiguousarray(np.asarray(v, np.float32).reshape(8, 128).T)
    b_s = f(inp["b_s"][0])
    gind = np.zeros((8, 512), np.float32)
    for g in range(8):
        gind[g, g * 64:(g + 1) * 64] = 1.0
    ii = np.arange(128)
    common = dict(
        w_in=w_in_p, w_out=f(inp["w_out"][0]), w_ff1=f(inp["w_ff1"][0]), w_ff2=f(inp["w_ff2"][0]),
        g1T=colT(inp["ln1_g"][0]), g2T=colT(inp["ln2_g"][0]), gfB=rep(inp["ln_f_g"]),
        lvg=rep(inp["ln_v_g"][0]), lvb=rep(inp["ln_v_b"][0]),
        wsT=np.ascontiguousarray(f(inp["w_s"][0]).transpose(0, 2, 1)),
        bs8=b_s, bs8s=np.ascontiguousarray(np.tile(b_s[:, 0:8], (1, 16))),
        ident=np.eye(128, dtype=np.float32),
        tril=(ii[:, None] <= ii[None, :]).astype(np.float32),
        gind=gind,
    )
    common.update(_consts())
    common.update(_sample_consts())
    cache = np.asarray(inp["cache_kv"], dtype=np.float32).reshape(-1, 1024)
    common["cache"] = cache
    pt = np.asarray(inp["page_table"]).astype(np.int32)
    pp = np.arange(128) // 64
    cw1 = f(inp["cmp_w1"][0]); cpos = f(inp["cmp_pos"][0]).reshape(2, 16, 128)
    cb2 = f(inp["cmp_b2"][0])
    common.update(dict(
        cw1=cw1, cpos=np.ascontiguousarray(cpos.transpose(2, 0, 1)),
        cpos64=np.ascontiguousarray(f(inp["cmp_pos"][0]).transpose(2, 0, 1)), cb1=np.ascontiguousarray(f(inp["cmp_b1"][0]).T),
        cw2=f(inp["cmp_w2"][0]), cb2k=np.ascontiguousarray(np.tile(cb2[0], 2)[:, None]),
        cb2v=rep(np.tile(cb2[1], 2)),
    ))
    xp = f(inp["x_prompt"]); xs = f(inp["x_sample"]).reshape(1024, 1024)
    swin = f(inp["state_win_kv"][0]).reshape(128, 512, 256)
    maps = []
    for c in range(8):
        b, h = c // 2, c % 2
        m = dict(common)
        m["xo"] = np.ascontiguousarray(xp[b, h * 2048:(h + 1) * 2048])
        m["xc"] = np.ascontiguousarray(xp[b, 0:2048])
        m["xs"] = np.ascontiguousarray(xs[c * 128:(c + 1) * 128])
        m["swin"] = np.ascontiguousarray(swin[c * 16:(c + 1) * 16])
        m.update(_core_consts(h))
        ptc = pt[c * 16:(c + 1) * 16].reshape(16, 32, 2)
        m["ptx"] = np.ascontiguousarray(ptc[:, :, pp].transpose(2, 0, 1))
        maps.append(m)
    return maps


STAGE = 3
_NC = {}


def kernel(**inp):
    return _run(_host_inputs(inp))


def _run(maps):
    npool = maps[0]["cache"].shape[0] // 64
    if (STAGE, npool) not in _NC:
        _NC[(STAGE, npool)] = build(STAGE, npool)
    nc = _NC[(STAGE, npool)]
    maps = [{"i_" + k: v for k, v in m.items()} for m in maps]
    res = run_bass_kernel_spmd(nc, maps, core_ids=list(range(8)))
    r = [{k[2:]: v for k, v in d.items()} for d in res.results]
    y_p = np.zeros((4, 4096, 1024), np.float32)
    kv_p = np.zeros((1, 4, 4096, 512), np.float32)
    win_p = np.zeros((1, 4, 512, 256), np.float32)
    for c in range(8):
        b, h = c // 2, c % 2
        y_p[b, h * 2048:(h + 1) * 2048] = r[c]["yo"]
        kv_p[0, b, h * 2048:(h + 1) * 2048] = r[c]["kvo"]
        if h == 1:
            win_p[0, b] = r[c]["wino"]
    y_s = np.concatenate([r[c]["ys"] for c in range(8)], 0).reshape(128, 8, 1024)
    kv_s = np.concatenate([r[c]["kvs"] for c in range(8)], 0).reshape(1, 128, 8, 4, 2, 64)
    win_s = np.concatenate([r[c]["wins"] for c in range(8)], 0).reshape(1, 128, 512, 2, 2, 64)
    v_s = np.concatenate([r[c]["vso"] for c in range(8)], 0).reshape(1, 128, 8, 512)
    return (y_p, y_s, kv_p.reshape(1, 4, 4096, 4, 2, 64), kv_s, win_p.reshape(1, 4, 512, 2, 2, 64), win_s, v_s)
```

```python
import contextlib
import numpy as np
import concourse.bass as bass
import concourse.mybir as mybir
from concourse.bass_utils import run_bass_kernel_spmd

F32 = mybir.dt.float32
BF16 = mybir.dt.bfloat16
I32 = mybir.dt.int32
AF = mybir.ActivationFunctionType
ALU = mybir.AluOpType
AX = mybir.AxisListType

ENGS = ("pe", "act", "dve", "pool", "sp")
EPS = 1e-6
NEG = -30000.0


class Op:
    __slots__ = ("eng", "fn", "deps", "idx", "sig", "sem", "val", "n_dma")

    def __init__(self, eng, fn):
        self.eng = eng
        self.fn = fn
        self.deps = []
        self.sig = False
        self.sem = None
        self.val = 0
        self.n_dma = 0


class Prog:
    def __init__(self, nc, stack):
        self.nc = nc
        self.stack = stack
        self.ops = []
        self.lastw = {}
        self.readers = {}
        self.esem = {}
        self.dsem = {}
        self.dcount = {}
        self.dlast = {}
        self.elast = {}
        self.ecount = {e: 0 for e in ENGS}
        for e in ("pe", "act", "dve", "pool"):
            self.esem[e] = stack.enter_context(nc.semaphore("s_" + e))
        self.nps = 0

    def sb(self, name, shape, dt, stack=None):
        return (stack or self.stack).enter_context(self.nc.sbuf_tensor(name, list(shape), dt))

    def _dsem(self, key):
        if key not in self.dsem:
            self.dsem[key] = self.stack.enter_context(self.nc.semaphore("d_" + str(len(self.dsem))))
            self.dcount[key] = 0
        return self.dsem[key]

    def op(self, eng, fn, reads=(), writes=(), dkey=None, ndma=1, extra=()):
        o = Op(eng, fn)
        o.idx = len(self.ops)
        deps = set(extra)
        _psr = [r for r in reads if isinstance(r, tuple) and r and r[0] == "ps"]
        if _psr:
            reads = [r for r in reads if not (isinstance(r, tuple) and r and r[0] == "ps")]
            writes = list(writes) + _psr
        if isinstance(dkey, str) and dkey.startswith("c"):
            dkey = "cser"
            if "cser" in self.dlast:
                deps.add(self.dlast["cser"])
        for r in reads:
            w = self.lastw.get(r)
            if w is not None:
                deps.add(w)
        for r in writes:
            w = self.lastw.get(r)
            if w is not None:
                deps.add(w)
            for rd in self.readers.get(r, ()):
                deps.add(rd)
        for r in reads:
            self.readers.setdefault(r, []).append(o)
        for r in writes:
            self.lastw[r] = o
            self.readers[r] = []
        if dkey is not None:
            o.sem = self._dsem(dkey)
            self.dcount[dkey] += 16 * ndma
            o.val = self.dcount[dkey]
            o.n_dma = ndma
            o.sig = True
            self.dlast[dkey] = o
        else:
            self.elast[eng] = o
        deps.discard(o)
        for d in deps:
            if d.eng == "pe" and eng == "pe" and d.n_dma == 0:
                continue
            o.deps.append(d)
        self.ops.append(o)
        return o

    def barrier(self):
        tg = list(self.elast.values()) + list(self.dlast.values())
        for e in ENGS:
            o = Op(e, lambda eng: None)
            o.deps = [d for d in tg]
            self.ops.append(o)
        self.lastw = {}
        self.readers = {}
        if not hasattr(self, "cuts"):
            self.cuts = []
        self.cuts.append(len(self.ops))

    def emit(self, final_waits=()):
        nc = self.nc
        for o in self.ops:
            for d in o.deps:
                if d.n_dma == 0:
                    d.sig = True
        for o in self.ops:
            if o.n_dma == 0 and o.sig:
                self.ecount[o.eng] += 1
                o.sem = self.esem[o.eng]
                o.val = self.ecount[o.eng]
        handles = {"pe": "tensor", "act": "scalar", "dve": "vector", "pool": "gpsimd", "sp": "sync"}
        cuts = [0] + list(getattr(self, "cuts", [])) + [len(self.ops)]
        seen_all = {e: {} for e in ENGS}
        nseg = len(cuts) - 1
        for si in range(nseg):
            seg = self.ops[cuts[si]:cuts[si + 1]]
            if not seg:
                continue
            per_eng = {e: [o for o in seg if o.eng == e] for e in ENGS}
            last_seg = (si == nseg - 1)
            with nc.Block() as block:
                for e in ENGS:
                    ops = per_eng[e]

                    def body(eng, ops=ops, e=e, last_seg=last_seg):
                        seen = seen_all[e]
                        for o in ops:
                            need = {}
                            for d in o.deps:
                                k = id(d.sem)
                                if seen.get(k, 0) >= d.val:
                                    continue
                                if k not in need or need[k][1] < d.val:
                                    need[k] = (d.sem, d.val)
                            for k, (s, v) in need.items():
                                eng.wait_ge(s, v)
                                seen[k] = v
                            insts = o.fn(eng)
                            if o.n_dma:
                                if not isinstance(insts, (list, tuple)):
                                    insts = [insts]
                                assert len(insts) == o.n_dma, (len(insts), o.n_dma)
                                for i in insts:
                                    i.then_inc(o.sem, 16)
                            elif o.sig:
                                if isinstance(insts, (list, tuple)):
                                    insts = insts[-1]
                                insts.then_inc(o.sem, 1)
                        if e == "sp" and last_seg:
                            for o in final_waits:
                                eng.wait_ge(o.sem, o.val)

                    getattr(block, handles[e])(body)


def I(method, **kw):
    return lambda e: getattr(e, method)(**kw)


def MM(lst):
    def f(e):
        last = None
        for (out, lhsT, rhs, start) in lst:
            last = e.matmul(out, lhsT, rhs, start=start, stop=True, skip_group_check=True)
        return last
    return f


def TR(lst):
    def f(e):
        last = None
        for (out, in_, ident) in lst:
            last = e.transpose(out, in_, ident)
        return last
    return f


class Rot:
    def __init__(self, p, name, n, shape, dt, stack=None):
        self.t = [p.sb("%s%d" % (name, i), shape, dt, stack) for i in range(n)]
        self.name = name
        self.n = n
        self.i = 0

    def get(self):
        k = self.i % self.n
        self.i += 1
        return self.t[k], (self.name, k)


NQ, NKV, NWG, NU, NV = 512, 512, 280, 512, 512
C_Q, C_KV, C_WG, C_U, C_V = 0, 512, 1024, 1304, 1816
INC = 2328


def build(stage, npool=10240):
    nc = bass.Bass("TRN2", target_bir_lowering=False)

    def DI(name, shape, dt=F32):
        return nc.dram_tensor("i_" + name, list(shape), dt, kind="ExternalInput").ap()

    def DO(name, shape, dt=F32):
        return nc.dram_tensor("o_" + name, list(shape), dt, kind="ExternalOutput").ap()

    xo = DI("xo", [2048, 1024]); xc = DI("xc", [2048, 1024]); xs = DI("xs", [128, 1024])
    swin = DI("swin", [16, 512, 256])
    w_in = DI("w_in", [1024, INC]); w_out = DI("w_out", [1024, 1024])
    w_ff1 = DI("w_ff1", [1024, 4096]); w_ff2 = DI("w_ff2", [4096, 1024])
    g1T = DI("g1T", [128, 8]); g2T = DI("g2T", [128, 8]); gfB = DI("gfB", [128, 1024])
    lvg = DI("lvg", [128, 512]); lvb = DI("lvb", [128, 512])
    wsT = DI("wsT", [8, 128, 128]); bs8 = DI("bs8", [8, 128]); bs8s = DI("bs8s", [8, 128])
    identd = DI("ident", [128, 128]); trild = DI("tril", [128, 128]); gindd = DI("gind", [8, 512])

    cw1 = DI("cw1", [2, 2048, 128]); cposd = DI("cpos", [128, 2, 16]); cb1d = DI("cb1", [128, 2])
    cw2 = DI("cw2", [2, 128, 64]); cb2kd = DI("cb2k", [128, 1]); cb2vd = DI("cb2v", [128, 128])
    coverd = DI("cover", [2, 128, 64])
    cmpmd = DI("cmpm", [8, 128, 512]); bandmd = DI("bandm", [8, 128, 512]); negcd = DI("negc", [4, 128, 512])
    Ed = DI("Eexp", [64, 32, 128]); selbd = DI("selb", [128, 16, 64]); visd = DI("vis", [128, 16, 64]); ctxbd = DI("ctxb", [128, 2])

    cache = DI("cache", [npool * 64, 1024]); ptxd = DI("ptx", [128, 16, 32], I32); pcold = DI("pcol", [128, 1])
    covsd = DI("covs", [128, 4, 129]); fbsd = DI("fbs", [64, 129]); sel2d = DI("sel2", [64, 64])
    cpos64d = DI("cpos64", [64, 2, 32]); cm8d = DI("cm8", [64, 8]); wmd = DI("wm", [64, 512]); selgd = DI("selg", [24, 3, 4, 128]); rep16d = DI("rep16", [16, 64])

    yo = DO("yo", [2048, 1024]); ys = DO("ys", [128, 1024])
    kvo = DO("kvo", [2048, 512]); kvs = DO("kvs", [128, 512])
    wino = DO("wino", [512, 256]); wins = DO("wins", [16, 512, 256]); vso = DO("vso", [128, 512])

    outs = []
    with contextlib.ExitStack() as st:
        p = Prog(nc, st)
        ps = [st.enter_context(nc.psum_tensor("ps%d" % i, [128, 512], F32)) for i in range(8)]
        rot = {"g": 0, "a": 0}

        def psg():
            k = rot["g"] % 4
            rot["g"] += 1
            return ps[k], ("ps", k)

        def psa():
            k = 4 + rot["a"] % 4
            rot["a"] += 1
            return ps[k], ("ps", k)

        ident_f = p.sb("ident_f", [128, 128], F32)
        ident = p.sb("ident", [128, 128], BF16)
        p.op("sp", I("dma_start", out=ident_f[:], in_=identd), writes=["ident_f"], dkey="c0")
        p.op("dve", I("tensor_copy", out=ident[:], in_=ident_f[:]), reads=["ident_f"], writes=["ident"])
        g1T_sb = p.sb("g1T_sb", [128, 8], F32); g2T_sb = p.sb("g2T_sb", [128, 8], F32)
        p.op("sp", I("dma_start", out=g1T_sb[:], in_=g1T), writes=["g1T"], dkey="c1")
        p.op("sp", I("dma_start", out=g2T_sb[:], in_=g2T), writes=["g2T"], dkey="c2")
        mixT = p.sb("mixT", [128, 8, 2176], BF16)
        sP = contextlib.ExitStack()
        Qbd = p.sb("Qbd", [128, 16, 64], BF16)
        KnT = p.sb("KnT", [128, 2, 128], BF16)
        p.op("pool", I("memset", ap=Qbd[:], constant=0.0), writes=["Qbd"])
        gates_sb = p.sb("gates_sb", [128, 17, 24], F32)
        KT = p.sb("KT", [128, 4, 4096], BF16, sP)
        vsaug = p.sb("vsaug", [128, 32, 2, 65], BF16, sP)
        vwaug = p.sb("vwaug", [128, 32, 2, 65], BF16, sP)
        qT = p.sb("qT", [128, 4, 2048], BF16, sP)
        p.op("pool", I("memset", ap=vsaug[:, :, :, 64:65], constant=1.0), writes=["vs1"])
        p.op("pool", I("memset", ap=vwaug[:, :, :, 64:65], constant=1.0), writes=["vw1"])

        with contextlib.ExitStack() as s1:
            win_sb = p.sb("win_sb", [128, 8, INC], BF16, s1)
            w_in_v = w_in.rearrange("(kc p) n -> p kc n", p=128)
            for kc in range(8):
                for h in range(2):
                    c0 = h * 1164
                    p.op("pool", I("dma_start", out=win_sb[:, kc, c0:c0 + 1164], in_=w_in_v[:, kc, c0:c0 + 1164]),
                         writes=[("win", kc, h)], dkey=("win", (kc * 2 + h) % 4))
            win_res = [("win", kc, h) for kc in range(8) for h in range(2)]
            lvg_sb = p.sb("lvg_sb", [128, 512], F32, s1); lvb_sb = p.sb("lvb_sb", [128, 512], F32, s1)
            p.op("sp", I("dma_start", out=lvg_sb[:], in_=lvg), writes=["lvg"], dkey="c3")
            p.op("sp", I("dma_start", out=lvb_sb[:], in_=lvb), writes=["lvb"], dkey="c4")
            tril_sb = p.sb("tril_sb", [128, 128], F32, s1)
            p.op("sp", I("dma_start", out=tril_sb[:], in_=trild), writes=["tril"], dkey="c5")
            ws_f = p.sb("ws_f", [128, 8, 128], F32, s1)
            ws_p = p.sb("ws_p", [128, 8, 128], BF16, s1)
            ws_s = p.sb("ws_s", [128, 8, 128], BF16, s1)
            p.op("sp", I("dma_start", out=ws_f[:], in_=wsT.rearrange("g j i -> j g i")), writes=["ws_f"], dkey="c6")
            trb = tril_sb[:, :].unsqueeze(1).broadcast_to([128, 8, 128])
            p.op("dve", I("tensor_tensor", out=ws_p[:], in0=ws_f[:], in1=trb, op=ALU.mult), reads=["ws_f", "tril"], writes=["ws_p"])
            ws_f2 = p.sb("ws_f2", [128, 8, 128], F32, s1)
            p.op("pool", I("memset", ap=ws_f2[:], constant=0.0), writes=["ws_f2"])

            def blk_dma(e, ws_f2=ws_f2):
                r = []
                for b in range(16):
                    r.append(e.dma_start(out=ws_f2[b * 8:(b + 1) * 8, :, b * 8:(b + 1) * 8],
                                         in_=wsT.rearrange("g j i -> j g i")[0:8, :, 0:8],
                                         allow_slow_non_contiguous=True))
                return r
            p.op("sp", blk_dma, reads=[], writes=["ws_f2"], dkey="c7", ndma=16)
            p.op("dve", I("tensor_tensor", out=ws_s[:], in0=ws_f2[:], in1=trb, op=ALU.mult), reads=["ws_f2", "tril"], writes=["ws_s"])
            bs_f = p.sb("bs_f", [8, 2, 128], F32, s1); bs_b = p.sb("bs_b", [8, 2, 128], BF16, s1)
            gind_f = p.sb("gind_f", [8, 512], F32, s1); gind = p.sb("gind", [8, 512], BF16, s1)
            p.op("sp", I("dma_start", out=bs_f[:, 0, :], in_=bs8), writes=["bs_f0"], dkey="c8")
            p.op("sp", I("dma_start", out=bs_f[:, 1, :], in_=bs8s), writes=["bs_f1"], dkey="c9")
            p.op("sp", I("dma_start", out=gind_f[:], in_=gindd), writes=["gind_f"], dkey="c10")
            p.op("dve", I("tensor_copy", out=bs_b[:], in_=bs_f[:]), reads=["bs_f0", "bs_f1"], writes=["bs_b"])
            p.op("dve", I("tensor_copy", out=gind[:], in_=gind_f[:]), reads=["gind_f"], writes=["gind"])

            xbuf = Rot(p, "xb", 2, [128, 1024], F32, s1)
            junk = Rot(p, "junk", 2, [128, 1024], BF16, s1)
            xn = Rot(p, "xn", 2, [128, 1024], BF16, s1)
            hT = Rot(p, "hT", 2, [128, 8, 128], BF16, s1)
            st4 = Rot(p, "st4", 4, [128, 8], F32, s1)
            zkv = Rot(p, "zkv", 2, [128, 512], F32, s1)
            zkvb = Rot(p, "zkvb", 2, [128, 512], BF16, s1)
            zwgb = Rot(p, "zwgb", 2, [128, 256], BF16, s1)
            zwg = Rot(p, "zwg", 2, [128, 280], F32, s1)
            qb = Rot(p, "qb", 2, [128, 512], BF16, s1)
            ub = Rot(p, "ub", 2, [128, 512], BF16, s1)
            vg = Rot(p, "vg", 2, [128, 512], F32, s1)
            vn = Rot(p, "vn", 2, [128, 512], F32, s1)
            vnb = Rot(p, "vnb", 2, [128, 512], BF16, s1)
            ob = Rot(p, "ob", 2, [128, 512], BF16, s1)
            bst = Rot(p, "bst", 2, [128, 6], F32, s1)

            def proj_tile(kind, ti):
                src = {"ctx": xc, "own": xo, "smp": xs}[kind]
                r0 = ti * 128
                xt, xr = xbuf.get()
                p.op("sp", I("dma_start", out=xt[:], in_=src[r0:r0 + 128, :]), writes=[xr], dkey=xr)
                jk, jr = junk.get(); s4, s4r = st4.get()
                p.op("act", I("activation", out=jk[:], in_=xt[:], func=AF.Square, accum_out=s4[:, 0:1]), reads=[xr], writes=[jr, (s4r, 0)])
                p.op("act", I("activation", out=s4[:, 1:2], in_=s4[:, 0:1], func=AF.Sqrt, scale=1.0 / 1024, bias=EPS), reads=[(s4r, 0)], writes=[(s4r, 1)])
                p.op("dve", I("reciprocal", out=s4[:, 2:3], in_=s4[:, 1:2]), reads=[(s4r, 1)], writes=[(s4r, 2)])
                xnt, xnr = xn.get()
                p.op("act", I("activation", out=xnt[:], in_=xt[:], func=AF.Copy, scale=s4[:, 2:3]), reads=[xr, (s4r, 2)], writes=[xnr])
                pt_, ptr = psg()
                ptb = pt_[:].bitcast(BF16)
                p.op("pe", TR([(ptb[:, k * 128:(k + 1) * 128], xnt[:, k * 128:(k + 1) * 128], ident[:]) for k in range(8)]),
                     reads=[xnr, "ident"], writes=[ptr])
                hTt, hTr = hT.get()
                p.op("dve", I("tensor_tensor", out=hTt[:], in0=ptb.rearrange("p (k t) -> p k t", k=8),
                              in1=g1T_sb[:, :].unsqueeze(2).broadcast_to([128, 8, 128]), op=ALU.mult),
                     reads=[ptr, "g1T"], writes=[hTr])

                def zgroup(c0, n):
                    z, zr = psa()
                    p.op("pe", MM([(z[:, 0:n], hTt[:, k, :], win_sb[:, k, c0:c0 + n], k == 0) for k in range(8)]),
                         reads=[hTr] + win_res, writes=[zr])
                    return z, zr

                col = (ti if kind == "ctx" else 16 + ti) * 128
                z, zr = zgroup(C_KV, NKV)
                zk, zkr = zkv.get()
                p.op("act", I("activation", out=zk[:], in_=z[:, :], func=AF.Copy), reads=[zr], writes=[zkr])
                if kind == "own":
                    outs.append(p.op("sp", I("dma_start", out=kvo[r0:r0 + 128, :], in_=zk[:]), reads=[zkr], dkey=("okv", ti % 2)))
                elif kind == "smp":
                    outs.append(p.op("sp", I("dma_start", out=kvs[:, :], in_=zk[:]), reads=[zkr], dkey="okvs"))
                z, zr = zgroup(C_WG, NWG)
                zw, zwr = zwg.get()
                p.op("act", I("activation", out=zw[:, 0:256], in_=z[:, 0:256], func=AF.Copy), reads=[zr], writes=[(zwr, 0)])
                if kind == "own" and ti >= 12:
                    outs.append(p.op("sp", I("dma_start", out=wino[(ti - 12) * 128:(ti - 11) * 128, :], in_=zw[:, 0:256]),
                                     reads=[(zwr, 0)], dkey=("owin", ti % 2)))
                if kind == "smp":
                    def wnew(e):
                        return [e.dma_start(out=wins[b, 504:512, :], in_=zw[b * 8:(b + 1) * 8, 0:256]) for b in range(16)]
                    outs.append(p.op("sp", wnew, reads=[(zwr, 0)], dkey="owins", ndma=16))
                zkbt, zkbr = zkvb.get()
                p.op("pool", I("tensor_copy", out=zkbt[:], in_=zk[:]), reads=[zkr], writes=[zkbr])
                zwbt, zwbr = zwgb.get()
                p.op("pool", I("tensor_copy", out=zwbt[:], in_=zw[:, 0:256]), reads=[(zwr, 0)], writes=[zwbr])
                if kind != "smp":
                    L = ti if kind == "ctx" else 16 + ti
                    pt3, pt3r = psg()
                    pt3b = pt3[:].bitcast(BF16)
                    p.op("pe", TR([(pt3b[:, 0:128], zkbt[:, 0:128], ident[:]), (pt3b[:, 128:256], zkbt[:, 128:256], ident[:]),
                                   (pt3b[:, 256:384], zkbt[:, 256:384], ident[:]), (pt3b[:, 384:512], zwbt[:, 0:128], ident[:])]),
                         reads=[zkbr, zwbr, "ident"], writes=[pt3r])
                    p.op("dve", I("tensor_copy", out=KT[:, :, col:col + 128], in_=pt3b[:, 0:512].rearrange("p (k t) -> p k t", k=4)),
                         reads=[pt3r], writes=[("KT", L)])
                    p.op("pool", I("tensor_copy", out=vsaug[:, L, :, 0:64], in_=zkbt[:, 384:512].rearrange("p (k d) -> p k d", k=2)),
                         reads=[zkbr], writes=[("vsaug", L)])
                    p.op("pool", I("tensor_copy", out=vwaug[:, L, :, 0:64], in_=zwbt[:, 128:256].rearrange("p (k d) -> p k d", k=2)),
                         reads=[zwbr], writes=[("vwaug", L)])
                if kind == "ctx":
                    return
                gi = 16 if kind == "smp" else ti
                p.op("act", I("activation", out=gates_sb[:, gi, :], in_=z[:, 256:280], func=AF.Sigmoid), reads=[zr], writes=[("gates", gi)])
                z, zr = zgroup(C_Q, NQ)
                qbt, qbr = qb.get()
                p.op("act", I("activation", out=qbt[:], in_=z[:, :], func=AF.Copy, scale=0.125), reads=[zr], writes=[qbr])
                pt4, pt4r = psg()
                pt4b = pt4[:].bitcast(BF16)
                p.op("pe", TR([(pt4b[:, k * 128:(k + 1) * 128], qbt[:, k * 128:(k + 1) * 128], ident[:]) for k in range(4)]),
                     reads=[qbr, "ident"], writes=[pt4r])
                if kind == "own":
                    p.op("dve", I("tensor_copy", out=qT[:, :, ti * 128:(ti + 1) * 128], in_=pt4b[:, 0:512].rearrange("p (k t) -> p k t", k=4)),
                         reads=[pt4r], writes=[("qT", ti)])
                else:
                    for k in range(2):
                        hs = slice(k * 64, (k + 1) * 64)
                        for g in range(4):
                            p.op("dve", I("tensor_copy", out=Qbd[hs, :, k * 32 + g * 8:k * 32 + g * 8 + 8],
                                          in_=pt4b[hs, g * 128:(g + 1) * 128].rearrange("p (b q) -> p b q", q=8)),
                                 reads=[pt4r, "Qbd"], writes=["Qbd"])
                    pt5, pt5r = psg()
                    pt5b = pt5[:].bitcast(BF16)
                    p.op("pe", TR([(pt5b[:, 0:128], zkbt[:, 256:384], ident[:]), (pt5b[:, 128:256], zwbt[:, 0:128], ident[:])]),
                         reads=[zkbr, zwbr, "ident"], writes=[pt5r])
                    p.op("dve", I("tensor_copy", out=KnT[:], in_=pt5b[:, 0:256].rearrange("p (k t) -> p k t", k=2)), reads=[pt5r], writes=["KnT"])
                z, zr = zgroup(C_U, NU)
                ut, ur = ub.get()
                p.op("act", I("activation", out=ut[:], in_=z[:, :], func=AF.Gelu_apprx_tanh), reads=[zr], writes=[ur])
                z, zr = zgroup(C_V, NV)
                vgt, vgr = vg.get()
                p.op("act", I("activation", out=vgt[:], in_=z[:, :], func=AF.Gelu_apprx_tanh), reads=[zr], writes=[vgr])
                b6, b6r = bst.get()
                p.op("dve", I("bn_stats", out=b6[:, 0:6], in_=vgt[:]), reads=[vgr], writes=[(b6r, 0)])
                p.op("dve", I("bn_aggr", out=s4[:, 3:5], in_=b6[:, 0:6]), reads=[(b6r, 0)], writes=[(s4r, 3)])
                p.op("act", I("activation", out=s4[:, 5:6], in_=s4[:, 4:5], func=AF.Sqrt, scale=1.0, bias=EPS), reads=[(s4r, 3)], writes=[(s4r, 5)])
                p.op("dve", I("reciprocal", out=s4[:, 6:7], in_=s4[:, 5:6]), reads=[(s4r, 5)], writes=[(s4r, 6)])
                vnt, vnr = vn.get()
                p.op("dve", I("tensor_scalar", out=vnt[:], in0=vgt[:], scalar1=s4[:, 3:4], scalar2=s4[:, 6:7], op0=ALU.subtract, op1=ALU.mult),
                     reads=[vgr, (s4r, 3), (s4r, 6)], writes=[vnr])
                p.op("pool", I("tensor_tensor", out=vnt[:], in0=vnt[:], in1=lvg_sb[:], op=ALU.mult), reads=[vnr, "lvg"], writes=[vnr])
                p.op("pool", I("tensor_tensor", out=vnt[:], in0=vnt[:], in1=lvb_sb[:], op=ALU.add), reads=[vnr, "lvb"], writes=[vnr])
                if kind == "smp":
                    outs.append(p.op("sp", I("dma_start", out=vso[:, :], in_=vnt[:]), reads=[vnr], dkey="ovs"))
                vbt, vbr = vnb.get()
                p.op("pool", I("tensor_copy", out=vbt[:], in_=vnt[:]), reads=[vnr], writes=[vbr])
                sp_, spr = psg()
                wsx = ws_s if kind == "smp" else ws_p
                bi = 1 if kind == "smp" else 0
                lst = [(sp_[:, g * 64:(g + 1) * 64], wsx[:, g, :], vbt[:, g * 64:(g + 1) * 64], g == 0) for g in range(8)]
                lst.append((sp_[:, :], bs_b[:, bi, :], gind[:, :], False))
                p.op("pe", MM(lst), reads=[vbr, "ws_p", "ws_s", "bs_b", "gind"], writes=[spr])
                obt, obr = ob.get()
                p.op("dve", I("tensor_tensor", out=obt[:], in0=sp_[:, :], in1=ut[:], op=ALU.mult), reads=[spr, ur], writes=[obr])
                pt2, pt2r = psg()
                pt2b = pt2[:].bitcast(BF16)
                p.op("pe", TR([(pt2b[:, k * 128:(k + 1) * 128], obt[:, k * 128:(k + 1) * 128], ident[:]) for k in range(4)]),
                     reads=[obr, "ident"], writes=[pt2r])
                mcol = (2048 if kind == "smp" else ti * 128)
                p.op("act", I("activation", out=mixT[:, 4:8, mcol:mcol + 128], in_=pt2b[:, 0:512].rearrange("p (k t) -> p k t", k=4), func=AF.Copy),
                     reads=[pt2r], writes=[("mixT", "b", mcol)])

            proj_tile("smp", 0)
            for ti in range(16):
                proj_tile("ctx", ti)
            for ti in range(16):
                proj_tile("own", ti)
            outs.append(p.op("sp", I("dma_start", out=wins[:, 0:504, :], in_=swin[:, 8:512, :]), dkey="owins2"))
            p.barrier()

        if stage < 2:
            p.op("pool", I("memset", ap=mixT[:, 0:4, :], constant=0.0), writes=[("mixT", "a")])
        elif stage < 3:
            p.op("pool", I("memset", ap=mixT[:, 0:4, 2048:2176], constant=0.0), writes=[("mixT", "a")])

        if stage >= 2:
          with contextlib.ExitStack() as s2:
            W1r = p.sb("W1r", [128, 2, 32, 128], BF16, s2)
            W2c = p.sb("W2c", [128, 2, 16, 128], BF16, s2)
            for stq in range(2):
                v1 = cw1[stq].rearrange("(s d) h -> d s h", d=64)
                for hf in range(2):
                    p.op("pool", I("dma_start", out=W1r[hf * 64:(hf + 1) * 64, stq, :, :], in_=v1), writes=[("W1r", stq, hf)], dkey=("w1r", hf))
                p.op("pool", I("dma_start", out=W2c[:, stq, :, :], in_=cw1[stq].rearrange("(c p) h -> p c h", p=128)), writes=[("W2c", stq)], dkey=("w2c", stq))
            W1r_res = [("W1r", a_, b_) for a_ in range(2) for b_ in range(2)]
            posc = p.sb("posc", [128, 2, 16], BF16, s2)
            p.op("pool", I("dma_start", out=posc[:], in_=cposd), writes=["posc"], dkey="c20")
            b1T = p.sb("b1T", [128, 2], F32, s2)
            p.op("sp", I("dma_start", out=b1T[:], in_=cb1d), writes=["b1T"], dkey="c21")
            w2pad = p.sb("w2pad", [128, 2, 2, 128], BF16, s2)
            p.op("dve", I("memset", ap=w2pad[:], constant=0.0), writes=["w2pad"])

            def w2dma(e, w2pad=w2pad):
                r = []
                for stq in range(2):
                    for va in range(2):
                        r.append(e.dma_start(out=w2pad[:, stq, va, va * 64:(va + 1) * 64], in_=cw2[stq]))
                return r
            p.op("pool", w2dma, writes=["w2pad"], dkey="c22", ndma=4)
            b2k2 = p.sb("b2k2", [128, 1], F32, s2)
            p.op("sp", I("dma_start", out=b2k2[:], in_=cb2kd), writes=["b2k2"], dkey="c23")
            b2vB = p.sb("b2vB", [128, 128], F32, s2)
            p.op("sp", I("dma_start", out=b2vB[:], in_=cb2vd), writes=["b2vB"], dkey="c24")
            kcT = p.sb("kcT", [128, 256], BF16, s2)
            vcaug = p.sb("vcaug", [128, 2, 2, 128], BF16, s2)
            p.op("dve", I("memset", ap=kcT[:], constant=0.0), writes=["kcT"])
            p.op("dve", I("memset", ap=vcaug[:], constant=0.0), writes=["vcaug"])

            def covdma(e, vcaug=vcaug):
                r = []
                for c2 in range(2):
                    for k in range(2):
                        r.append(e.dma_start(out=vcaug[:, c2, k, 64:128], in_=coverd[c2]))
                return r
            p.op("pool", covdma, writes=["vcaug"], dkey="c25", ndma=4)
            b1tot = p.sb("b1tot", [128, 2], F32, s2)
            z, zr = psg()
            lst = []
            for stq in range(2):
                for c in range(16):
                    lst.append((z[:, stq:stq + 1], W2c[:, stq, c, :], posc[:, stq, c:c + 1], (stq == 0 and c == 0)))
            p.op("pe", MM(lst), reads=[("W2c", 0), ("W2c", 1), "posc"], writes=[zr])
            p.op("dve", I("tensor_tensor", out=b1tot[:], in0=z[:, 0:2], in1=b1T[:], op=ALU.add), reads=[zr, "b1T"], writes=["b1tot"])
            ghT = p.sb("ghT", [128, 2, 2, 256], BF16, s2)
            KT_res = [("KT", L) for L in range(32)]
            for stq in range(2):
                for k in range(2):
                    z, zr = psg()
                    hs = slice(k * 64, (k + 1) * 64)
                    p.op("pe", MM([(z[:, 0:255], W1r[hs, stq, s_, :], KT[hs, stq, s_:s_ + 16 * 254 + 1:16], s_ == 0) for s_ in range(32)]),
                         reads=W1r_res + KT_res, writes=[zr])
                    p.op("act", I("activation", out=ghT[:, stq, k, 0:255], in_=z[:, 0:255], func=AF.Gelu_apprx_tanh, bias=b1tot[:, stq:stq + 1]),
                         reads=[zr, "b1tot"], writes=[("ghT", stq, k)])
            z, zr = psg()
            p.op("pe", MM([(z[:, 0:255], w2pad[:, 0, 0, :], ghT[:, 0, 0, 0:255], True), (z[:, 0:255], w2pad[:, 0, 1, :], ghT[:, 0, 1, 0:255], False)]),
                 reads=["w2pad", ("ghT", 0, 0), ("ghT", 0, 1)], writes=[zr])
            p.op("act", I("activation", out=kcT[:, 0:255], in_=z[:, 0:255], func=AF.Identity, bias=b2k2[:, 0:1]), reads=[zr, "b2k2"], writes=["kcT"])
            for c2 in range(2):
                nb = 128 if c2 == 0 else 127
                z, zr = psg()
                p.op("pe", MM([(z[0:nb, k * 64:(k + 1) * 64], ghT[:, 1, k, c2 * 128:c2 * 128 + nb], w2pad[:, 1, 0, 0:64], k == 0) for k in range(2)]),
                     reads=["w2pad", ("ghT", 1, 0), ("ghT", 1, 1)], writes=[zr])
                p.op("dve", I("tensor_tensor", out=vcaug[0:nb, c2, :, 0:64], in0=z[0:nb, 0:128].rearrange("p (k d) -> p k d", k=2),
                              in1=b2vB[0:nb, :].rearrange("p (k d) -> p k d", k=2), op=ALU.add),
                     reads=[zr, "b2vB", "vcaug"], writes=["vcaug"])

            cmpm = p.sb("cmpm", [128, 8, 512], BF16, s2)
            bandm = p.sb("bandm", [128, 8, 512], BF16, s2)
            negc = p.sb("negc", [128, 4, 512], BF16, s2)
            Eexp = p.sb("Eexp", [64, 32, 128], BF16, s2)
            for c in range(8):
                p.op("pool", I("dma_start", out=cmpm[:, c, :], in_=cmpmd[c]), writes=[("cmpm", c)], dkey=("cm", c % 2))
                p.op("pool", I("dma_start", out=bandm[:, c, :], in_=bandmd[c]), writes=[("bandm", c)], dkey=("bm", c % 2))
            for c in range(4):
                p.op("pool", I("dma_start", out=negc[:, c, :], in_=negcd[c]), writes=[("negc", c)], dkey=("nm", c % 2))
                p.op("pool", I("dma_start", out=Eexp[:, c * 8:(c + 1) * 8, :], in_=Ed[:, c * 8:(c + 1) * 8, :]), writes=[("E", c)], dkey=("em", c % 2))
            selb_sb = p.sb("selb_sb", [128, 16, 64], F32, s2)
            vis_sb = p.sb("vis_sb", [128, 16, 64], F32, s2)
            ctxb = p.sb("ctxb", [128, 2], F32, s2)
            p.op("sp", I("dma_start", out=selb_sb[:], in_=selbd), writes=["selb"], dkey="c26")
            p.op("sp", I("dma_start", out=vis_sb[:], in_=visd), writes=["vis"], dkey="c27")
            p.op("sp", I("dma_start", out=ctxb[:], in_=ctxbd), writes=["ctxb"], dkey="c28")
            pT = Rot(p, "pT", 4, [128, 512], BF16, s2)
            pTm = Rot(p, "pTm", 5, [128, 512], BF16, s2)
            maskb = Rot(p, "maskb", 2, [128, 512], BF16, s2)
            pcm = p.sb("pcm", [128, 8, 512], BF16, s2)
            selT = Rot(p, "selT", 2, [64, 512], BF16, s2)
            selbf = Rot(p, "selbf", 2, [128, 64], BF16, s2)
            sc_r = Rot(p, "sc", 2, [128, 64], F32, s2)
            scr_r = Rot(p, "scr", 2, [128, 64], F32, s2)
            sm = Rot(p, "sm", 4, [128, 32], F32, s2)
            oacc = Rot(p, "oacc", 8, [128, 4, 64], F32, s2)
            otmp = Rot(p, "otmp", 2, [128, 4, 64], F32, s2)
            oa_tok = p.sb("oa_tok", [128, 4, 512], BF16, s2)
            mres = [("cmpm", c) for c in range(8)] + [("bandm", c) for c in range(8)] + [("negc", c) for c in range(4)] + [("E", c) for c in range(4)]

            def sT_exp(kstream, kcol, k, g, qt, use_ctx):
                hs = slice(k * 64, (k + 1) * 64)
                z, zr = psg()
                if kstream is None:
                    lhsT = kcT[hs, kcol:kcol + 128]
                    rd = ["kcT"]
                else:
                    lhsT = KT[hs, kstream, kcol:kcol + 128]
                    rd = [("KT", kcol // 128)]
                p.op("pe", MM([(z[:, :], lhsT, qT[hs, g, qt * 512:(qt + 1) * 512], True)]),
                     reads=rd + [("qT", qt * 4 + j) for j in range(4)], writes=[zr])
                t, tr = pT.get()
                bi = 0 if use_ctx else 1
                p.op("act", I("activation", out=t[:], in_=z[:, :], func=AF.Exp, bias=ctxb[:, bi:bi + 1]), reads=[zr, "ctxb"], writes=[tr])
                return t, tr

            def finish_branch(obanks, br, k, qt, accs, width, first):
                for sub in range(4):
                    ti = qt * 4 + sub
                    o, orr = obanks[sub]
                    o3 = o[:, 0:4 * width].rearrange("p (g w) -> p g w", g=4)
                    m, mr = sm.get()
                    if width == 128:
                        p.op("dve", I("tensor_reduce", out=m[:, 0:4], in_=o3[:, :, 64:128], axis=AX.X, op=ALU.add), reads=[orr], writes=[(mr, 0)])
                    else:
                        p.op("dve", I("tensor_copy", out=m[:, 0:4], in_=o3[:, :, 64]), reads=[orr], writes=[(mr, 0)])
                    p.op("dve", I("tensor_scalar", out=m[:, 12:16], in0=m[:, 0:4], scalar1=1e-30, scalar2=None, op0=ALU.max), reads=[(mr, 0)], writes=[(mr, 5)])
                    p.op("dve", I("reciprocal", out=m[:, 4:8], in_=m[:, 12:16]), reads=[(mr, 5)], writes=[(mr, 1)])
                    gc = br * 8 + k * 4
                    p.op("dve", I("tensor_tensor", out=m[:, 8:12], in0=m[:, 4:8], in1=gates_sb[:, ti, gc:gc + 4], op=ALU.mult),
                         reads=[(mr, 1), ("gates", ti)], writes=[(mr, 2)])
                    coefb = m[:, 8:12].unsqueeze(2).broadcast_to([128, 4, 64])
                    a, ar = accs[sub]
                    if first:
                        p.op("dve", I("tensor_tensor", out=a[:], in0=o3[:, :, 0:64], in1=coefb, op=ALU.mult), reads=[orr, (mr, 2)], writes=[ar])
                    else:
                        tt, ttr = otmp.get()
                        p.op("dve", I("tensor_tensor", out=tt[:], in0=o3[:, :, 0:64], in1=coefb, op=ALU.mult), reads=[orr, (mr, 2)], writes=[ttr])
                        p.op("pool", I("tensor_tensor", out=a[:], in0=a[:], in1=tt[:], op=ALU.add), reads=[ttr, ar], writes=[ar])
                    if width == 128:
                        sc, scr = sc_r.get()
                        for g in range(4):
                            in1 = selb_sb[:, ti, :] if g == 0 else sc[:]
                            p.op("dve", I("scalar_tensor_tensor", out=sc[:], in0=o3[:, g, 64:128], scalar=m[:, 4 + g:5 + g], in1=in1, op0=ALU.mult, op1=ALU.add),
                                 reads=[orr, (mr, 1), "selb", scr], writes=[scr])
                        p.op("dve", I("max", out=m[:, 16:24], in_=sc[:]), reads=[scr], writes=[(mr, 3)])
                        s2_, s2r = scr_r.get()
                        p.op("dve", I("match_replace", out=s2_[:], in_to_replace=m[:, 16:24], in_values=sc[:], imm_value=-3.0e38), reads=[scr, (mr, 3)], writes=[s2r])
                        p.op("dve", I("max", out=m[:, 24:32], in_=s2_[:]), reads=[s2r], writes=[(mr, 4)])
                        sb_, sbr = selbf.get()
                        p.op("dve", I("scalar_tensor_tensor", out=sb_[:], in0=sc[:], scalar=m[:, 31:32], in1=vis_sb[:, ti, :], op0=ALU.is_ge, op1=ALU.mult),
                             reads=[scr, (mr, 4), "vis"], writes=[sbr])
                        z, zr = psg()
                        zb = z[:].bitcast(BF16)
                        p.op("pe", TR([(zb[0:64, 0:128], sb_[:, :], ident[:])]), reads=[sbr, "ident"], writes=[zr])
                        p.op("act", I("activation", out=cur_selT[0][0:64, sub * 128:(sub + 1) * 128], in_=zb[0:64, 0:128], func=AF.Copy),
                             reads=[zr], writes=[(cur_selT[1], sub)])

            cur_selT = [None, None]
            for qt in range(4):
                for k in range(2):
                    hs = slice(k * 64, (k + 1) * 64)
                    accs = [oacc.get() for _ in range(4)]
                    cur_selT[0], cur_selT[1] = selT.get()
                    for c2 in range(2):
                        for g in range(4):
                            t, tr = sT_exp(None, c2 * 128, k, g, qt, c2 == 0)
                            p.op("pool", I("tensor_tensor", out=pcm[:, c2 * 4 + g, :], in0=t[:], in1=cmpm[:, qt * 2 + c2, :], op=ALU.mult),
                                 reads=[tr] + mres, writes=[("pcm", c2 * 4 + g)])
                    ob = [psa() for _ in range(4)]
                    for sub in range(4):
                        o, orr = ob[sub]
                        lst = []
                        for g in range(4):
                            for c2 in range(2):
                                lst.append((o[:, g * 128:(g + 1) * 128], pcm[:, c2 * 4 + g, sub * 128:(sub + 1) * 128], vcaug[:, c2, k, :], (g == 0 and c2 == 0)))
                        p.op("pe", MM(lst), reads=[("pcm", j) for j in range(8)] + ["vcaug"], writes=[orr])
                    finish_branch(ob, 0, k, qt, accs, 128, True)
                    nch = 16 + 4 * (qt + 1)
                    ob = [psa() for _ in range(4)]
                    obr = [r_ for (_, r_) in ob]
                    pend = []
                    LAG = 2

                    def pv_slc(it, ob=ob, obr=obr, k=k):
                        tm, tmr, c, g = it
                        p.op("pe", MM([(ob[sub][0][:, g * 65:(g + 1) * 65], tm[:, sub * 128:(sub + 1) * 128], vsaug[:, c, k, :], (c == 0 and g == 0))
                                       for sub in range(4)]),
                             reads=[tmr, ("vsaug", c), "vs1"], writes=obr)
                    for c in range(nch):
                        z, zr = psg()
                        d = c - (16 + 4 * qt)
                        lst = [(z[:, :], Eexp[:, c, :], cur_selT[0][:, :], True)]
                        if d >= 0:
                            lst.append((z[:, :], ident[:], negc[:, d, :], False))
                        p.op("pe", MM(lst), reads=[(cur_selT[1], j) for j in range(4)] + mres + ["ident"], writes=[zr])
                        mb, mbr = maskb.get()
                        p.op("dve", I("tensor_scalar", out=mb[:], in0=z[:, :], scalar1=0.0, scalar2=None, op0=ALU.max), reads=[zr], writes=[mbr])
                        for g in range(4):
                            t, tr = sT_exp(2, c * 128, k, g, qt, c < 16)
                            tm, tmr = pTm.get()
                            p.op("dve" if g % 2 == 0 else "pool", I("tensor_tensor", out=tm[:], in0=t[:], in1=mb[:], op=ALU.mult), reads=[tr, mbr], writes=[tmr])
                            pend.append((tm, tmr, c, g))
                            if len(pend) > LAG:
                                pv_slc(pend.pop(0))
                    while pend:
                        pv_slc(pend.pop(0))
                    finish_branch(ob, 1, k, qt, accs, 65, False)
                    ob = [psa() for _ in range(4)]
                    obr = [r_ for (_, r_) in ob]
                    pend = []

                    def pv_win(it, ob=ob, obr=obr, k=k):
                        tm, tmr, c, g, kc_ = it
                        p.op("pe", MM([(ob[sub][0][:, g * 65:(g + 1) * 65], tm[:, sub * 128:(sub + 1) * 128], vwaug[:, kc_, k, :], (c == 0 and g == 0))
                                       for sub in range(4)]),
                             reads=[tmr, ("vwaug", kc_), "vw1"], writes=obr)
                    for c in range(8):
                        kc_ = 16 + 4 * qt - 4 + c
                        for g in range(4):
                            t, tr = sT_exp(3, kc_ * 128, k, g, qt, kc_ < 16)
                            tm, tmr = pTm.get()
                            p.op("dve" if g % 2 == 0 else "pool", I("tensor_tensor", out=tm[:], in0=t[:], in1=bandm[:, c, :], op=ALU.mult), reads=[tr] + mres, writes=[tmr])
                            pend.append((tm, tmr, c, g, kc_))
                            if len(pend) > LAG:
                                pv_win(pend.pop(0))
                    while pend:
                        pv_win(pend.pop(0))
                    finish_branch(ob, 2, k, qt, accs, 65, False)
                    for sub in range(4):
                        a, ar = accs[sub]
                        p.op("pool", I("tensor_copy", out=oa_tok[:, sub, k * 256:(k + 1) * 256], in_=a[:].rearrange("p g d -> p (g d)")),
                             reads=[ar], writes=[("oa_tok", sub, k)])
                for sub in range(4):
                    ti = qt * 4 + sub
                    z, zr = psg()
                    zb = z[:].bitcast(BF16)
                    p.op("pe", TR([(zb[:, j * 128:(j + 1) * 128], oa_tok[:, sub, j * 128:(j + 1) * 128], ident[:]) for j in range(4)]),
                         reads=[("oa_tok", sub, 0), ("oa_tok", sub, 1), "ident"], writes=[zr])
                    p.op("act", I("activation", out=mixT[:, 0:4, ti * 128:(ti + 1) * 128], in_=zb[:, 0:512].rearrange("p (k t) -> p k t", k=4), func=AF.Copy),
                         reads=[zr], writes=[("mixT", "a", ti * 128)])
            p.barrier()

        sP.close()
        import os as _os0
        if stage >= 3 and _os0.environ.get('DBG_NOS') != '1':
          with contextlib.ExitStack() as sS:
            idx_i = p.sb("idx_i", [128, 512], I32, sS); idx_f = p.sb("idx_f", [128, 512], F32, sS); idx = p.sb("idx", [128, 512], I32, sS)
            pcol = p.sb("pcol", [128, 1], F32, sS)
            p.op("sp", I("dma_start", out=idx_i[:], in_=ptxd.rearrange("p b c -> p (b c)")), writes=["idx_i"], dkey="c30")
            p.op("sp", I("dma_start", out=pcol[:], in_=pcold), writes=["pcol"], dkey="c31")
            p.op("dve", I("tensor_copy", out=idx_f[:], in_=idx_i[:]), reads=["idx_i"], writes=["idx_f"])
            p.op("dve", I("tensor_scalar", out=idx_f[:], in0=idx_f[:], scalar1=64.0, scalar2=pcol[:, 0:1], op0=ALU.mult, op1=ALU.add),
                 reads=["idx_f", "pcol"], writes=["idx_f"])
            p.op("dve", I("tensor_copy", out=idx[:], in_=idx_f[:]), reads=["idx_f"], writes=["idx"])
            W1s = p.sb("W1s", [128, 2, 32, 128], BF16, sS)
            for stq in range(2):
                v1 = cw1[stq].rearrange("(s d) h -> d s h", d=64)
                for hf in range(2):
                    p.op("pool", I("dma_start", out=W1s[hf * 64:(hf + 1) * 64, stq, :, :], in_=v1), writes=[("W1s", stq, hf)], dkey=("w1r", hf))
            W1s_res = [("W1s", a_, b_) for a_ in range(2) for b_ in range(2)]
            posd_sb = p.sb("posd_sb", [64, 2, 32], BF16, sS)
            p.op("pool", I("dma_start", out=posd_sb[:], in_=cpos64d), writes=["posd"], dkey="c20")
            b1T = p.sb("b1Ts", [128, 2], F32, sS)
            p.op("sp", I("dma_start", out=b1T[:], in_=cb1d), writes=["b1T"], dkey="c21")
            w2pad = p.sb("w2pads", [128, 2, 2, 128], BF16, sS)
            p.op("dve", I("memset", ap=w2pad[:], constant=0.0), writes=["w2pad"])

            def w2dma_s(e, w2pad=w2pad):
                r = []
                for stq in range(2):
                    for va in range(2):
                        r.append(e.dma_start(out=w2pad[:, stq, va, va * 64:(va + 1) * 64], in_=cw2[stq]))
                return r
            p.op("pool", w2dma_s, writes=["w2pad"], dkey="c22", ndma=4)
            b2k2 = p.sb("b2k2s", [128, 1], F32, sS)
            p.op("sp", I("dma_start", out=b2k2[:], in_=cb2kd), writes=["b2k2"], dkey="c23")
            b2vB = p.sb("b2vBs", [128, 128], F32, sS)
            p.op("sp", I("dma_start", out=b2vB[:], in_=cb2vd), writes=["b2vB"], dkey="c24")
            b1tot = p.sb("b1tots", [128, 2], F32, sS)
            z, zr = psg()
            lst = []
            for stq in range(2):
                for c in range(32):
                    lst.append((z[:, stq:stq + 1], W1s[0:64, stq, c, :], posd_sb[:, stq, c:c + 1], (stq == 0 and c == 0)))
            p.op("pe", MM(lst), reads=W1s_res + ["posd"], writes=[zr])
            p.op("dve", I("tensor_tensor", out=b1tot[:], in0=z[:, 0:2], in1=b1T[:], op=ALU.add), reads=[zr, "b1T"], writes=["b1tot"])
            fbs = p.sb("fbs", [64, 129], F32, sS); sel2 = p.sb("sel2", [64, 64], F32, sS)
            cm8f = p.sb("cm8f", [64, 8], F32, sS); cm8 = p.sb("cm8", [64, 8], BF16, sS)
            wm = p.sb("wm", [64, 512], BF16, sS); selg = p.sb("selg", [24, 3, 4, 128], F32, sS)
            p.op("sp", I("dma_start", out=fbs[:], in_=fbsd), writes=["fbs"], dkey="c32")
            p.op("sp", I("dma_start", out=sel2[:], in_=sel2d), writes=["sel2"], dkey="c33")
            p.op("sp", I("dma_start", out=cm8f[:], in_=cm8d), writes=["cm8f"], dkey="c34")
            p.op("dve", I("tensor_copy", out=cm8[:], in_=cm8f[:]), reads=["cm8f"], writes=["cm8"])
            p.op("pool", I("dma_start", out=wm[:], in_=wmd), writes=["wm"], dkey="c35")
            p.op("sp", I("dma_start", out=selg[:], in_=selgd), writes=["selg"], dkey="c36")
            vcs = p.sb("vcs", [128, 4, 257], BF16, sS)
            p.op("dve", I("memset", ap=vcs[:], constant=0.0), writes=["vcs"])
            p.op("pool", I("dma_start", out=vcs[:, :, 128:257], in_=covsd), writes=["vcs"], reads=["vcs"], dkey="c37")
            Vn = p.sb("Vn", [8, 2, 16, 129], BF16, sS)
            p.op("dve", I("memset", ap=Vn[:], constant=1.0), writes=["Vn"])
            p.op("pool", I("dma_start", out=Vn[:, 0, :, 0:128], in_=kvs.rearrange("(b q) c -> q b c", q=8)[:, :, 384:512]), reads=["Vn"], writes=["Vn"], dkey="c38")
            p.op("pool", I("dma_start", out=Vn[:, 1, :, 0:128], in_=wins[:, 504:512, 128:256].rearrange("b q c -> q b c")), reads=["Vn"], writes=["Vn"], dkey="c39")
            X2T = p.sb("X2T", [128, 2, 2, 8, 512], BF16, sS)
            KsT = p.sb("KsT", [128, 2, 4096], BF16, sS)
            Vs = p.sb("Vs", [128, 32, 2, 129], BF16, sS)
            p.op("pool", I("memset", ap=Vs[:, :, :, 128:129], constant=1.0), writes=["Vs1"])
            G = Rot(p, "G", 6, [128, 1024], BF16, sS)
            gh = p.sb("gh", [128, 2, 2, 512], BF16, sS)
            kcTs = p.sb("kcTs", [128, 512], BF16, sS)
            Pc = Rot(p, "Pc", 2, [64, 512], BF16, sS)
            for t_ in Pc.t:
                p.op("dve", I("memset", ap=t_[:], constant=0.0), writes=[("Pc", Pc.t.index(t_))])
            PcT = Rot(p, "PcT", 2, [128, 4, 64], BF16, sS)
            Pq = Rot(p, "Pq", 3, [64, 512], BF16, sS)
            Pqm = Rot(p, "Pqm", 4, [64, 512], BF16, sS)
            PsT = Rot(p, "PsT", 4, [128, 4, 64], BF16, sS)
            Pn = Rot(p, "Pn", 2, [64, 8], BF16, sS); Pnm = Rot(p, "Pnm", 2, [64, 8], BF16, sS); PnT = Rot(p, "PnT", 2, [8, 64], BF16, sS)
            on_r = Rot(p, "on", 3, [64, 128], BF16, sS)
            smm = Rot(p, "smm", 6, [64, 32], F32, sS)
            Pe32 = Rot(p, "Pe32", 2, [64, 512], F32, sS)
            Pg = Rot(p, "Pg", 2, [128, 4, 16], BF16, sS)
            Pg32 = Rot(p, "Pg32", 2, [128, 4, 16], F32, sS)
            sel16 = Rot(p, "sel16", 2, [16, 129], BF16, sS)
            rep16f = p.sb("rep16f", [16, 64], F32, sS); rep16 = p.sb("rep16", [16, 64], BF16, sS)
            p.op("sp", I("dma_start", out=rep16f[:], in_=rep16d), writes=["rep16f"], dkey="c41")
            p.op("dve", I("tensor_copy", out=rep16[:], in_=rep16f[:]), reads=["rep16f"], writes=["rep16"])
            scs = Rot(p, "scs", 2, [64, 129], F32, sS); scs2 = Rot(p, "scs2", 2, [64, 129], F32, sS)
            sel_r = Rot(p, "sel", 2, [64, 129], BF16, sS)
            SW = Rot(p, "SW", 2, [128, 4, 256], BF16, sS)
            SWv = Rot(p, "SWv", 2, [128, 4, 129], BF16, sS)
            for i_, t_ in enumerate(SWv.t):
                p.op("pool", I("memset", ap=t_[:, :, 128:129], constant=1.0), writes=[("SWv1", i_)])
            KwT = Rot(p, "KwT", 2, [128, 512], BF16, sS)
            OBR = p.sb("OBR", [128, 3, 4, 128], F32, sS)
            id64 = ident[0:64, 0:64]
            G_all = [("G", i_) for i_ in range(6)]

            def to_obr(ont, onr, br, b):
                z, zr = psg()
                zb = z[:].bitcast(BF16)
                p.op("pe", TR([(zb[:, 0:64], ont[:, :], id64)]), reads=[onr, "ident"], writes=[zr])
                for k in range(2):
                    hs = slice(k * 64, (k + 1) * 64)
                    p.op("act", I("activation", out=OBR[hs, br, :, b * 8:(b + 1) * 8], in_=zb[hs, k * 32:(k + 1) * 32].rearrange("p (g q) -> p g q", q=8), func=AF.Copy),
                         reads=[zr], writes=[("OBR", br, b, k)])

            def finish_s(o, orr, br, b, rs_from_cover):
                m, mr = smm.get()
                p.op("dve", I("tensor_scalar", out=m[:, 1:2], in0=o[0:64, 128:129], scalar1=1e-30, scalar2=None, op0=ALU.max), reads=[orr], writes=[(mr, 1)])
                p.op("dve", I("reciprocal", out=m[:, 2:3], in_=m[:, 1:2]), reads=[(mr, 1)], writes=[(mr, 2)])
                ont, onr = on_r.get()
                p.op("dve", I("tensor_scalar", out=ont[:], in0=o[0:64, 0:128], scalar1=m[:, 2:3], scalar2=None, op0=ALU.mult), reads=[orr, (mr, 2)], writes=[onr])
                to_obr(ont, onr, br, b)
                return m, mr

            def new_keys(o, orr, which, b, tag):
                z, zr = psg()
                p.op("pe", MM([(z[0:64, 0:8], Qbd[:, b, :], KnT[:, which, b * 8:(b + 1) * 8], True)]), reads=["Qbd", "KnT"], writes=[zr])
                t, tr = Pn.get()
                p.op("act", I("activation", out=t[:], in_=z[0:64, 0:8], func=AF.Exp), reads=[zr], writes=[tr])
                tm, tmr = Pnm.get()
                p.op("dve", I("tensor_tensor", out=tm[:], in0=t[:], in1=cm8[:], op=ALU.mult), reads=[tr, "cm8"], writes=[tmr])
                z2, z2r = psg()
                z2b = z2[:].bitcast(BF16)
                p.op("pe", TR([(z2b[0:8, 0:64], tm[:, :], id64)]), reads=[tmr, "ident"], writes=[z2r])
                tt, ttr = PnT.get()
                p.op("act", I("activation", out=tt[:], in_=z2b[0:8, 0:64], func=AF.Copy), reads=[z2r], writes=[ttr])
                p.op("pe", MM([(o[0:64, 0:129], tt[:, :], Vn[:, which, b, :], False)]), reads=[ttr, "Vn"], writes=[orr])

            def dk_A(kT_ap, kT_res, mask_fn):
                z, zr = psg()
                p.op("pe", MM([(z[0:64, :], Qbd[:, b_cur[0], :], kT_ap, True)]), reads=["Qbd"] + kT_res, writes=[zr])
                t, tr = Pq.get()
                p.op("act", I("activation", out=t[:], in_=z[0:64, :], func=AF.Exp), reads=[zr], writes=[tr])
                tm, tmr = Pqm.get()
                mask_fn(tm, tmr, t, tr)
                return tm, tmr

            def dk_B(tm, tmr):
                z2, z2r = psg()
                z2b = z2[:].bitcast(BF16)
                p.op("pe", TR([(z2b[:, j * 64:(j + 1) * 64], tm[:, j * 128:(j + 1) * 128], id64) for j in range(4)]), reads=[tmr, "ident"], writes=[z2r])
                tt, ttr = PsT.get()
                p.op("act", I("activation", out=tt[:], in_=z2b[:, 0:256].rearrange("p (j c) -> p j c", j=4), func=AF.Copy), reads=[z2r], writes=[ttr])
                return tt, ttr

            def dk_C(o, orr, tt, ttr, v_fn, v_res, first):
                p.op("pe", MM([(o[0:64, 0:129], tt[:, j, :], v_fn(j), first and j == 0) for j in range(4)]), reads=[ttr] + v_res, writes=[orr])

            def dense_keys(o, orr, kT_ap, kT_res, mask_fn, v_fn, v_res, first):
                tm, tmr = dk_A(kT_ap, kT_res, mask_fn)
                tt, ttr = dk_B(tm, tmr)
                dk_C(o, orr, tt, ttr, v_fn, v_res, first)

            b_cur = [0]
            cache_v = cache
            import os as _os
            SPART = int(_os.environ.get('DBG_SPART', '5'))
            for b in range(int(_os.environ.get('DBG_NB', '16'))):
                b_cur[0] = b
                for c in range(32):
                    Gt, Gr = G.get()
                    p.op("pool", I("indirect_dma_start", out=Gt[:], out_offset=None, in_=cache_v,
                                   in_offset=bass.IndirectOffsetOnAxis(ap=idx[:, b * 32 + c:b * 32 + c + 1], axis=0)),
                         reads=["idx"], writes=[Gr], dkey=Gr)
                    if _os.environ.get('DBG_NOTR') == '1':
                        p.op("pool", I("tensor_copy", out=Vs[:, c, :, 0:128], in_=Gt[:].rearrange("p (t r) -> p t r", t=2)[:, :, 384:512]), reads=[Gr], writes=[("Vs", c)])
                        continue
                    z, zr = psg()
                    zb = z[:].bitcast(BF16)
                    lst = []
                    for stq in range(3):
                        for t2 in range(2):
                            o_ = t2 * 512 + stq * 128
                            lst.append((zb[:, (stq * 2 + t2) * 128:(stq * 2 + t2 + 1) * 128], Gt[:, o_:o_ + 128], ident[:]))
                    p.op("pe", TR(lst), reads=[Gr, "ident"], writes=[zr])
                    for stq in range(2):
                        for t2 in range(2):
                            o_ = (stq * 2 + t2) * 128
                            p.op("act" if t2 == 0 else "dve",
                                 I("activation", out=X2T[:, stq, t2, :, 16 * c:16 * c + 16], in_=zb[:, o_:o_ + 128].rearrange("p (i c) -> p c i", i=16, c=8), func=AF.Copy)
                                 if t2 == 0 else
                                 I("tensor_copy", out=X2T[:, stq, t2, :, 16 * c:16 * c + 16], in_=zb[:, o_:o_ + 128].rearrange("p (i c) -> p c i", i=16, c=8)),
                                 reads=[zr], writes=[("X2T", c, stq, t2)])
                    p.op("dve", I("tensor_copy", out=KsT[:, :, c * 128:(c + 1) * 128], in_=zb[:, 512:768].rearrange("p (k t) -> p k t", k=2)),
                         reads=[zr], writes=[("KsT", c)])
                    p.op("pool", I("tensor_copy", out=Vs[:, c, :, 0:128], in_=Gt[:].rearrange("p (t r) -> p t r", t=2)[:, :, 384:512]), reads=[Gr], writes=[("Vs", c)])
                X2T_res = [("X2T", c, q_, t_) for c in range(32) for q_ in range(2) for t_ in range(2)]
                if SPART < 2:
                    continue
                for stq in range(2):
                    zz = [psg(), psg()]
                    lst = []
                    for s_ in range(32):
                        for k in range(2):
                            hs = slice(k * 64, (k + 1) * 64)
                            cc = s_ // 2
                            rhs_ = X2T[hs, stq, s_ % 2, cc, 0:511] if cc < 8 else X2T[hs, stq, s_ % 2, cc - 8, 1:512]
                            lst.append((zz[k][0][:, 0:511], W1s[hs, stq, s_, :], rhs_, s_ == 0))
                    p.op("pe", MM(lst), reads=W1s_res + X2T_res, writes=[zz[0][1], zz[1][1]])
                    for k in range(2):
                        p.op("act", I("activation", out=gh[:, stq, k, 0:511], in_=zz[k][0][:, 0:511], func=AF.Gelu_apprx_tanh, bias=b1tot[:, stq:stq + 1]),
                             reads=[zz[k][1], "b1tot"], writes=[("gh", stq, k)])
                z, zr = psg()
                p.op("pe", MM([(z[:, 0:511], w2pad[:, 0, 0, :], gh[:, 0, 0, 0:511], True), (z[:, 0:511], w2pad[:, 0, 1, :], gh[:, 0, 1, 0:511], False)]),
                     reads=["w2pad", ("gh", 0, 0), ("gh", 0, 1)], writes=[zr])
                p.op("act", I("activation", out=kcTs[:, 0:511], in_=z[:, 0:511], func=AF.Identity, bias=b2k2[:, 0:1]), reads=[zr, "b2k2"], writes=["kcTs"])
                for c4 in range(4):
                    nb = 128 if c4 < 3 else 127
                    z, zr = psg()
                    p.op("pe", MM([(z[0:nb, k * 64:(k + 1) * 64], gh[:, 1, k, c4 * 128:c4 * 128 + nb], w2pad[:, 1, 0, 0:64], k == 0) for k in range(2)]),
                         reads=["w2pad", ("gh", 1, 0), ("gh", 1, 1)], writes=[zr])
                    p.op("dve", I("tensor_tensor", out=vcs[0:nb, c4, 0:128], in0=z[0:nb, 0:128], in1=b2vB[0:nb, :], op=ALU.add),
                         reads=[zr, "b2vB", "vcs"], writes=[("vcs", c4)])
                if SPART < 3:
                    continue
                z, zr = psg()
                p.op("pe", MM([(z[0:64, 0:511], Qbd[:, b, :], kcTs[:, 0:511], True)]), reads=["Qbd", "kcTs"], writes=[zr])
                m, mr = smm.get()
                pe_, per = Pe32.get()
                p.op("act", I("activation", out=pe_[:, 0:511], in_=z[0:64, 0:511], func=AF.Exp, accum_out=m[:, 0:1]), reads=[zr], writes=[per, (mr, 0)])
                p.op("dve", I("tensor_scalar", out=m[:, 1:2], in0=m[:, 0:1], scalar1=1e-30, scalar2=None, op0=ALU.max), reads=[(mr, 0)], writes=[(mr, 1)])
                p.op("dve", I("reciprocal", out=m[:, 2:3], in_=m[:, 1:2]), reads=[(mr, 1)], writes=[(mr, 2)])
                pc, pcr = Pc.get()
                p.op("dve", I("tensor_scalar", out=pc[:, 0:511], in0=pe_[:, 0:511], scalar1=m[:, 2:3], scalar2=None, op0=ALU.mult), reads=[per, (mr, 2)], writes=[pcr])
                z2, z2r = psg()
                z2b = z2[:].bitcast(BF16)
                p.op("pe", TR([(z2b[:, j * 64:(j + 1) * 64], pc[:, j * 128:(j + 1) * 128], id64) for j in range(4)]), reads=[pcr, "ident"], writes=[z2r])
                pct, pctr = PcT.get()
                p.op("act", I("activation", out=pct[:], in_=z2b[:, 0:256].rearrange("p (j c) -> p j c", j=4), func=AF.Copy), reads=[z2r], writes=[pctr])
                pg32, pg32r = Pg32.get()
                p.op("dve", I("tensor_reduce", out=pg32[:].rearrange("p j (k q) -> p j k q", k=2),
                              in_=pct[:].rearrange("p j (k g q) -> p j k q g", k=2, g=4), axis=AX.X, op=ALU.add), reads=[pctr], writes=[pg32r])
                pg, pgr = Pg.get()
                p.op("dve", I("tensor_copy", out=pg[:], in_=pg32[:]), reads=[pg32r], writes=[pgr])
                oc, ocr = psa()
                p.op("pe", MM([(oc[0:64, 0:128], pct[:, j, :], vcs[:, j, 0:128], j == 0) for j in range(4)]),
                     reads=[pctr, "vcs"] + [("vcs", j) for j in range(4)], writes=[ocr])
                ont, onr = on_r.get()
                p.op("act", I("activation", out=ont[:], in_=oc[0:64, 0:128], func=AF.Copy), reads=[ocr], writes=[onr])
                to_obr(ont, onr, 0, b)
                z, zr = psg()
                p.op("pe", MM([(z[0:16, 0:129], pg[:, j, :], vcs[:, j, 128:257], j == 0) for j in range(4)]), reads=[pgr, "vcs"], writes=[zr])
                sc, scr = scs.get()
                p.op("dve", I("tensor_tensor", out=sc[0:16, :], in0=z[0:16, 0:129], in1=fbs[0:16, :], op=ALU.add), reads=[zr, "fbs"], writes=[scr])
                p.op("dve", I("max", out=m[0:16, 8:16], in_=sc[0:16, :]), reads=[scr], writes=[(mr, 3)])
                sc2, sc2r = scs2.get()
                p.op("dve", I("match_replace", out=sc2[0:16, :], in_to_replace=m[0:16, 8:16], in_values=sc[0:16, :], imm_value=-3.0e38), reads=[scr, (mr, 3)], writes=[sc2r])
                p.op("dve", I("max", out=m[0:16, 16:24], in_=sc2[0:16, :]), reads=[sc2r], writes=[(mr, 4)])
                s16, s16r = sel16.get()
                p.op("dve", I("tensor_scalar", out=s16[:], in0=sc[0:16, :], scalar1=m[0:16, 23:24], scalar2=None, op0=ALU.is_ge), reads=[scr, (mr, 4)], writes=[s16r])
                z, zr = psg()
                p.op("pe", MM([(z[0:64, 0:129], rep16[:, :], s16[:, :], True)]), reads=[s16r, "rep16"], writes=[zr])
                sel, selr = sel_r.get()
                p.op("act", I("activation", out=sel[:], in_=z[0:64, 0:129], func=AF.Copy), reads=[zr], writes=[selr])
                if SPART < 4:
                    continue
                osl, oslr = psa()
                qa, qb_ = [], []
                nC = [0]

                def run_C(it):
                    tt, ttr, t2, mm_ = it
                    dk_C(osl, oslr, tt, ttr, lambda j, t2=t2, mm_=mm_: Vs[:, 4 * mm_ + j, t2, :], [("Vs", 4 * mm_ + j) for j in range(4)] + ["Vs1"], nC[0] == 0)
                    nC[0] += 1

                def run_B(it):
                    tm, tmr, t2, mm_ = it
                    tt, ttr = dk_B(tm, tmr)
                    qb_.append((tt, ttr, t2, mm_))
                    if len(qb_) > 1:
                        run_C(qb_.pop(0))
                for mm_ in range(8):
                    for t2 in range(2):
                        def mask_sel(tm, tmr, t, tr, mm_=mm_):
                            p.op("dve", I("tensor_tensor", out=tm[:].rearrange("p (j r) -> p j r", r=32), in0=t[:].rearrange("p (j r) -> p j r", r=32),
                                          in1=sel[:, 16 * mm_:16 * mm_ + 16].unsqueeze(2).broadcast_to([64, 16, 32]), op=ALU.mult),
                                 reads=[tr, selr], writes=[tmr])
                        tm, tmr = dk_A(KsT[:, t2, mm_ * 512:(mm_ + 1) * 512], [("KsT", 4 * mm_ + j) for j in range(4)], mask_sel)
                        qa.append((tm, tmr, t2, mm_))
                        if len(qa) > 1:
                            run_B(qa.pop(0))
                while qa:
                    run_B(qa.pop(0))
                while qb_:
                    run_C(qb_.pop(0))
                new_keys(osl, oslr, 0, b, "s")
                finish_s(osl, oslr, 1, b, False)
                if SPART < 5:
                    continue
                swt, swr = SW.get()
                p.op("pool", I("dma_start", out=swt[:], in_=swin[b].rearrange("(c p) f -> p c f", p=128)), writes=[swr], dkey=swr)
                z, zr = psg()
                zb = z[:].bitcast(BF16)
                p.op("pe", TR([(zb[:, c * 128:(c + 1) * 128], swt[:, c, 0:128], ident[:]) for c in range(4)]), reads=[swr, "ident"], writes=[zr])
                kw, kwr = KwT.get()
                p.op("act", I("activation", out=kw[:], in_=zb[:, 0:512], func=AF.Copy), reads=[zr], writes=[kwr])
                sv, svr = SWv.get()
                p.op("pool", I("tensor_copy", out=sv[:, :, 0:128], in_=swt[:, :, 128:256]), reads=[swr], writes=[svr])
                ow, owr = psa()

                def mask_win(tm, tmr, t, tr):
                    p.op("dve", I("tensor_tensor", out=tm[:], in0=t[:], in1=wm[:], op=ALU.mult), reads=[tr, "wm"], writes=[tmr])
                dense_keys(ow, owr, kw[:, :], [kwr], mask_win, lambda j, sv=sv: sv[:, j, :], [svr], True)
                new_keys(ow, owr, 1, b, "w")
                finish_s(ow, owr, 2, b, False)
            ghi = p.sb("ghi", [128, 24], BF16, sS); ghi32 = p.sb("ghi32", [128, 24], F32, sS); glo = p.sb("glo", [128, 24], BF16, sS)
            p.op("dve", I("tensor_copy", out=ghi[:], in_=gates_sb[:, 16, :]), reads=[("gates", 16)], writes=["ghi"])
            p.op("dve", I("tensor_copy", out=ghi32[:], in_=ghi[:]), reads=["ghi"], writes=["ghi32"])
            p.op("dve", I("tensor_tensor", out=glo[:], in0=gates_sb[:, 16, :], in1=ghi32[:], op=ALU.subtract), reads=[("gates", 16), "ghi32"], writes=["glo"])
            z, zr = psg()
            zb = z[:].bitcast(BF16)
            p.op("pe", TR([(zb[0:24, 0:128], ghi[:, :], ident[:]), (zb[0:24, 128:256], glo[:, :], ident[:])]), reads=["ghi", "glo", "ident"], writes=[zr])
            gT = p.sb("gT", [24, 2, 128], BF16, sS)
            p.op("act", I("activation", out=gT[:], in_=zb[0:24, 0:256].rearrange("p (a t) -> p a t", a=2), func=AF.Copy), reads=[zr], writes=["gT"])
            selgb = p.sb("selgb", [24, 3, 4, 128], BF16, sS)
            p.op("dve", I("tensor_copy", out=selgb[:], in_=selg[:]), reads=["selg"], writes=["selgb"])
            oaT = p.sb("oaT", [128, 512], F32, sS)
            oat2 = p.sb("oat2", [128, 512], F32, sS)
            obr_res = [("OBR", br, b, k) for br in range(3) for b in range(16) for k in range(2)]
            for br in range(3):
                z, zr = psg()
                lst = []
                for g in range(4):
                    lst.append((z[:, g * 128:(g + 1) * 128], selgb[:, br, g, :], gT[:, 0, :], g == 0))
                    lst.append((z[:, g * 128:(g + 1) * 128], selgb[:, br, g, :], gT[:, 1, :], False))
                p.op("pe", MM(lst), reads=["selgb", "gT"], writes=[zr])
                src = OBR[:, br, :, :].rearrange("p g t -> p (g t)")
                if br == 0:
                    p.op("dve", I("tensor_tensor", out=oaT[:], in0=z[:, :], in1=src, op=ALU.mult), reads=[zr] + obr_res, writes=["oaT"])
                else:
                    p.op("dve", I("tensor_tensor", out=oat2[:], in0=z[:, :], in1=src, op=ALU.mult), reads=[zr] + obr_res, writes=["oat2"])
                    p.op("pool", I("tensor_tensor", out=oaT[:], in0=oaT[:], in1=oat2[:], op=ALU.add), reads=["oaT", "oat2"], writes=["oaT"])
            if _os.environ.get('DBG_NOFIN') != '1':
                p.op("act", I("activation", out=mixT[:, 0:4, 2048:2176], in_=oaT[:].rearrange("p (g t) -> p g t", g=4), func=AF.Copy), reads=["oaT"], writes=[("mixT", "a")])
            p.barrier()
        with contextlib.ExitStack() as s3:
            wout_sb = p.sb("wout_sb", [128, 8, 1024], BF16, s3)
            w_out_v = w_out.rearrange("(kc p) n -> p kc n", p=128)
            for kc in range(8):
                p.op("pool", I("dma_start", out=wout_sb[:, kc, :], in_=w_out_v[:, kc, :]), writes=[("wout", kc)], dkey=("wout", kc % 4))
            wouts_sb = p.sb("wouts_sb", [128, 4, 1024], BF16, s3)

            def wouts_dma(e, wouts_sb=wouts_sb):
                r = []
                for g in range(4):
                    for k in range(2):
                        h_ = 4 * k + g
                        r.append(e.dma_start(out=wouts_sb[k * 64:(k + 1) * 64, g, :], in_=w_out[h_ * 64:(h_ + 1) * 64, :]))
                return r
            p.op("pool", wouts_dma, writes=["wouts"], dkey="c40", ndma=8)
            wout_res = [("wout", kc) for kc in range(8)] + ["wouts"]
            gf_sb = p.sb("gf_sb", [128, 1024], F32, s3)
            p.op("sp", I("dma_start", out=gf_sb[:], in_=gfB), writes=["gf"], dkey="c11")
            w1s = Rot(p, "w1s", 2, [128, 8, 512], BF16, s3)
            w2s = Rot(p, "w2s", 3, [128, 4, 512], BF16, s3)
            fT = p.sb("fT", [128, 32, 640], BF16, s3)
            hnT = p.sb("hnT", [128, 8, 640], BF16, s3)
            h2 = p.sb("h2", [128, 5, 1024], F32, s3)
            xbuf = Rot(p, "xb3", 2, [128, 1024], F32, s3)
            junk = Rot(p, "junk3", 2, [128, 1024], BF16, s3)
            hnb = Rot(p, "hnb", 2, [128, 1024], BF16, s3)
            st4 = Rot(p, "st43", 4, [128, 8], F32, s3)
            rl = Rot(p, "rl", 3, [128, 512], F32, s3)
            yb = Rot(p, "yb", 5, [128, 1024], F32, s3)
            w_ff1_v = w_ff1.rearrange("(kc p) f -> p kc f", p=128)
            w_ff2_v = w_ff2.rearrange("(fc p) n -> p fc n", p=128)

            groups = [[("own", t) for t in range(0, 4)], [("own", t) for t in range(4, 8)],
                      [("own", t) for t in range(8, 12)], [("own", t) for t in range(12, 16)] + [("smp", 0)]]
            for grp in groups:
                nt = len(grp)
                ntok = nt * 128
                for si, (kind, ti) in enumerate(grp):
                    src = xo if kind == "own" else xs
                    r0 = ti * 128
                    mcol = 2048 if kind == "smp" else ti * 128
                    xt, xr = xbuf.get()
                    p.op("sp", I("dma_start", out=xt[:], in_=src[r0:r0 + 128, :]), writes=[xr], dkey=xr)
                    for half in range(2):
                        z, zr = psa()
                        p.op("pe", MM([(z[:, :], mixT[:, k, mcol:mcol + 128],
                                        (wouts_sb if (kind == "smp" and k < 4 and stage >= 3) else wout_sb)[:, k, half * 512:(half + 1) * 512], k == 0) for k in range(8)]),
                             reads=wout_res + [("mixT", "a"), ("mixT", "a", mcol), ("mixT", "b", mcol)], writes=[zr])
                        p.op("dve", I("tensor_tensor", out=h2[:, si, half * 512:(half + 1) * 512], in0=z[:, :], in1=xt[:, half * 512:(half + 1) * 512], op=ALU.add),
                             reads=[zr, xr], writes=[("h2", si, half)])
                    jk, jr = junk.get(); s4, s4r = st4.get()
                    p.op("act", I("activation", out=jk[:], in_=h2[:, si, :], func=AF.Square, accum_out=s4[:, 0:1]),
                         reads=[("h2", si, 0), ("h2", si, 1)], writes=[jr, (s4r, 0)])
                    p.op("act", I("activation", out=s4[:, 1:2], in_=s4[:, 0:1], func=AF.Sqrt, scale=1.0 / 1024, bias=EPS), reads=[(s4r, 0)], writes=[(s4r, 1)])
                    p.op("dve", I("reciprocal", out=s4[:, 2:3], in_=s4[:, 1:2]), reads=[(s4r, 1)], writes=[(s4r, 2)])
                    hb, hbr = hnb.get()
                    p.op("act", I("activation", out=hb[:], in_=h2[:, si, :], func=AF.Copy, scale=s4[:, 2:3]),
                         reads=[("h2", si, 0), ("h2", si, 1), (s4r, 2)], writes=[hbr])
                    pt_, ptr = psg()
                    ptb = pt_[:].bitcast(BF16)
                    p.op("pe", TR([(ptb[:, k * 128:(k + 1) * 128], hb[:, k * 128:(k + 1) * 128], ident[:]) for k in range(8)]),
                         reads=[hbr, "ident"], writes=[ptr])
                    p.op("dve", I("tensor_tensor", out=hnT[:, :, si * 128:(si + 1) * 128], in0=ptb.rearrange("p (k t) -> p k t", k=8),
                                  in1=g2T_sb[:, :].unsqueeze(2).broadcast_to([128, 8, 128]), op=ALU.mult),
                         reads=[ptr, "g2T"], writes=[("hnT", si)])
                hn_res = [("hnT", si) for si in range(nt)]
                for c in range(8):
                    w1t, w1r = w1s.get()
                    p.op("pool", I("dma_start", out=w1t[:], in_=w_ff1_v[:, :, c * 512:(c + 1) * 512]), writes=[w1r], dkey=w1r)
                    for f4 in range(4):
                        fc = c * 4 + f4
                        for (t0, tn) in ([(0, 512), (512, 128)] if nt == 5 else [(0, ntok)]):
                            z, zr = psg()
                            p.op("pe", MM([(z[:, 0:tn], w1t[:, k, f4 * 128:(f4 + 1) * 128], hnT[:, k, t0:t0 + tn], k == 0) for k in range(8)]),
                                 reads=[w1r] + hn_res, writes=[zr])
                            rt, rr = rl.get()
                            p.op("act", I("activation", out=rt[:, 0:tn], in_=z[:, 0:tn], func=AF.Relu), reads=[zr], writes=[rr])
                            p.op("pool", I("tensor_tensor", out=fT[:, fc, t0:t0 + tn], in0=rt[:, 0:tn], in1=rt[:, 0:tn], op=ALU.mult),
                                 reads=[rr], writes=[("fT", fc, t0)])
                yts = [yb.get() for _ in range(nt)]
                for half in range(2):
                    accs = [psa() for _ in range(min(nt, 4))] + ([psg()] if nt == 5 else [])
                    for c in range(8):
                        w2t, w2r = w2s.get()
                        p.op("pool", I("dma_start", out=w2t[:], in_=w_ff2_v[:, c * 4:(c + 1) * 4, half * 512:(half + 1) * 512]), writes=[w2r], dkey=w2r)
                        for si in range(nt):
                            z, zr = accs[si]
                            p.op("pe", MM([(z[:, :], fT[:, c * 4 + f4, si * 128:(si + 1) * 128], w2t[:, f4, :], (c == 0 and f4 == 0)) for f4 in range(4)]),
                                 reads=[w2r] + [("fT", c * 4 + f4, 0) for f4 in range(4)] + [("fT", c * 4 + f4, 512) for f4 in range(4)], writes=[zr])
                    for si in range(nt):
                        z, zr = accs[si]
                        yt, yr = yts[si]
                        p.op("dve", I("tensor_tensor", out=yt[:, half * 512:(half + 1) * 512], in0=z[:, :], in1=h2[:, si, half * 512:(half + 1) * 512], op=ALU.add),
                             reads=[zr, ("h2", si, half)], writes=[(yr, half)])
                for si, (kind, ti) in enumerate(grp):
                    dst = yo if kind == "own" else ys
                    r0 = ti * 128
                    yt, yr = yts[si]
                    jk, jr = junk.get(); s4, s4r = st4.get()
                    p.op("act", I("activation", out=jk[:], in_=yt[:], func=AF.Square, accum_out=s4[:, 0:1]), reads=[(yr, 0), (yr, 1)], writes=[jr, (s4r, 0)])
                    p.op("act", I("activation", out=s4[:, 1:2], in_=s4[:, 0:1], func=AF.Sqrt, scale=1.0 / 1024, bias=EPS), reads=[(s4r, 0)], writes=[(s4r, 1)])
                    p.op("dve", I("reciprocal", out=s4[:, 2:3], in_=s4[:, 1:2]), reads=[(s4r, 1)], writes=[(s4r, 2)])
                    p.op("dve", I("scalar_tensor_tensor", out=yt[:], in0=yt[:], scalar=s4[:, 2:3], in1=gf_sb[:], op0=ALU.mult, op1=ALU.mult),
                         reads=[(yr, 0), (yr, 1), (s4r, 2), "gf"], writes=[(yr, 0), (yr, 1)])
                    outs.append(p.op("sp", I("dma_start", out=dst[r0:r0 + 128, :], in_=yt[:]), reads=[(yr, 0), (yr, 1)], dkey=("oy", si)))
        p.emit(final_waits=outs)
    return nc


def _consts():
    I_ = np.arange(128)[:, None]
    q_ = np.arange(512)[None, :]
    cover = np.zeros((2, 128, 64), np.float32)
    for c2 in range(2):
        for i in range(128):
            bi = c2 * 128 + i
            if bi > 254:
                continue
            for j in range(64):
                ov = min(16 * bi + 32, 64 * j + 64) - max(16 * bi, 64 * j)
                if ov > 0:
                    cover[c2, i, j] = ov / 32.0
    cmpm = np.zeros((8, 128, 512), np.float32)
    for qt in range(4):
        for c2 in range(2):
            bi = c2 * 128 + I_
            cmpm[qt * 2 + c2] = ((bi <= 254) & (16 * bi + 31 <= 2048 + 512 * qt + q_)).astype(np.float32)
    bandm = np.zeros((8, 128, 512), np.float32)
    for c in range(8):
        kk = 128 * c + I_
        bandm[c] = ((kk > q_) & (kk <= q_ + 512)).astype(np.float32)
    negc = np.zeros((4, 128, 512), np.float32)
    for d in range(4):
        negc[d] = -((128 * d + I_) > q_).astype(np.float32)
    E = np.zeros((64, 32, 128), np.float32)
    for c in range(32):
        for m_ in range(128):
            E[2 * c + m_ // 64, c, m_] = 1.0
    return dict(cover=cover, cmpm=cmpm, bandm=bandm, negc=negc, Eexp=E)


def _sample_consts():
    covs = np.zeros((128, 4, 129), np.float32)
    for c4 in range(4):
        for i in range(128):
            bi = c4 * 128 + i
            if bi > 510:
                continue
            for j in range(129):
                ov = min(16 * bi + 32, 64 * j + 64) - max(16 * bi, 64 * j)
                if ov > 0:
                    covs[i, c4, j] = ov / 32.0
    fbs = np.zeros((64, 129), np.float32)
    fbs[:, [0, 127, 128]] = 1e9
    r = np.arange(64)
    k_, g_, q_ = r // 32, (r // 8) % 4, r % 8
    sel2 = ((k_[:, None] == k_[None, :]) & (q_[:, None] == q_[None, :])).astype(np.float32)
    cm8 = (np.arange(8)[None, :] <= q_[:, None]).astype(np.float32)
    wm = (np.arange(512)[None, :] > q_[:, None]).astype(np.float32)
    selg = np.zeros((24, 3, 4, 128), np.float32)
    for br in range(3):
        for g in range(4):
            for k in range(2):
                selg[br * 8 + 4 * k + g, br, g, k * 64:(k + 1) * 64] = 1.0
    pcol = (np.arange(128) % 64).astype(np.float32)[:, None]
    r16 = np.arange(16)
    rep16 = ((r16[:, None] // 8 == k_[None, :]) & (r16[:, None] % 8 == q_[None, :])).astype(np.float32)
    return dict(covs=covs, fbs=fbs, sel2=sel2, cm8=cm8, wm=wm, selg=selg, pcol=pcol, rep16=rep16)


def _core_consts(h):
    first = 0 if h == 1 else 32
    selb = np.zeros((128, 16, 64), np.float32)
    vis = np.zeros((128, 16, 64), np.float32)
    j = np.arange(64)[None, :]
    for t in range(16):
        tl = 2048 + t * 128 + np.arange(128)[:, None]
        cur = tl // 64
        visible = (j >= first) & (j <= cur)
        forced = (j == first) | (j == cur) | (j == cur - 1)
        b = np.where(forced, 1e9, 0.0)
        b = np.where(visible, b, -1e30)
        selb[:, t, :] = b
        vis[:, t, :] = visible
    ctxb = np.zeros((128, 2), np.float32)
    if h == 0:
        ctxb[:, 0] = NEG
    return dict(selb=selb, vis=vis, ctxb=ctxb)


def _host_inputs(inp):
    f = lambda a: np.ascontiguousarray(np.asarray(a, dtype=np.float32))
    w_in = f(inp["w_in"][0])
    qperm = []
    for g in range(4):
        for k in range(2):
            h = k * 4 + g
            qperm += list(range(h * 64, (h + 1) * 64))
    perm = np.array(qperm + list(range(512, INC)))
    w_in_p = np.ascontiguousarray(w_in[:, perm])
    rep = lambda v, n=128: np.ascontiguousarray(np.broadcast_to(np.asarray(v, np.float32)[None, :], (n, len(v))))
    colT = lambda v: np.ascontiguousarray(np.asarray(v, np.float32).reshape(8, 128).T)
    b_s = f(inp["b_s"][0])
    gind = np.zeros((8, 512), np.float32)
    for g in range(8):
        gind[g, g * 64:(g + 1) * 64] = 1.0
    ii = np.arange(128)
    common = dict(
        w_in=w_in_p, w_out=f(inp["w_out"][0]), w_ff1=f(inp["w_ff1"][0]), w_ff2=f(inp["w_ff2"][0]),
        g1T=colT(inp["ln1_g"][0]), g2T=colT(inp["ln2_g"][0]), gfB=rep(inp["ln_f_g"]),
        lvg=rep(inp["ln_v_g"][0]), lvb=rep(inp["ln_v_b"][0]),
        wsT=np.ascontiguousarray(f(inp["w_s"][0]).transpose(0, 2, 1)),
        bs8=b_s, bs8s=np.ascontiguousarray(np.tile(b_s[:, 0:8], (1, 16))),
        ident=np.eye(128, dtype=np.float32),
        tril=(ii[:, None] <= ii[None, :]).astype(np.float32),
        gind=gind,
    )
    common.update(_consts())
    common.update(_sample_consts())
    cache = np.asarray(inp["cache_kv"], dtype=np.float32).reshape(-1, 1024)
    common["cache"] = cache
    pt = np.asarray(inp["page_table"]).astype(np.int32)
    pp = np.arange(128) // 64
    cw1 = f(inp["cmp_w1"][0]); cpos = f(inp["cmp_pos"][0]).reshape(2, 16, 128)
    cb2 = f(inp["cmp_b2"][0])
    common.update(dict(
        cw1=cw1, cpos=np.ascontiguousarray(cpos.transpose(2, 0, 1)),
        cpos64=np.ascontiguousarray(f(inp["cmp_pos"][0]).transpose(2, 0, 1)), cb1=np.ascontiguousarray(f(inp["cmp_b1"][0]).T),
        cw2=f(inp["cmp_w2"][0]), cb2k=np.ascontiguousarray(np.tile(cb2[0], 2)[:, None]),
        cb2v=rep(np.tile(cb2[1], 2)),
    ))
    xp = f(inp["x_prompt"]); xs = f(inp["x_sample"]).reshape(1024, 1024)
    swin = f(inp["state_win_kv"][0]).reshape(128, 512, 256)
    maps = []
    for c in range(8):
        b, h = c // 2, c % 2
        m = dict(common)
        m["xo"] = np.ascontiguousarray(xp[b, h * 2048:(h + 1) * 2048])
        m["xc"] = np.ascontiguousarray(xp[b, 0:2048])
        m["xs"] = np.ascontiguousarray(xs[c * 128:(c + 1) * 128])
        m["swin"] = np.ascontiguousarray(swin[c * 16:(c + 1) * 16])
        m.update(_core_consts(h))
        ptc = pt[c * 16:(c + 1) * 16].reshape(16, 32, 2)
        m["ptx"] = np.ascontiguousarray(ptc[:, :, pp].transpose(2, 0, 1))
        maps.append(m)
    return maps


STAGE = 3
_NC = {}


def kernel(**inp):
    return _run(_host_inputs(inp))


def _run(maps):
    npool = maps[0]["cache"].shape[0] // 64
    if (STAGE, npool) not in _NC:
        _NC[(STAGE, npool)] = build(STAGE, npool)
    nc = _NC[(STAGE, npool)]
    maps = [{"i_" + k: v for k, v in m.items()} for m in maps]
    res = run_bass_kernel_spmd(nc, maps, core_ids=list(range(8)))
    r = [{k[2:]: v for k, v in d.items()} for d in res.results]
    y_p = np.zeros((4, 4096, 1024), np.float32)
    kv_p = np.zeros((1, 4, 4096, 512), np.float32)
    win_p = np.zeros((1, 4, 512, 256), np.float32)
    for c in range(8):
        b, h = c // 2, c % 2
        y_p[b, h * 2048:(h + 1) * 2048] = r[c]["yo"]
        kv_p[0, b, h * 2048:(h + 1) * 2048] = r[c]["kvo"]
        if h == 1:
            win_p[0, b] = r[c]["wino"]
    y_s = np.concatenate([r[c]["ys"] for c in range(8)], 0).reshape(128, 8, 1024)
    kv_s = np.concatenate([r[c]["kvs"] for c in range(8)], 0).reshape(1, 128, 8, 4, 2, 64)
    win_s = np.concatenate([r[c]["wins"] for c in range(8)], 0).reshape(1, 128, 512, 2, 2, 64)
    v_s = np.concatenate([r[c]["vso"] for c in range(8)], 0).reshape(1, 128, 8, 512)
    return (y_p, y_s, kv_p.reshape(1, 4, 4096, 4, 2, 64), kv_s, win_p.reshape(1, 4, 512, 2, 2, 64), win_s, v_s)
```
